# Optimizing a Trainium2 kernel written in Bass

```python
import jax
import jax.numpy as jnp
from jax import lax
import numpy as np

D_MODEL = 2048
BATCH = 4
SEQ = 4096
DEPTH = 2

GRID_W = 64
CTX_LEN = 256
HEAD_DIM = 128
ROPE_THETA = 10000.0
EPS = 1e-6
A_HEADS = 8
A_KV_HEADS = 2
WINDOW = 128
WBLOCK = 128
B_HEADS = 8
Q_LORA = 512
KV_LORA = 512
NOPE_DIM = 128
ROPE_DIM = 64
V_DIM = 128
Q_BLOCK = 128
C_GROUPS = 8
C_GROUP_DIM = 128
CHUNK = 128

A_WIDTH = A_HEADS * HEAD_DIM
A_KV_WIDTH = A_KV_HEADS * HEAD_DIM
B_WIDTH = B_HEADS * V_DIM
C_WIDTH = C_GROUPS * C_GROUP_DIM
IN_SPLITS = (A_WIDTH, A_KV_WIDTH, A_KV_WIDTH, A_WIDTH,
             Q_LORA, KV_LORA, ROPE_DIM, B_WIDTH,
             C_WIDTH, C_WIDTH, C_WIDTH,
             D_MODEL, D_MODEL, D_MODEL)
IN_COLS = sum(IN_SPLITS)
IN_OFFSETS = tuple(sum(IN_SPLITS[:i + 1]) for i in range(len(IN_SPLITS) - 1))

kernel_name = 'hybrid_gated_swa_mla_sgu_prefix_dit'


def rms_norm(x, g):
    xf = x.astype(jnp.float32)
    y = xf * lax.rsqrt(jnp.mean(xf * xf, axis=-1, keepdims=True) + EPS)
    return (y * g.astype(jnp.float32)).astype(x.dtype)


def _rope_1d(x, pos):
    half = x.shape[-1] // 2
    inv = ROPE_THETA ** (-jnp.arange(half, dtype=jnp.float32) / half)
    ang = pos[:, None] * inv[None, :]
    cos = jnp.cos(ang)[:, None, :].astype(x.dtype)
    sin = jnp.sin(ang)[:, None, :].astype(x.dtype)
    x1, x2 = x[..., :half], x[..., half:]
    return jnp.concatenate([x1 * cos - x2 * sin, x1 * sin + x2 * cos], axis=-1)


def axial_rope(x, row, col):
    d2 = x.shape[-1] // 2
    return jnp.concatenate([_rope_1d(x[..., :d2], row), _rope_1d(x[..., d2:], col)], axis=-1)


def window_gqa(q, k, v, kc, vc, sink):
    b, n, h, d = q.shape
    kvh = k.shape[2]
    g = h // kvh
    nb = n // WBLOCK
    scale = d ** -0.5
    qb = q.reshape(b, nb, WBLOCK, kvh, g, d)
    pad = ((0, 0), (WBLOCK, WBLOCK), (0, 0), (0, 0))
    kp = jnp.pad(k, pad).reshape(b, nb + 2, WBLOCK, kvh, d)
    vp = jnp.pad(v, pad).reshape(b, nb + 2, WBLOCK, kvh, d)
    kw = jnp.concatenate([kp[:, :-2], kp[:, 1:-1], kp[:, 2:]], axis=2)
    vw = jnp.concatenate([vp[:, :-2], vp[:, 1:-1], vp[:, 2:]], axis=2)
    s_loc = jnp.einsum('bnqkgd,bnjkd->bnkgqj', qb, kw).astype(jnp.float32) * scale
    qi = jnp.arange(WBLOCK)[:, None]
    kj = jnp.arange(3 * WBLOCK)[None, :]
    rel = kj - qi
    band = (rel >= WBLOCK - WINDOW) & (rel <= WBLOCK + WINDOW)
    jpos = jnp.arange(nb)[:, None] * WBLOCK - WBLOCK + jnp.arange(3 * WBLOCK)[None, :]
    valid = band[None] & ((jpos >= 0) & (jpos < n))[:, None, :]
    s_loc = jnp.where(valid[None, :, None, None], s_loc, -1e30)
    s_ctx = jnp.einsum('bnqkgd,bjkd->bnkgqj', qb, kc).astype(jnp.float32) * scale
    s_sink = jnp.broadcast_to(sink.astype(jnp.float32).reshape(kvh, g)[None, None, :, :, None, None],
                              s_loc.shape[:-1] + (1,))
    p = jax.nn.softmax(jnp.concatenate([s_loc, s_ctx, s_sink], axis=-1), axis=-1).astype(v.dtype)
    nw = 3 * WBLOCK
    nc = kc.shape[1]
    o = (jnp.einsum('bnkgqj,bnjkd->bnqkgd', p[..., :nw], vw)
         + jnp.einsum('bnkgqj,bjkd->bnqkgd', p[..., nw:nw + nc], vc))
    return o.reshape(b, n, h * d)


def context_gqa(q, k, v, sink):
    b, l, h, d = q.shape
    kvh = k.shape[2]
    g = h // kvh
    qg = q.reshape(b, l, kvh, g, d)
    s = jnp.einsum('bqkgd,bjkd->bkgqj', qg, k).astype(jnp.float32) * (d ** -0.5)
    s_sink = jnp.broadcast_to(sink.astype(jnp.float32).reshape(kvh, g)[None, :, :, None, None],
                              s.shape[:-1] + (1,))
    p = jax.nn.softmax(jnp.concatenate([s, s_sink], axis=-1), axis=-1)[..., :l].astype(v.dtype)
    return jnp.einsum('bkgqj,bjkd->bqkgd', p, v).reshape(b, l, h * d)


def mla_project(cq, ckv, g_q, g_kv, w_uq, w_ukv):
    b, n, _ = cq.shape
    q = (rms_norm(cq, g_q) @ w_uq).reshape(b, n, B_HEADS, NOPE_DIM + ROPE_DIM)
    kv = (rms_norm(ckv, g_kv) @ w_ukv).reshape(b, n, B_HEADS, NOPE_DIM + V_DIM)
    return q[..., :NOPE_DIM], q[..., NOPE_DIM:], kv[..., :NOPE_DIM], kv[..., NOPE_DIM:]


def mla_dense(qn, qr, kn, kr, v, knc, krc, vc):
    b, n, h, _ = qn.shape
    nb = n // Q_BLOCK
    scale = (NOPE_DIM + ROPE_DIM) ** -0.5

    def block(qs):
        qn_b, qr_b = qs
        s_lat = jnp.einsum('bqhd,bkhd->bhqk', qn_b, kn) + jnp.einsum('bqhd,bkd->bhqk', qr_b, kr)
        s_ctx = jnp.einsum('bqhd,bkhd->bhqk', qn_b, knc) + jnp.einsum('bqhd,bkd->bhqk', qr_b, krc)
        p = jax.nn.softmax(jnp.concatenate([s_lat, s_ctx], axis=-1).astype(jnp.float32) * scale,
                           axis=-1).astype(v.dtype)
        return (jnp.einsum('bhqk,bkhd->bqhd', p[..., :n], v)
                + jnp.einsum('bhqk,bkhd->bqhd', p[..., n:], vc))

    def to_blocks(t):
        return t.reshape(b, nb, Q_BLOCK, h, t.shape[-1]).swapaxes(0, 1)

    o = lax.map(block, (to_blocks(qn), to_blocks(qr)))
    return o.swapaxes(0, 1).reshape(b, n, h * V_DIM)


def mla_context(qn, qr, kn, kr, v):
    b, l, h, _ = qn.shape
    scale = (NOPE_DIM + ROPE_DIM) ** -0.5
    s = jnp.einsum('bqhd,bkhd->bhqk', qn, kn) + jnp.einsum('bqhd,bkd->bhqk', qr, kr)
    p = jax.nn.softmax(s.astype(jnp.float32) * scale, axis=-1).astype(v.dtype)
    return jnp.einsum('bhqk,bkhd->bqhd', p, v).reshape(b, l, h * V_DIM)


def chunk_sgu(u, v, ln_g, ln_b, w_s, b_s):
    b, n, _ = v.shape
    vf = v.astype(jnp.float32)
    mu = jnp.mean(vf, axis=-1, keepdims=True)
    var = jnp.mean(jnp.square(vf - mu), axis=-1, keepdims=True)
    vn = ((vf - mu) * lax.rsqrt(var + EPS) * ln_g.astype(jnp.float32)
          + ln_b.astype(jnp.float32)).astype(v.dtype)
    vb = vn.reshape(b, n // CHUNK, CHUNK, C_GROUPS, C_GROUP_DIM)
    mixed = jnp.einsum('gpq,bcqgd->bcpgd', w_s, vb) + b_s.T[None, None, :, :, None]
    return u * mixed.reshape(b, n, C_WIDTH)


def merge_branches(ya, za, ga, yb, zb, gb, yc, zc, gc, w_pa, w_pb, w_pc, w_out):
    m = (jax.nn.sigmoid(ga) * ((ya * jax.nn.silu(za)) @ w_pa)
         + jax.nn.sigmoid(gb) * ((yb * jax.nn.silu(zb)) @ w_pb)
         + jax.nn.sigmoid(gc) * ((yc * jax.nn.silu(zc)) @ w_pc))
    return m @ w_out


def hybrid_layer(x, xc, c, c_ctx, row, col, ada_w, ada_b, norm_g, w_in, sink_a, mla_gq, mla_gkv,
                 w_uq, w_ukv, sgu_ln_g, sgu_ln_b, sgu_w, sgu_b, w_pa, w_pb, w_pc, w_out, need_ctx_out):
    b, n, _ = x.shape
    lc = xc.shape[1]
    shift, scale, gate = jnp.split((jax.nn.silu(c) @ ada_w + ada_b)[:, None, :], 3, axis=-1)
    shift_c, scale_c, gate_c = jnp.split(jax.nn.silu(c_ctx) @ ada_w + ada_b, 3)
    h = rms_norm(x, norm_g) * (1 + scale) + shift
    hc = rms_norm(xc, norm_g) * (1 + scale_c) + shift_c
    (aq, ak, av, az, bcq, bckv, bkr, bz, cu, cv, cz, g_a, g_b, g_c) = jnp.split(h @ w_in, IN_OFFSETS, axis=-1)
    (aqc, akc, avc, azc, bcqc, bckvc, bkrc, bzc, cuc, cvc, czc, g_ac, g_bc, g_cc) = jnp.split(
        hc @ w_in, IN_OFFSETS, axis=-1)

    q_a = axial_rope(aq.reshape(b, n, A_HEADS, HEAD_DIM), row, col)
    k_a = axial_rope(ak.reshape(b, n, A_KV_HEADS, HEAD_DIM), row, col)
    v_a = av.reshape(b, n, A_KV_HEADS, HEAD_DIM)
    k_ac = akc.reshape(b, lc, A_KV_HEADS, HEAD_DIM)
    v_ac = avc.reshape(b, lc, A_KV_HEADS, HEAD_DIM)
    y_a = window_gqa(q_a, k_a, v_a, k_ac, v_ac, sink_a)

    qn, qr, kn, vb = mla_project(bcq, bckv, mla_gq, mla_gkv, w_uq, w_ukv)
    qr = axial_rope(qr, row, col)
    kr = axial_rope(bkr[:, :, None, :], row, col)[:, :, 0]
    qnc, qrc, knc, vbc = mla_project(bcqc, bckvc, mla_gq, mla_gkv, w_uq, w_ukv)
    y_b = mla_dense(qn, qr, kn, kr, vb, knc, bkrc, vbc)

    y_c = chunk_sgu(cu, cv, sgu_ln_g, sgu_ln_b, sgu_w, sgu_b)

    x_new = x + gate * merge_branches(y_a, az, g_a, y_b, bz, g_b, y_c, cz, g_c, w_pa, w_pb, w_pc, w_out)

    if need_ctx_out:
        y_ac = context_gqa(aqc.reshape(b, lc, A_HEADS, HEAD_DIM), k_ac, v_ac, sink_a)
        y_bc = mla_context(qnc, qrc, knc, bkrc, vbc)
        y_cc = chunk_sgu(cuc, cvc, sgu_ln_g, sgu_ln_b, sgu_w, sgu_b)
        xc_new = xc + gate_c * merge_branches(y_ac, azc, g_ac, y_bc, bzc, g_bc, y_cc, czc, g_cc,
                                              w_pa, w_pb, w_pc, w_out)
    else:
        xc_new = xc
    return x_new, xc_new


def setup_inputs(seed: int = 0) -> dict:
    key = jax.random.key(seed)
    ks = jax.random.split(key, 24)
    f32 = jnp.float32
    L = DEPTH
    D = D_MODEL

    def nrm(k, shape, s):
        return jax.random.normal(k, shape, f32) * s

    return {
        'x': nrm(ks[0], (BATCH, SEQ, D), 1.0),
        'c': nrm(ks[1], (BATCH, D), 1.0),
        'ctx': nrm(ks[2], (BATCH, CTX_LEN, D), 1.0),
        'c_ctx': nrm(ks[3], (D,), 1.0),
        'ada_w': nrm(ks[4], (L, D, 3 * D), 0.5 * D ** -0.5),
        'ada_b': nrm(ks[5], (L, 3 * D), 0.02),
        'norm_g': 1.0 + nrm(ks[6], (L, D), 0.02),
        'w_in': nrm(ks[7], (L, D, IN_COLS), D ** -0.5),
        'sink_a': nrm(ks[8], (L, A_HEADS), 0.5),
        'mla_gq': 1.0 + nrm(ks[9], (L, Q_LORA), 0.02),
        'mla_gkv': 1.0 + nrm(ks[10], (L, KV_LORA), 0.02),
        'w_uq': nrm(ks[11], (L, Q_LORA, B_HEADS * (NOPE_DIM + ROPE_DIM)), Q_LORA ** -0.5),
        'w_ukv': nrm(ks[12], (L, KV_LORA, B_HEADS * (NOPE_DIM + V_DIM)), KV_LORA ** -0.5),
        'sgu_ln_g': 1.0 + nrm(ks[13], (L, C_WIDTH), 0.02),
        'sgu_ln_b': nrm(ks[14], (L, C_WIDTH), 0.02),
        'sgu_w': nrm(ks[15], (L, C_GROUPS, CHUNK, CHUNK), CHUNK ** -0.5),
        'sgu_b': 1.0 + nrm(ks[16], (L, C_GROUPS, CHUNK), 0.1),
        'w_pa': nrm(ks[17], (L, A_WIDTH, D), A_WIDTH ** -0.5),
        'w_pb': nrm(ks[18], (L, B_WIDTH, D), B_WIDTH ** -0.5),
        'w_pc': nrm(ks[19], (L, C_WIDTH, D), C_WIDTH ** -0.5),
        'w_out': nrm(ks[20], (L, D, D), D ** -0.5),
        'final_g': 1.0 + nrm(ks[21], (D,), 0.02),
    }


def reference(x, c, ctx, c_ctx, ada_w, ada_b, norm_g, w_in, sink_a, mla_gq, mla_gkv, w_uq, w_ukv,
              sgu_ln_g, sgu_ln_b, sgu_w, sgu_b, w_pa, w_pb, w_pc, w_out, final_g):
    n = x.shape[1]
    rows = n // GRID_W
    row = jnp.repeat(jnp.arange(rows, dtype=jnp.float32), GRID_W)
    col = jnp.tile(jnp.arange(GRID_W, dtype=jnp.float32), rows)
    xc = ctx
    for l in range(DEPTH):
        x, xc = hybrid_layer(x, xc, c, c_ctx, row, col, ada_w[l], ada_b[l], norm_g[l], w_in[l],
                             sink_a[l], mla_gq[l], mla_gkv[l], w_uq[l], w_ukv[l], sgu_ln_g[l],
                             sgu_ln_b[l], sgu_w[l], sgu_b[l], w_pa[l], w_pb[l], w_pc[l], w_out[l],
                             l < DEPTH - 1)
    return rms_norm(x, final_g)
```

```python
import numpy as np
from contextlib import ExitStack
import concourse.bass as bass
import concourse.mybir as mybir
from concourse.bass_utils import run_bass_kernel_spmd

F32 = mybir.dt.float32
BF16 = mybir.dt.bfloat16
AF = mybir.ActivationFunctionType
ALU = mybir.AluOpType
AX = mybir.AxisListType

ENGS = ["pe", "act", "dve", "pool", "sp"]
SAME_ENGINE_SYNC = {"pe": False, "act": True, "dve": True, "pool": True, "sp": False}

D = 2048
NTOK = 2048
NCTX = 256
NLOC = NTOK + NCTX
NTILE = NLOC // 128
TBS = [(0, 512), (512, 512), (1024, 512), (1536, 512), (2048, 256)]
GROUPS = [[0, 1, 2], [3, 4]]
IN_COLS = 13888
EPS = 1e-6
THETA = 10000.0


class Buf:
    __slots__ = ("name", "w", "r")

    def __init__(self, name=""):
        self.name = name
        self.w = None
        self.r = []


class Sched:
    def __init__(self, nc, stack, n_dma_sems=56, n_cc=8):
        self.nc = nc
        self.streams = {e: [] for e in ENGS}
        self.esem = {e: stack.enter_context(nc.semaphore("es_" + e)) for e in ENGS}
        self.dsem = [stack.enter_context(nc.semaphore("ds_%d" % i)) for i in range(n_dma_sems)]
        self.csem = [stack.enter_context(nc.semaphore("cs_%d" % i)) for i in range(n_cc)]
        self.psem = [stack.enter_context(nc.semaphore("pf_%d" % i)) for i in range(4)]
        self.puse = [0] * 4
        self.pnext = 0
        self.ncc = 0
        self.duse = [0] * n_dma_sems
        self.dnext = 0
        self.dnext_sw = 0
        self.n_hw = n_dma_sems - 16
        self.seen = {e: {} for e in ENGS}
        self.targets = {e: set() for e in ENGS}
        self.cnt = {e: 0 for e in ENGS}

    def _need(self, eng, tok):
        if tok[0] == 'e' and tok[1] == eng and not SAME_ENGINE_SYNC[eng]:
            return False
        return self.seen[eng].get(tok[0:2], 0) < tok[2]

    def _waits(self, eng, reads, writes, extra=()):
        deps = {}

        def add(tok):
            if tok is None:
                return
            k = tok[0:2]
            if deps.get(k, 0) < tok[2]:
                deps[k] = tok[2]
        for b in reads:
            add(b.w)
        for b in writes:
            add(b.w)
            for t in b.r:
                add(t)
        for t in extra:
            add(t)
        out = []
        for k, v in deps.items():
            tok = (k[0], k[1], v)
            if self._need(eng, tok):
                out.append(tok)
                self.seen[eng][k] = v
                if k[0] == 'e':
                    self.targets[k[1]].add(v)
        return out

    def _mark(self, tok, reads, writes):
        k = tok[0:2]
        for b in writes:
            b.w = tok
            b.r = []
        for b in reads:
            b.r = [t for t in b.r if t[0:2] != k] + [tok]

    def op(self, eng, fn, reads=(), writes=()):
        waits = self._waits(eng, reads, writes)
        self.cnt[eng] += 1
        idx = self.cnt[eng]
        tok = ('e', eng, idx)
        self.streams[eng].append([waits, fn, 'op', idx])
        self._mark(tok, reads, writes)
        return tok

    def dma(self, eng, out_ap, in_ap, reads=(), writes=()):
        if eng == 'pool':
            i = self.n_hw + self.dnext_sw
            self.dnext_sw = (self.dnext_sw + 1) % (len(self.dsem) - self.n_hw)
        else:
            i = self.dnext
            self.dnext = (self.dnext + 1) % self.n_hw
        extra = []
        if self.duse[i] > 0:
            extra.append(('d', i, 16 * self.duse[i]))
        waits = self._waits(eng, reads, writes, extra)
        self.duse[i] += 1
        tok = ('d', i, 16 * self.duse[i])
        self.streams[eng].append([waits, (out_ap, in_ap), 'dma', i])
        self._mark(tok, reads, writes)
        return tok

    def dma_nobar(self, eng, out_ap, in_ap, reads=(), writes=()):
        i = self.pnext
        self.pnext = (self.pnext + 1) % len(self.psem)
        extra = []
        if self.puse[i] > 0:
            extra.append(('p', i, 16 * self.puse[i]))
        waits = self._waits(eng, reads, writes, extra)
        self.puse[i] += 1
        tok = ('p', i, 16 * self.puse[i])
        self.streams[eng].append([waits, (out_ap, in_ap), 'pdma', i])
        self._mark(tok, reads, writes)
        return tok

    def cc(self, fn, reads=(), writes=()):
        i = self.ncc
        self.ncc += 1
        waits = self._waits('pool', reads, writes)
        tok = ('c', i, 1)
        self.streams['pool'].append([waits, fn, 'cc', i])
        self._mark(tok, reads, writes)
        return tok

    def barrier(self):
        cnt = self.cnt
        extra = [('e', e, cnt[e]) for e in ENGS if e != 'sp' and cnt[e] > 0]
        extra += [('d', i, 16 * u) for i, u in enumerate(self.duse) if u > 0]
        waits = self._waits('sp', (), (), extra)
        cnt['sp'] += 1
        idx = cnt['sp']
        self.streams['sp'].append([waits, None, 'inc', idx])
        tok = ('e', 'sp', idx)
        for e in ENGS:
            if e == 'sp':
                continue
            w = self._waits(e, (), (), [tok])
            self.streams[e].append([w, None, 'nop', None])
            for e2 in ENGS:
                self.seen[e][('e', e2)] = max(self.seen[e].get(('e', e2), 0), cnt[e2])
            for i, u in enumerate(self.duse):
                self.seen[e][('d', i)] = 16 * u

    def wait_cc(self):
        extra = [('c', i, 1) for i in range(self.ncc)]
        extra += [('p', i, 16 * u) for i, u in enumerate(self.puse) if u > 0]
        w = self._waits('sp', (), (), extra)
        self.streams['sp'].append([w, None, 'nop', None])

    def replay(self, block):
        vals = {}
        for e in ENGS:
            tg = sorted(self.targets[e])
            vals[e] = {t: n + 1 for n, t in enumerate(tg)}

        def emit_stream(e, eng):
            for waits, fn, kind, info in self.streams[e]:
                for tok in waits:
                    if tok[0] == 'e':
                        eng.wait_ge(self.esem[tok[1]], vals[tok[1]][tok[2]])
                    elif tok[0] == 'd':
                        eng.wait_ge(self.dsem[tok[1]], tok[2])
                    elif tok[0] == 'p':
                        eng.wait_ge(self.psem[tok[1]], tok[2])
                    else:
                        eng.wait_ge(self.csem[tok[1]], tok[2])
                if kind == 'op':
                    ins = fn(eng)
                    if info in vals[e]:
                        ins.then_inc(self.esem[e], 1)
                elif kind == 'dma':
                    o, i_ = fn
                    eng.dma_start(out=o, in_=i_).then_inc(self.dsem[info], 16)
                elif kind == 'pdma':
                    o, i_ = fn
                    eng.dma_start(out=o, in_=i_).then_inc(self.psem[info], 16)
                elif kind == 'cc':
                    fn(eng).then_inc(self.csem[info], 1)
                elif kind == 'inc':
                    if info in vals[e]:
                        eng.sem_inc(self.esem[e], 1)

        @block.tensor
        def _(eng):
            emit_stream('pe', eng)

        @block.scalar
        def _(eng):
            emit_stream('act', eng)

        @block.vector
        def _(eng):
            emit_stream('dve', eng)

        @block.gpsimd
        def _(eng):
            emit_stream('pool', eng)

        @block.sync
        def _(eng):
            emit_stream('sp', eng)


def build(n_layers=2, dbg=()):
    nc = bass.Bass("TRN2", target_bir_lowering=False)

    def din(name, shape, dt=F32):
        return nc.dram_tensor(name, shape, dt, kind="ExternalInput").ap()

    def dint(name, shape, dt):
        return nc.dram_tensor(name, shape, dt, kind="Internal").ap()

    x_in = din("x", [NTOK, D])
    ctx_in = din("ctx", [NCTX, D])
    cvec = din("cvec", [128, 32])
    ada_w = din("ada_w", [2, D, 3 * D])
    ada_b = din("ada_b", [2, 2, 3 * D])
    normg = din("normg", [2, 128, 16])
    w_in = din("w_in", [2, D, IN_COLS])
    sinkb = din("sinkb", [2, 128, 8 * 128])
    gqb = din("gqb", [2, 128, 512])
    gkvb = din("gkvb", [2, 128, 512])
    w_uq = din("w_uq", [2, 512, 1536])
    w_ukv = din("w_ukv", [2, 512, 2048])
    lngb = din("lngb", [2, 128, 1024])
    lnbb = din("lnbb", [2, 128, 1024])
    wsT = din("wsT", [2, 128, 1024])
    bsb = din("bsb", [2, 128, 1024])
    w_p = [din("w_pa", [2, 1024, D]), din("w_pb", [2, 1024, D]), din("w_pc", [2, 1024, D])]
    w_out = din("w_out", [2, D, D])
    fgb = din("fgb", [128, D])
    rope = din("rope", [128, 4 * NTOK])
    cst = din("cst", [128, 580])
    masks_in = din("masks", [128, 4 * 512])
    out = nc.dram_tensor("out", [NTOK, D], F32, kind="ExternalOutput").ap()

    X1 = dint("X1", [NLOC, D], F32)
    SG = dint("SG", [128, 48 * NLOC], BF16)
    SZB = dint("SZB", [128, 8 * NLOC], BF16)
    CQN = dint("CQN", [128, 4 * NLOC], BF16)
    CKVC = dint("CKVC", [128, 4 * NCTX], BF16)
    KRC = dint("KRC", [64, NCTX], BF16)
    SND = [dint("SND0", [128, 4096], BF16), dint("SND1", [128, 4096], BF16), dint("SND2", [128, 3072], BF16)]
    GTH = [[dint("GTH%d_%d" % (l, i), [256, 4096 if i < 2 else 3072], BF16) for i in range(3)] for l in range(2)]
    KA = dint("KA", [128, 2 * NLOC], BF16)
    VA = dint("VA", [128, NTILE * 256], BF16)
    Y = dint("Y", [128, 24 * NLOC], BF16)
    MT = dint("MT", [128, 16 * NLOC], BF16)
    GATEB = dint("GATEB", [128, 2 * 2 * D], F32)
    dbg_out = {}
    for name, shape in dbg:
        dbg_out[name] = nc.dram_tensor("dbg_" + name, shape, F32, kind="ExternalOutput").ap()

    with ExitStack() as st:
        S = Sched(nc, st)

        uid = [0]

        def T(stack, name, shape, dt):
            uid[0] += 1
            return stack.enter_context(nc.sbuf_tensor("%s_u%d" % (name, uid[0]), shape, dt))

        cs = T(st, "cs", [128, 580], F32)
        ident = cs[:, 0:128]
        R128 = cs[:, 128:256]
        R64 = cs[0:64, 256:320]
        sel = cs[0:2, 320:576]
        I2 = cs[0:2, 576:578]
        ones = T(st, "ones", [128, 128], BF16)
        sact = T(st, "sact", [128, 32], BF16)
        cvt = T(st, "cvt", [128, 32], F32)
        G1 = [T(st, "G1_%d" % l, [128, 32], F32) for l in range(2)]
        S1 = [T(st, "S1_%d" % l, [128, 32], F32) for l in range(2)]
        sst = [T(st, "sst%d" % i, [128, 16], F32) for i in range(4)]
        ones32 = T(st, "ones32", [128, 128], F32)
        ps = [st.enter_context(nc.psum_tensor("ps%d" % i, [128, 1024], F32)) for i in range(4)]
        block = st.enter_context(nc.Block())

        def bank(b):
            return ps[b // 2][:, (b % 2) * 512:(b % 2) * 512 + 512]
        pb = [Buf("bank%d" % i) for i in range(8)]
        bcs, bones, bsact = Buf(), Buf(), Buf()
        bmods = [Buf(), Buf()]
        bss = [Buf(), Buf(), Buf(), Buf()]

        def pipeline(steps, depth=1):
            n = len(steps)
            for i in range(min(depth, n)):
                steps[i][0]()
            for i in range(n):
                if i + depth < n:
                    steps[i + depth][0]()
                steps[i][1]()

        def MM(o, lhsT, rhs, start, stop, r, w):
            S.op('pe', lambda e: e.matmul(o, lhsT, rhs, start=start, stop=stop), r, w)

        def TR(o, i_, r, w):
            S.op('pe', lambda e: e.transpose(out=o, in_=i_, identity=ident), list(r) + [bcs], w)

        def ACT(o, i_, func, r, w, scale=None, bias=None, accum=None):
            kw = {}
            if scale is not None:
                kw['scale'] = scale
            if bias is not None:
                kw['bias'] = bias
            if accum is not None:
                kw['accum_out'] = accum
            S.op('act', lambda e: e.activation(out=o, in_=i_, func=func, **kw), r, w)

        def TT(o, a, b, op, r, w, eng='dve'):
            S.op(eng, lambda e: e.tensor_tensor(out=o, in0=a, in1=b, op=op), r, w)

        def TS(o, a, s1, s2, op0, op1, r, w, eng='dve'):
            if s2 is None:
                S.op(eng, lambda e: e.tensor_scalar(out=o, in0=a, scalar1=s1, scalar2=None, op0=op0), r, w)
            else:
                S.op(eng, lambda e: e.tensor_scalar(out=o, in0=a, scalar1=s1, scalar2=s2, op0=op0, op1=op1), r, w)

        def STT(o, a, s, b, op0, op1, r, w, eng='dve'):
            S.op(eng, lambda e: e.scalar_tensor_tensor(out=o, in0=a, scalar=s, in1=b, op0=op0, op1=op1), r, w)

        def CP(o, i_, r, w, eng='dve'):
            if eng == 'act':
                S.op('act', lambda e: e.activation(out=o, in_=i_, func=AF.Copy), r, w)
            else:
                S.op(eng, lambda e: e.tensor_copy(out=o, in_=i_), r, w)

        def MS(o, v, w, eng='dve'):
            S.op(eng, lambda e: e.memset(o, v), (), w)

        def RCP(o, i_, r, w):
            S.op('dve', lambda e: e.reciprocal(out=o, in_=i_), r, w)

        def rstd_from_ss(k, n_inv, r_extra=()):
            t = sst[k]
            TS(t[:, 1:2], t[:, 0:1], n_inv, EPS, ALU.mult, ALU.add, [bss[k]], [bss[k]])
            ACT(t[:, 2:3], t[:, 1:2], AF.Sqrt, [bss[k]], [bss[k]])
            RCP(t[:, 3:4], t[:, 2:3], [bss[k]], [bss[k]])

        def wsrc(ap2d):
            return ap2d.rearrange("(k p) n -> p k n", p=128)

        S.dma('sp', cs[:], cst[:, :], writes=[bcs])
        S.dma('sp', cvt[:], cvec[:, :], writes=[bsact])
        MS(ones[:], 1.0, [bones])
        MS(ones32[:], 1.0, [bones])
        ACT(sact[:], cvt[:], AF.Silu, [bsact], [bsact])

        def mods_setup(l, ph):
            st_ = {}
            st_['wb'] = [T(ph, "mw%d" % i, [128, 16, 512], BF16) for i in range(2)]
            st_['bw'] = [Buf(), Buf()]
            st_['mrow'] = T(ph, "mrow", [2, 3 * D], F32)
            st_['brow'] = T(ph, "brow", [2, 3 * D], F32)
            st_['ngt'] = T(ph, "ngt", [128, 16], F32)
            st_['mcol'] = T(ph, "mcol", [128, 96], F32)
            st_['gt'] = T(ph, "gt", [128, D], F32)
            for nm in ('bmrow', 'bbrow', 'bng', 'bmcol', 'bgt'):
                st_[nm] = Buf()
            st_['l'] = l
            S.dma('sp', st_['brow'][:], ada_b[l], writes=[st_['bbrow']])
            S.dma('sp', st_['ngt'][:], normg[l], writes=[st_['bng']])
            return st_

        def mods_load(st_, nt):
            l = st_['l']
            S.dma('pool', st_['wb'][nt % 2][:], wsrc(ada_w[l][:, nt * 512:(nt + 1) * 512]), writes=[st_['bw'][nt % 2]])

        def mods_tile(st_, nt):
            wb, bw, mrow, brow = st_['wb'], st_['bw'], st_['mrow'], st_['brow']
            for kc in range(16):
                MM(bank(7)[0:2, :], sact[:, 2 * kc:2 * kc + 2], wb[nt % 2][:, kc, :], kc == 0, kc == 15,
                   [bsact, bw[nt % 2]], [pb[7]])
            TT(mrow[0:2, nt * 512:(nt + 1) * 512], bank(7)[0:2, :], brow[0:2, nt * 512:(nt + 1) * 512], ALU.add,
               [pb[7], st_['bbrow']], [st_['bmrow']])

        def mods_finish(st_):
            l = st_['l']
            mrow, ngt, mcol, gt = st_['mrow'], st_['ngt'], st_['mcol'], st_['gt']
            bmrow, bng, bmcol, bgt = st_['bmrow'], st_['bng'], st_['bmcol'], st_['bgt']
            for f in range(48):
                MM(bank(6)[:, 2 * f:2 * f + 2], mrow[0:2, f * 128:(f + 1) * 128], I2, True, True, [bmrow, bcs], [pb[6]])
            CP(mcol[:], bank(6)[:, 0:96], [pb[6]], [bmcol])
            mc3 = mcol[:].rearrange("p (f r) -> p f r", r=2)
            for r in range(2):
                STT(G1[l][:, r * 16:(r + 1) * 16], mc3[:, 16:32, r], 1.0, ngt[:], ALU.add, ALU.mult, [bmcol, bng], [bmods[l]])
                CP(S1[l][:, r * 16:(r + 1) * 16], mc3[:, 0:16, r], [bmcol], [bmods[l]])
            for r in range(2):
                for n in range(4):
                    MM(bank(5)[:, :], sel[:, r * 128:(r + 1) * 128], mrow[0:2, 2 * D + n * 512:2 * D + (n + 1) * 512], True, True,
                       [bmrow, bcs], [pb[5]])
                    CP(gt[:, n * 512:(n + 1) * 512], bank(5)[:, :], [pb[5]], [bgt], eng='act')
                S.dma('sp', GATEB[:, (l * 2 + r) * D:(l * 2 + r + 1) * D], gt[:], reads=[bgt])

        def phase_mods(l):
            with ExitStack() as ph:
                st_ = mods_setup(l, ph)
                for nt in range(12):
                    mods_load(st_, nt)
                    mods_tile(st_, nt)
                mods_finish(st_)
                S.barrier()

        rope_pend = []

        def rope_flush():
            cur = rope_pend[:]
            del rope_pend[:]
            for f in cur:
                f()

        def gemm_fm(wt, bw, c0, M, nk, rhs_of, rhs_bufs, consumer, tbs=(0, 1, 2, 3, 4)):
            for grp in GROUPS:
                g = [tb for tb in grp if tb in tbs]
                if not g:
                    continue
                for kc in range(nk):
                    for tb in g:
                        t0, n = TBS[tb]
                        MM(bank(tb)[0:M, 0:n], wt[:, kc, c0:c0 + M], rhs_of(kc, t0, n), kc == 0, kc == nk - 1,
                           [bw] + rhs_bufs(tb), [pb[tb]])
                rope_flush()
                for tb in g:
                    t0, n = TBS[tb]
                    consumer(tb, t0, n, bank(tb)[0:M, 0:n], pb[tb])

        def layer(l):
            last = (l == n_layers - 1)
            with ExitStack() as lay:
                hT = T(lay, "hT", [128, 16, NLOC], BF16)
                bhT = [Buf("hT%d" % i) for i in range(NTILE)]
                bgth = [Buf(), Buf(), Buf()]
                wl = ExitStack()
                wbL = [T(wl, "wbL%d" % i, [128, 16, 512], BF16) for i in range(2)]
                bwL = [Buf(), Buf()]
                pref = set()

                def wload(k, c0, ncol=512):
                    if (k, c0) in pref:
                        pref.discard((k, c0))
                        return
                    S.dma('pool', wbL[k][:, :, 0:ncol], wsrc(w_in[l][:, c0:c0 + ncol]), writes=[bwL[k]])

                def wprefetch(k, c0, ncol=512):
                    S.dma_nobar('pool', wbL[k][:, :, 0:ncol], wsrc(w_in[l][:, c0:c0 + ncol]), writes=[bwL[k]])
                    pref.add((k, c0))

                def h_rhs(kc, t0, n):
                    return hT[:, kc, t0:t0 + n]

                def h_bufs(tb):
                    t0, n = TBS[tb]
                    return bhT[t0 // 128:(t0 + n) // 128]

                def xsrc(i):
                    if l == 0:
                        return x_in[i * 128:(i + 1) * 128, :] if i < 16 else ctx_in[(i - 16) * 128:(i - 15) * 128, :]
                    return X1[i * 128:(i + 1) * 128, :]

                with ExitStack() as ph:
                    xt = [T(ph, "xt%d" % i, [128, D], F32) for i in range(4)]
                    junk = T(ph, "junk", [128, D], BF16)
                    bx = [Buf(), Buf(), Buf(), Buf()]
                    bj = Buf()

                    def p1A(i):
                        k = i % 4
                        S.dma('sp', xt[k][:], xsrc(i), writes=[bx[k]])
                        MS(sst[k][:, 0:1], 0.0, [bss[k]])
                        ACT(junk[:], xt[k][:], AF.Square, [bx[k]], [bj, bss[k]], accum=sst[k][:, 0:1])
                        rstd_from_ss(k, 1.0 / D)
                        TS(xt[k][:], xt[k][:], sst[k][:, 3:4], None, ALU.mult, None, [bss[k], bx[k]], [bx[k]])

                    def p1B(i):
                        k = i % 4
                        r = 0 if i < 16 else 1
                        for q in range(4):
                            b = 4 + q
                            for j in range(4):
                                kc = q * 4 + j
                                TR(bank(b)[:, j * 128:(j + 1) * 128], xt[k][:, kc * 128:(kc + 1) * 128], [bx[k]], [pb[b]])
                            for j in range(4):
                                kc = q * 4 + j
                                o = hT[:, kc, i * 128:(i + 1) * 128]
                                src = bank(b)[:, j * 128:(j + 1) * 128]
                                gc = G1[l][:, r * 16 + kc:r * 16 + kc + 1]
                                sc = S1[l][:, r * 16 + kc:r * 16 + kc + 1]
                                if j % 2 == 0:
                                    ACT(o, src, AF.Identity, [pb[b], bmods[l]], [bhT[i]], scale=gc, bias=sc)
                                else:
                                    TS(o, src, gc, sc, ALU.mult, ALU.add, [pb[b], bmods[l]], [bhT[i]])
                    pipeline([(lambda i=i: p1A(i), lambda i=i: p1B(i)) for i in range(NTILE)], depth=2)
                    wprefetch(0, 7744)
                    S.barrier()
                if 'hT' in dbg_out:
                    with ExitStack() as ph:
                        tmp = T(ph, "dbgt", [128, NLOC], F32)
                        bt = Buf()
                        for kc in range(16):
                            CP(tmp[:], hT[:, kc, :], bhT, [bt])
                            S.dma('sp', dbg_out['hT'][l * 16 + kc], tmp[:], reads=[bt])
                        S.barrier()

                def fm_to_dram(jobs, side_l=None):
                    with ExitStack() as ph:
                        wb = wbL
                        bw = bwL
                        stg = [T(ph, "stg%d" % i, [128, NLOC], BF16) for i in range(2)]
                        bstg = [Buf(), Buf()]
                        mst = mods_setup(side_l, ph) if side_l is not None else None
                        tiles = []
                        for (c0, nblk, func, dst) in jobs:
                            for wi in range((nblk + 3) // 4):
                                tiles.append((c0, nblk, func, dst, wi))
                        for ti, (c0, nblk, func, dst, wi) in enumerate(tiles):
                            ncol = min(512, nblk * 128 - wi * 512)
                            k = ti % 2
                            wload(k, c0 + wi * 512, ncol)
                            if mst is not None and ti < 12:
                                mods_load(mst, ti)
                                if ti >= 1:
                                    mods_tile(mst, ti - 1)
                            if mst is not None and ti == 12:
                                mods_tile(mst, 11)
                                mods_finish(mst)
                            for j in range(ncol // 128):
                                blk = wi * 4 + j
                                sk = (ti * 4 + j) % 2

                                def cons(tb, t0, n, pap, pbuf, sk=sk, func=func):
                                    ACT(stg[sk][:, t0:t0 + n], pap, func, [pbuf], [bstg[sk]])
                                gemm_fm(wb[k], bw[k], j * 128, 128, 16, h_rhs, h_bufs, cons)
                                S.dma('sp', dst[:, blk * NLOC:(blk + 1) * NLOC], stg[sk][:], reads=[bstg[sk]])
                        wprefetch(0, 2560)
                        wprefetch(1, 3072)
                        S.barrier()

                fm_to_dram([(7744, 48, AF.Sigmoid, SG), (3648, 8, AF.Silu, SZB)], side_l=(l + 1 if l + 1 < n_layers else None))

                def rope_fm(M, pap, pbuf, rawf, braw, Rm, cos, sin, tmpa, btmp, o, bo, pbank, btab):
                    CP(rawf[0:M, :], pap, [pbuf], [braw], eng='act')

                    def stage2():
                        MM(bank(pbank)[0:M, 0:512], Rm, rawf[0:M, :], True, True, [braw, bcs], [pb[pbank]])
                        TT(tmpa[0:M, :], rawf[0:M, :], cos, ALU.mult, [braw, btab], [btmp])
                        TT(rawf[0:M, :], bank(pbank)[0:M, 0:512], sin, ALU.mult, [pb[pbank], braw, btab], [braw])
                        TT(o, tmpa[0:M, :], rawf[0:M, :], ALU.add, [btmp, braw], [bo])
                    rope_pend.append(stage2)

                with ExitStack() as ph:
                    wb = wbL
                    bw = bwL
                    gB = [T(ph, "gB%d" % i, [128, 512], F32) for i in range(2)]
                    bgB = Buf()
                    tab = T(ph, "tab", [128, 4, NTOK], F32)
                    btab = Buf()
                    latT1 = T(ph, "latT", [128, 4, NLOC], BF16)
                    latT = [latT1, latT1]
                    blat1 = Buf()
                    blat = [blat1, blat1]
                    nrm = [T(ph, "nrm%d" % i, [128, 512], F32) for i in range(3)]
                    bnrm = [Buf(), Buf(), Buf()]
                    junk = T(ph, "junk2", [128, 512], BF16)
                    bj = Buf()
                    krT = T(ph, "krT", [128, NLOC], BF16)
                    bkr = Buf()
                    MS(krT[64:128, :], 0.0, [bkr])
                    akT = T(ph, "akT", [128, 2, NLOC], BF16)
                    bak = Buf()
                    avt = T(ph, "avt", [128, NTILE, 256], BF16)
                    bav = Buf()
                    rawf = [T(ph, "rawf%d" % i, [128, 512], F32) for i in range(3)]
                    braw = [Buf(), Buf(), Buf()]
                    tmpa = [T(ph, "tmpa%d" % i, [128, 512], F32) for i in range(3)]
                    btmp = [Buf(), Buf(), Buf()]
                    S.dma('sp', gB[0][:], gqb[l], writes=[bgB])
                    S.dma('sp', gB[1][:], gkvb[l], writes=[bgB])
                    S.dma('sp', tab[:].rearrange("p a t -> p (a t)"), rope[:, :], writes=[btab])
                    bsnd = [Buf(), Buf(), Buf()]
                    for which in range(2):
                        c0 = 2560 + which * 512
                        wload(which, c0)
                        def p2A(i, which=which):
                            k = i % 3
                            b = k
                            for kc in range(16):
                                MM(bank(b)[:, :], hT[:, kc, i * 128:(i + 1) * 128], wb[which][:, kc, :], kc == 0, kc == 15,
                                   [bhT[i], bw[which]], [pb[b]])
                            MS(sst[k][:, 0:1], 0.0, [bss[k]])
                            ACT(junk[:], bank(b)[:, :], AF.Square, [pb[b]], [bj, bss[k]], accum=sst[k][:, 0:1])
                            rstd_from_ss(k, 1.0 / 512)
                            STT(nrm[k][:], bank(b)[:, :], sst[k][:, 3:4], gB[which][:], ALU.mult, ALU.mult,
                                [pb[b], bss[k], bgB], [bnrm[k]])

                        def p2B(i, which=which):
                            k = i % 3
                            tbk = 3 + i % 2
                            for c in range(4):
                                TR(bank(tbk)[:, c * 128:(c + 1) * 128], nrm[k][:, c * 128:(c + 1) * 128], [bnrm[k]], [pb[tbk]])
                            CP(latT[which][:, :, i * 128:(i + 1) * 128], bank(tbk)[:, :].rearrange("p (c t) -> p c t", c=4),
                               [pb[tbk]], [blat[which]], eng='act')
                        pipeline([(lambda i=i: p2A(i), lambda i=i: p2B(i)) for i in range(NTILE)], depth=2)
                        if which == 0:
                            S.dma('sp', CQN[:, :].rearrange("p (c t) -> p c t", c=4), latT[0][:], reads=[blat[0]])
                    S.dma('sp', SND[0][:, :].rearrange("p (c t) -> p c t", c=2), latT[1][:, 0:2, 0:NTOK], reads=[blat[1]], writes=[bsnd[0]])
                    S.dma('sp', SND[1][:, :].rearrange("p (c t) -> p c t", c=2), latT[1][:, 2:4, 0:NTOK], reads=[blat[1]], writes=[bsnd[1]])
                    S.dma('sp', CKVC[:, :].rearrange("p (c t) -> p c t", c=4), latT[1][:, :, NTOK:NLOC], reads=[blat[1]])
                    wload(0, 3584, 64)

                    def cons_kr(tb, t0, n, pap, pbuf):
                        if tb < 4:
                            k = tb % 3
                            rope_fm(64, pap, pbuf, rawf[k], braw[k], R64, tab[0:64, 2, t0:t0 + n], tab[0:64, 3, t0:t0 + n],
                                    tmpa[k], btmp[k], krT[0:64, t0:t0 + n], bkr, 5 + k, btab)
                        else:
                            CP(krT[0:64, t0:t0 + n], pap, [pbuf], [bkr], eng='act')
                    gemm_fm(wb[0], bw[0], 0, 64, 16, h_rhs, h_bufs, cons_kr)
                    rope_flush()
                    S.dma('sp', SND[2][:, 0:NTOK], krT[:, 0:NTOK], reads=[bkr], writes=[bsnd[2]])
                    S.dma('sp', KRC[:, :], krT[0:64, NTOK:NLOC], reads=[bkr])
                    wload(1, 1024)
                    for h in range(2):
                        def cons_ak(tb, t0, n, pap, pbuf, h=h):
                            if tb < 4:
                                k = tb % 3
                                rope_fm(128, pap, pbuf, rawf[k], braw[k], R128, tab[:, 0, t0:t0 + n], tab[:, 1, t0:t0 + n],
                                        tmpa[k], btmp[k], akT[:, h, t0:t0 + n], bak, 5 + k, btab)
                            else:
                                CP(akT[:, h, t0:t0 + n], pap, [pbuf], [bak], eng='act')
                        gemm_fm(wb[1], bw[1], h * 128, 128, 16, h_rhs, h_bufs, cons_ak)
                    rope_flush()
                    for i in range(NTILE):
                        b = 5 + i % 2
                        for kc in range(16):
                            MM(bank(b)[:, 0:256], hT[:, kc, i * 128:(i + 1) * 128], wb[1][:, kc, 256:512], kc == 0, kc == 15,
                               [bhT[i], bw[1]], [pb[b]])
                        CP(avt[:, i, :], bank(b)[:, 0:256], [pb[b]], [bav], eng='act' if i % 2 else 'dve')
                    S.dma('sp', KA[:, :].rearrange("p (h t) -> p h t", h=2), akT[:], reads=[bak])
                    S.dma('sp', VA[:, :].rearrange("p (i c) -> p i c", c=256), avt[:], reads=[bav])
                    for which, t0 in ((0, 0), (1, NTOK - 128)):
                        S.dma('sp', SND[2][:, 2048 + which * 256:2048 + (which + 1) * 256].rearrange("p (h t) -> p h t", h=2),
                              akT[:, :, t0:t0 + 128], reads=[bak], writes=[bsnd[2]])
                        S.dma('sp', SND[2][:, 2560 + which * 256:2560 + (which + 1) * 256], avt[:, t0 // 128, :],
                              reads=[bav], writes=[bsnd[2]])
                    for i in range(3):
                        def ccf(e, i=i):
                            return e.collective_compute("AllGather", ALU.bypass, replica_groups=[[0, 1], [2, 3], [4, 5], [6, 7]],
                                                        ins=[SND[i][:, :]], outs=[GTH[l][i][:, :]])
                        S.cc(ccf, reads=[bsnd[i]], writes=[bgth[i]])
                    wprefetch(0, 5696)
                    wprefetch(1, 6208)
                    S.barrier()

                with ExitStack() as ph:
                    wb = wbL
                    bw = bwL
                    lnG = T(ph, "lnG", [128, 1024], F32)
                    lnB = T(ph, "lnB", [128, 1024], F32)
                    BS = T(ph, "BS", [128, 1024], F32)
                    wst = T(ph, "wst", [128, 8, 128], BF16)
                    bc3 = Buf()
                    MIX = T(ph, "MIX", [128, 8, NLOC], BF16)
                    bmix = [Buf() for _ in range(8)]
                    cvf = [T(ph, "cvf%d" % i, [128, 1024], F32) for i in range(2)]
                    bcvf = [Buf(), Buf()]
                    vn = [T(ph, "vn%d" % i, [128, 1024], BF16) for i in range(2)]
                    bvn = [Buf(), Buf()]
                    junk = T(ph, "junk3", [128, 1024], BF16)
                    bj = Buf()
                    szt = [T(ph, "szt%d" % i, [128, NLOC], BF16) for i in range(2)]
                    bsz = [Buf(), Buf()]
                    S.dma('sp', lnG[:], lngb[l], writes=[bc3])
                    S.dma('sp', lnB[:], lnbb[l], writes=[bc3])
                    S.dma('sp', BS[:], bsb[l], writes=[bc3])
                    S.dma('pool', wst[:].rearrange("p g q -> p (g q)"), wsT[l], writes=[bc3])
                    for hf in range(2):
                        wload(hf, 5696 + hf * 512)
                    def p3A(i):
                        k = i % 2
                        P = ps[k]
                        for hf in range(2):
                            for kc in range(16):
                                MM(P[:, hf * 512:(hf + 1) * 512], hT[:, kc, i * 128:(i + 1) * 128], wb[hf][:, kc, :], kc == 0, kc == 15,
                                   [bhT[i], bw[hf]], [pb[2 * k + hf]])
                        pbs = [pb[2 * k], pb[2 * k + 1]]
                        t = sst[k]
                        MS(t[:, 0:1], 0.0, [bss[k]])
                        S.op('dve', lambda e, t=t, P=P: e.reduce_sum(out=t[:, 4:5], in_=P[:, :], axis=AX.X), pbs, [bss[k]])
                        ACT(junk[:], P[:, :], AF.Square, pbs, [bj, bss[k]], accum=t[:, 0:1])
                        TS(t[:, 5:6], t[:, 4:5], 1.0 / 1024, None, ALU.mult, None, [bss[k]], [bss[k]])
                        TT(t[:, 6:7], t[:, 5:6], t[:, 5:6], ALU.mult, [bss[k]], [bss[k]])
                        TS(t[:, 7:8], t[:, 0:1], 1.0 / 1024, None, ALU.mult, None, [bss[k]], [bss[k]])
                        TT(t[:, 7:8], t[:, 7:8], t[:, 6:7], ALU.subtract, [bss[k]], [bss[k]])
                        TS(t[:, 1:2], t[:, 7:8], EPS, None, ALU.add, None, [bss[k]], [bss[k]])
                        ACT(t[:, 2:3], t[:, 1:2], AF.Sqrt, [bss[k]], [bss[k]])
                        RCP(t[:, 3:4], t[:, 2:3], [bss[k]], [bss[k]])
                        STT(t[:, 8:9], t[:, 5:6], -1.0, t[:, 3:4], ALU.mult, ALU.mult, [bss[k]], [bss[k]])
                        ACT(cvf[k][:], P[:, :], AF.Identity, pbs + [bss[k]], [bcvf[k]], scale=t[:, 3:4], bias=t[:, 8:9])
                        TT(cvf[k][:], cvf[k][:], lnG[:], ALU.mult, [bcvf[k], bc3], [bcvf[k]])
                        TT(vn[k][:], cvf[k][:], lnB[:], ALU.add, [bcvf[k], bc3], [bvn[k]])

                    def p3B(i):
                        k = i % 2
                        Pm = ps[2 + k]
                        for g in range(8):
                            MM(Pm[:, g * 128:(g + 1) * 128], vn[k][:, g * 128:(g + 1) * 128], wst[:, g, :], True, True,
                               [bvn[k], bc3], [pb[4 + 2 * k + g // 4]])
                        TT(MIX[:, :, i * 128:(i + 1) * 128], Pm[:, :].rearrange("p (g q) -> p g q", g=8),
                           BS[:].rearrange("p (g q) -> p g q", g=8), ALU.add, [pb[4 + 2 * k], pb[5 + 2 * k], bc3], bmix)
                    pipeline([(lambda i=i: p3A(i), lambda i=i: p3B(i)) for i in range(NTILE)])
                    for wi in range(2):
                        wload(wi, 6720 + wi * 512)
                        for j in range(4):
                            g = wi * 4 + j
                            sk = g % 2

                            def cons_cz(tb, t0, n, pap, pbuf, sk=sk):
                                ACT(szt[sk][:, t0:t0 + n], pap, AF.Silu, [pbuf], [bsz[sk]])
                            gemm_fm(wb[wi], bw[wi], j * 128, 128, 16, h_rhs, h_bufs, cons_cz)
                            TT(MIX[:, g, :], MIX[:, g, :], szt[sk][:], ALU.mult, [bmix[g], bsz[sk]], [bmix[g]])
                    for wi in range(2):
                        wload(wi, 4672 + wi * 512)
                        for j in range(4):
                            g = wi * 4 + j

                            def cons_cu(tb, t0, n, pap, pbuf, g=g):
                                TT(MIX[:, g, t0:t0 + n], pap, MIX[:, g, t0:t0 + n], ALU.mult, [pbuf, bmix[g]], [bmix[g]])
                            gemm_fm(wb[wi], bw[wi], j * 128, 128, 16, h_rhs, h_bufs, cons_cu)
                            S.dma('sp', Y[:, (16 + g) * NLOC:(17 + g) * NLOC], MIX[:, g, :], reads=[bmix[g]])
                    wprefetch(0, 0)
                    wprefetch(1, 1536)
                    S.barrier()

                with ExitStack() as ph:
                    wb = wbL
                    bw = bwL
                    tab = T(ph, "tab128", [128, 2, NTOK], F32)
                    mk = T(ph, "mk", [128, 4, 512], BF16)
                    skb = T(ph, "skb", [128, 1024], F32)
                    ESB = T(ph, "ESB", [128, 1024], F32)
                    bc5 = Buf()
                    S.dma('sp', tab[:], rope[:, 0:2 * NTOK].rearrange("p (a t) -> p a t", a=2), writes=[bc5])
                    S.dma('pool', mk[:].rearrange("p a t -> p (a t)"), masks_in[:, :], writes=[bc5])
                    S.dma('sp', skb[:], sinkb[l], writes=[bc5])
                    ACT(ESB[:], skb[:], AF.Exp, [bc5], [bc5])
                    KAT = T(ph, "KAT", [128, 20 * 128], BF16)
                    VAT = T(ph, "VAT", [128, 20, 128], BF16)
                    bkvl = [Buf() for _ in range(8)]
                    QT = T(ph, "QT", [128, 4, NLOC], BF16)
                    bqt = Buf()
                    SZ = T(ph, "SZ", [128, 4, NLOC], BF16)
                    bsz = Buf()
                    rawf = [T(ph, "arawf%d" % i, [128, 512], F32) for i in range(3)]
                    braw = [Buf(), Buf(), Buf()]
                    tmpa = [T(ph, "atmpa%d" % i, [128, 512], F32) for i in range(3)]
                    btmp = [Buf(), Buf(), Buf()]
                    PT = [T(ph, "aPT%d" % i, [128, 512], BF16) for i in range(3)]
                    bpt = [Buf(), Buf(), Buf()]
                    lt = [T(ph, "lt%d" % i, [128, 512], F32) for i in range(2)]
                    blt = [Buf(), Buf()]
                    of = [T(ph, "aof%d" % i, [128, 512], F32) for i in range(2)]
                    bof = [Buf(), Buf()]
                    scale_a = 128 ** -0.5
                    gi5 = [0]
                    G2 = GTH[l][2]
                    for g in range(2):
                        S.dma('sp', KAT[:, 128:128 + NTOK], KA[:, g * NLOC:g * NLOC + NTOK], writes=[bkvl[0]])
                        S.dma('sp', KAT[:, 18 * 128:20 * 128], KA[:, g * NLOC + NTOK:(g + 1) * NLOC], writes=[bkvl[1]])
                        S.dma('sp', KAT[:, 0:128], G2[0:128, 2048 + 256 + g * 128:2048 + 256 + (g + 1) * 128], reads=[bgth[2]], writes=[bkvl[2]])
                        S.dma('sp', KAT[:, 17 * 128:18 * 128], G2[128:256, 2048 + g * 128:2048 + (g + 1) * 128], reads=[bgth[2]], writes=[bkvl[3]])
                        va3 = VA[:, :].rearrange("p (i c) -> p i c", c=256)
                        S.dma('sp', VAT[:, 1:17, :], va3[:, 0:16, g * 128:(g + 1) * 128], writes=[bkvl[4]])
                        S.dma('sp', VAT[:, 18:20, :], va3[:, 16:18, g * 128:(g + 1) * 128], writes=[bkvl[5]])
                        S.dma('sp', VAT[:, 0, :], G2[0:128, 2560 + 256 + g * 128:2560 + 256 + (g + 1) * 128], reads=[bgth[2]], writes=[bkvl[6]])
                        S.dma('sp', VAT[:, 17, :], G2[128:256, 2560 + g * 128:2560 + (g + 1) * 128], reads=[bgth[2]], writes=[bkvl[7]])
                        wload(0, g * 512)
                        wload(1, 1536 + g * 512)
                        for hh in range(4):
                            def cons_q(tb, t0, n, pap, pbuf, hh=hh):
                                if tb < 4:
                                    k = tb % 3
                                    rope_fm(128, pap, pbuf, rawf[k], braw[k], R128, tab[:, 0, t0:t0 + n], tab[:, 1, t0:t0 + n],
                                            tmpa[k], btmp[k], QT[:, hh, t0:t0 + n], bqt, 5 + k, bc5)
                                else:
                                    CP(QT[:, hh, t0:t0 + n], pap, [pbuf], [bqt], eng='act')
                            gemm_fm(wb[0], bw[0], hh * 128, 128, 16, h_rhs, h_bufs, cons_q)
                        for hh in range(4):
                            def cons_z(tb, t0, n, pap, pbuf, hh=hh):
                                ACT(SZ[:, hh, t0:t0 + n], pap, AF.Silu, [pbuf], [bsz])
                            gemm_fm(wb[1], bw[1], hh * 128, 128, 16, h_rhs, h_bufs, cons_z)
                        rope_flush()
                        nqb = 16 if last else 18
                        steps = []
                        for qb in range(nqb):
                            k = qb % 2
                            q0 = qb * 128
                            if qb < 16:
                                keys = [(qb, 2 if qb == 0 else 0), (qb + 1, None), (qb + 2, 3 if qb == 15 else 1), (18, None), (19, None)]
                            else:
                                keys = [(18, None), (19, None)]
                            for idx, (kt, m) in enumerate(keys):
                                g_ = gi5[0]
                                gi5[0] += 1

                                def A(kt=kt, m=m, q0=q0, g_=g_):
                                    sb = 4 + g_ % 2
                                    p3 = g_ % 3
                                    MM(bank(sb)[:, :].rearrange("p (a q) -> p a q", a=4), KAT[:, kt * 128:(kt + 1) * 128], QT[:, :, q0:q0 + 128],
                                       True, True, bkvl + [bqt], [pb[sb]])
                                    ACT(PT[p3][:], bank(sb)[:, :], AF.Exp, [pb[sb]], [bpt[p3]], scale=scale_a)
                                    if m is not None:
                                        TT(PT[p3][:], PT[p3][:], mk[:, m, :], ALU.mult, [bpt[p3], bc5], [bpt[p3]])

                                def B(kt=kt, k=k, q0=q0, g_=g_, idx=idx, nk_=len(keys), g=g):
                                    p3 = g_ % 3
                                    bO = 0 + k
                                    bL = 2 + k
                                    MM(bank(bO)[:, :], VAT[:, kt, :], PT[p3][:], idx == 0, idx == nk_ - 1, bkvl + [bpt[p3]], [pb[bO]])
                                    MM(bank(bL)[:, :], ones[:, :], PT[p3][:], idx == 0, idx == nk_ - 1, [bones, bpt[p3]], [pb[bL]])
                                    if idx == nk_ - 1:
                                        TT(lt[k][:], bank(bL)[:, :], ESB[:, g * 512:(g + 1) * 512], ALU.add, [pb[bL], bc5], [blt[k]])
                                        ACT(lt[k][:], lt[k][:], AF.Ln, [blt[k]], [blt[k]])
                                        ACT(lt[k][:], lt[k][:], AF.Exp, [blt[k]], [blt[k]], scale=-1.0)
                                        TT(of[k][:], bank(bO)[:, :], lt[k][:], ALU.mult, [pb[bO], blt[k]], [bof[k]])
                                        TT(SZ[:, :, q0:q0 + 128], of[k][:].rearrange("p (a q) -> p a q", a=4), SZ[:, :, q0:q0 + 128], ALU.mult,
                                           [bof[k], bsz], [bsz])
                                steps.append((A, B))
                        pipeline(steps)
                        S.dma('sp', Y[:, (g * 4) * NLOC:(g * 4 + 4) * NLOC].rearrange("p (a t) -> p a t", a=4), SZ[:], reads=[bsz])
                    S.barrier()
                wl.close()
                with ExitStack() as ph:
                    NK = 2 * NTOK + NCTX
                    cqs = [T(ph, "cqs%d" % i, [128, 4, 512], BF16) for i in range(2)]
                    bcqs = [Buf(), Buf()]
                    CQN3 = CQN[:, :].rearrange("p (c t) -> p c t", c=4)
                    ckT = T(ph, "ckT", [128, 4, NK], BF16)
                    krF = T(ph, "krF", [128, NK], BF16)
                    tab = T(ph, "tab64", [64, 2, NTOK], F32)
                    bldl = [Buf() for _ in range(9)]
                    bld = bldl[0]
                    ii = 0
                    for gi in range(2):
                        for half in range(2):
                            S.dma('sp', ckT[:, 2 * gi:2 * gi + 2, half * NTOK:(half + 1) * NTOK],
                                  GTH[l][gi][half * 128:(half + 1) * 128, :].rearrange("p (c t) -> p c t", c=2), reads=[bgth[gi]], writes=[bldl[ii]])
                            ii += 1
                    S.dma('sp', ckT[:, :, 2 * NTOK:NK], CKVC[:, :].rearrange("p (c t) -> p c t", c=4), writes=[bldl[4]])
                    for half in range(2):
                        S.dma('sp', krF[0:64, half * NTOK:(half + 1) * NTOK], GTH[l][2][half * 128:half * 128 + 64, 0:NTOK], reads=[bgth[2]], writes=[bldl[5 + half]])
                    S.dma('sp', krF[0:64, 2 * NTOK:NK], KRC[:, :], writes=[bldl[7]])
                    S.dma('sp', tab[:], rope[0:64, 2 * NTOK:4 * NTOK].rearrange("p (a t) -> p a t", a=2), writes=[bldl[8]])
                    MS(krF[64:128, :], 0.0, [bldl[7]])
                    wq = [T(ph, "wq%d" % i, [128, 4, 192], BF16) for i in range(2)]
                    wkv = [T(ph, "wkv%d" % i, [128, 4, 256], BF16) for i in range(2)]
                    bwh = [Buf(), Buf()]
                    KnT = T(ph, "KnT", [128, NK], BF16)
                    bkn = Buf()
                    Vh = T(ph, "Vh", [128, NK // 128, 128], BF16)
                    bvh = Buf()
                    qn = [T(ph, "qn%d" % i, [128, 512], BF16) for i in range(2)]
                    qr = [T(ph, "qr%d" % i, [128, 512], BF16) for i in range(2)]
                    bq = [Buf(), Buf()]
                    for i_ in range(2):
                        MS(qr[i_][64:128, :], 0.0, [bq[i_]])
                    rawf = T(ph, "mrawf", [64, 512], F32)
                    braw = Buf()
                    tmpa = T(ph, "mtmpa", [64, 512], F32)
                    btmp = Buf()
                    PT = [T(ph, "PT%d" % i, [128, 2, 512], BF16) for i in range(3)]
                    bpt = [Buf(), Buf(), Buf()]
                    rinv = [T(ph, "rinv%d" % i, [128, 512], F32) for i in range(2)]
                    brinv = [Buf(), Buf()]
                    accD = [T(ph, "accD%d" % i, [128, 2, 512], F32) for i in range(2)]
                    accP = [T(ph, "accP%d" % i, [128, 2, 512], F32) for i in range(2)]
                    baccD = [Buf(), Buf()]
                    baccP = [Buf(), Buf()]
                    szb = [T(ph, "szb%d" % i, [128, 512], BF16) for i in range(3)]
                    bszb = [Buf(), Buf(), Buf()]
                    ybt = [T(ph, "ybt%d" % i, [128, 512], BF16) for i in range(2)]
                    bybt = [Buf(), Buf()]
                    scale_b = (128 + 64) ** -0.5
                    NKT = NK // 128

                    def load_w(h):
                        hk = h % 2
                        S.dma('pool', wq[hk][:], wsrc(w_uq[l][:, h * 192:(h + 1) * 192]), writes=[bwh[hk]])
                        S.dma('pool', wkv[hk][:], wsrc(w_ukv[l][:, h * 256:(h + 1) * 256]), writes=[bwh[hk]])

                    def kv_proj(h):
                        hk = h % 2
                        for kb in range((NK + 511) // 512):
                            k0 = kb * 512
                            n = min(512, NK - k0)
                            b = kb % 2
                            for kc in range(4):
                                MM(bank(b)[:, 0:n], wkv[hk][:, kc, 0:128], ckT[:, kc, k0:k0 + n], kc == 0, kc == 3, [bwh[hk]] + bldl, [pb[b]])
                            CP(KnT[:, k0:k0 + n], bank(b)[:, 0:n], [pb[b]], [bkn], eng='act')
                        for kt in range(NKT):
                            b = (kt // 4) % 2
                            j = kt % 4
                            for kc in range(4):
                                MM(bank(b)[:, j * 128:(j + 1) * 128], ckT[:, kc, kt * 128:(kt + 1) * 128], wkv[hk][:, kc, 128:256],
                                   kc == 0, kc == 3, [bwh[hk]] + bldl, [pb[b]])
                            if j == 3 or kt == NKT - 1:
                                k0 = kt - j
                                CP(Vh[:, k0:kt + 1, :], bank(b)[:, 0:(j + 1) * 128].rearrange("p (a d) -> p a d", d=128), [pb[b]], [bvh],
                                   eng='act' if (kt // 4) % 2 else 'dve')

                    def prologue(h, tb, k, part, z=0):
                        hk = h % 2
                        t0, n = TBS[tb]
                        if part == 0:
                            for kc in range(4):
                                MM(bank(0)[:, 0:n], wq[hk][:, kc, 0:128], cqs[k][:, kc, 0:n], kc == 0, kc == 3, [bwh[hk], bcqs[k]], [pb[0]])
                            CP(qn[k][:, 0:n], bank(0)[:, 0:n], [pb[0]], [bq[k]], eng='act')
                            for kc in range(4):
                                MM(bank(1)[0:64, 0:n], wq[hk][:, kc, 128:192], cqs[k][:, kc, 0:n], kc == 0, kc == 3, [bwh[hk], bcqs[k]], [pb[1]])
                            if tb < 4:
                                CP(rawf[0:64, :], bank(1)[0:64, 0:n], [pb[1]], [braw], eng='act')
                            else:
                                CP(qr[k][0:64, 0:n], bank(1)[0:64, 0:n], [pb[1]], [bq[k]], eng='dve')
                            S.dma('sp', szb[z][:, 0:n], SZB[:, h * NLOC + t0:h * NLOC + t0 + n], writes=[bszb[z]])
                        elif tb < 4:
                            MM(bank(1)[0:64, 0:512], R64, rawf[0:64, :], True, True, [braw, bcs], [pb[1]])
                            TT(tmpa[0:64, :], rawf[0:64, :], tab[0:64, 0, t0:t0 + n], ALU.mult, [braw, bldl[8]], [btmp])
                            TT(rawf[0:64, :], bank(1)[0:64, 0:512], tab[0:64, 1, t0:t0 + n], ALU.mult, [pb[1], braw, bldl[8]], [braw])
                            TT(qr[k][0:64, 0:n], tmpa[0:64, :], rawf[0:64, :], ALU.add, [btmp, braw], [bq[k]])

                    def load_cq(tb, k):
                        t0, n = TBS[tb]
                        S.dma('sp', cqs[k][:, :, 0:n], CQN3[:, :, t0:t0 + n], writes=[bcqs[k]])

                    def finalize(h, tb, k, usedP, z):
                        t0, n = TBS[tb]
                        bO = 2 + k
                        if usedP:
                            TT(accD[k][:, :, 0:n], accD[k][:, :, 0:n], accP[k][:, :, 0:n], ALU.add, [baccD[k], baccP[k]], [baccD[k]])
                        TT(rinv[k][:, 0:n], accD[k][:, 0, 0:n], accD[k][:, 1, 0:n], ALU.add, [baccD[k]], [brinv[k]])

                        def later():
                            MM(bank(1)[:, 0:n], ones32[:, :], rinv[k][:, 0:n], True, True, [bones, brinv[k]], [pb[1]])
                            ACT(rinv[k][:, 0:n], bank(1)[:, 0:n], AF.Ln, [pb[1]], [brinv[k]])
                            ACT(rinv[k][:, 0:n], rinv[k][:, 0:n], AF.Exp, [brinv[k]], [brinv[k]], scale=-1.0)
                            TT(rinv[k][:, 0:n], bank(bO)[:, 0:n], rinv[k][:, 0:n], ALU.mult, [pb[bO], brinv[k]], [brinv[k]])
                            TT(ybt[k][:, 0:n], rinv[k][:, 0:n], szb[z][:, 0:n], ALU.mult, [brinv[k], bszb[z]], [bybt[k]])
                            S.dma('sp', Y[:, (8 + h) * NLOC + t0:(8 + h) * NLOC + t0 + n], ybt[k][:, 0:n], reads=[bybt[k]])
                        deferred.append((unit_of[0], later))

                    def flush_deferred(upto=None):
                        while deferred and (upto is None or deferred[0][0] <= upto):
                            deferred.pop(0)[1]()

                    deferred = []
                    unit_of = [0]
                    units = [(h, tb) for h in range(8) for tb in range(5) if not (tb == 4 and last)]
                    gp = [0]
                    load_w(0)
                    load_cq(units[0][1], 0)
                    prologue(0, units[0][1], 0, 0, 0)
                    prologue(0, units[0][1], 0, 1, 0)
                    ui = 0
                    for h in range(8):
                        if h + 1 < 8:
                            load_w(h + 1)
                        kv_proj(h)
                        steps = []
                        while ui < len(units) and units[ui][0] == h:
                            _, tb = units[ui]
                            k = ui % 2
                            t0, n = TBS[tb]
                            kts = list(range(NKT)) if tb < 4 else [32, 33]
                            pairs = [(kts[2 * j], kts[2 * j + 1]) for j in range(len(kts) // 2)]
                            nxt = units[ui + 1] if ui + 1 < len(units) else None
                            npair = len(pairs)
                            hook0 = min(2, npair - 1)
                            hook1 = min(4, npair - 1)
                            hookf = min(5, npair - 1)
                            for j, (ka, kb_) in enumerate(pairs):
                                g_ = gp[0]
                                gp[0] += 1

                                def A(k=k, n=n, j=j, ka=ka, kb_=kb_, g_=g_, nxt=nxt, hf=(j == hookf), hk0=(j == hook0), hk1=(j == hook1), ui=ui, npair=npair):
                                    pp = 2 + g_ % 2
                                    p3 = g_ % 3
                                    for t, kt in enumerate((ka, kb_)):
                                        sb = 2 * pp + t
                                        MM(bank(sb)[:, 0:n], KnT[:, kt * 128:(kt + 1) * 128], qn[k][:, 0:n], True, False, [bkn, bq[k]], [pb[sb]])
                                        MM(bank(sb)[:, 0:n], krF[:, kt * 128:(kt + 1) * 128], qr[k][:, 0:n], False, True, bldl + [bq[k]], [pb[sb]])
                                    ACT(PT[p3][:, :, 0:n], ps[pp][:, :].rearrange("p (t q) -> p t q", t=2)[:, :, 0:n], AF.Exp,
                                        [pb[2 * pp], pb[2 * pp + 1]], [bpt[p3]], scale=scale_b)
                                    if j == 0 and nxt is not None:
                                        load_cq(nxt[1], (ui + 1) % 2)
                                    if j == min(1, npair - 1):
                                        flush_deferred(ui - 2)
                                    if hf:
                                        flush_deferred()
                                    if hk0 and nxt is not None:
                                        prologue(nxt[0], nxt[1], (ui + 1) % 2, 0, (ui + 1) % 3)
                                    if hk1 and nxt is not None:
                                        prologue(nxt[0], nxt[1], (ui + 1) % 2, 1, (ui + 1) % 3)

                                def B(h=h, tb=tb, k=k, n=n, j=j, ka=ka, kb_=kb_, g_=g_, npair=npair, ui=ui):
                                    p3 = g_ % 3
                                    bO = 2 + k
                                    for t, kt in enumerate((ka, kb_)):
                                        MM(bank(bO)[:, 0:n], Vh[:, kt, :], PT[p3][:, t, 0:n], j == 0 and t == 0, j == npair - 1 and t == 1,
                                           [bvh, bpt[p3]], [pb[bO]])
                                    if j % 2 == 1:
                                        eng, acc, bacc, first = 'pool', accP[k], baccP[k], (j == 1)
                                    else:
                                        eng, acc, bacc, first = 'dve', accD[k], baccD[k], (j == 0)
                                    if first:
                                        CP(acc[:, :, 0:n], PT[p3][:, :, 0:n], [bpt[p3]], [bacc], eng=eng)
                                    else:
                                        TT(acc[:, :, 0:n], acc[:, :, 0:n], PT[p3][:, :, 0:n], ALU.add, [bacc, bpt[p3]], [bacc], eng=eng)
                                    if j == npair - 1:
                                        unit_of[0] = ui
                                        finalize(h, tb, k, npair > 1, ui % 3)
                                steps.append((A, B))
                            ui += 1
                        pipeline(steps)
                    flush_deferred()
                    S.barrier()

            with ExitStack() as ph:
                YT = T(ph, "YT", [128, 24, NLOC], BF16)
                by2 = [[Buf() for _ in range(5)] for _ in range(3)]
                for tb_ in (0, 1, 2, 3, 4):
                    t0_, n_ = TBS[tb_]
                    for br in range(3):
                        S.dma('sp', YT[:, br * 8:(br + 1) * 8, t0_:t0_ + n_],
                              Y[:, br * 8 * NLOC:(br + 1) * 8 * NLOC].rearrange("p (a t) -> p a t", a=8)[:, :, t0_:t0_ + n_], writes=[by2[br][tb_]])
                wp = [[T(ph, "wp%d_%d" % (br, i), [128, 8, 256], BF16) for i in range(2)] for br in range(3)]
                bwp = [[Buf(), Buf()] for _ in range(3)]
                sgt = [[T(ph, "sgt%d_%d" % (br, i), [128, NLOC], BF16) for i in range(2)] for br in range(3)]
                bsg = [[Buf(), Buf()] for _ in range(3)]
                acc = T(ph, "acc", [128, NLOC], F32)
                tmp = T(ph, "mtmp", [128, NLOC], F32)
                bacc = [Buf() for _ in range(5)]
                btmp = [Buf() for _ in range(5)]
                mj = [T(ph, "mj%d" % i, [128, NLOC], BF16) for i in range(2)]
                bmj = [Buf(), Buf()]
                rot = [0]

                def nb(n):
                    r = [(rot[0] + i) % 8 for i in range(n)]
                    rot[0] = (rot[0] + n) % 8
                    return r
                for cg in range(8):
                    ck = cg % 2
                    for br in range(3):
                        S.dma('pool', wp[br][ck][:], wsrc(w_p[br][l][:, cg * 256:(cg + 1) * 256]), writes=[bwp[br][ck]])
                    for jj in range(2):
                        j = cg * 2 + jj
                        jk = j % 2
                        for br in range(3):
                            S.dma('sp', sgt[br][jk][:], SG[:, (br * 16 + j) * NLOC:(br * 16 + j + 1) * NLOC], writes=[bsg[br][jk]])
                        for grp in GROUPS:
                            for br in range(3):
                                bks = nb(len(grp))
                                for kc in range(8):
                                    for tb, b in zip(grp, bks):
                                        t0, n = TBS[tb]
                                        MM(bank(b)[:, 0:n], wp[br][ck][:, kc, jj * 128:(jj + 1) * 128], YT[:, br * 8 + kc, t0:t0 + n],
                                           kc == 0, kc == 7, [bwp[br][ck], by2[br][tb]], [pb[b]])
                                for tb, b in zip(grp, bks):
                                    t0, n = TBS[tb]
                                    sg = sgt[br][jk][:, t0:t0 + n]
                                    if br == 0:
                                        TT(acc[:, t0:t0 + n], bank(b)[:, 0:n], sg, ALU.mult, [pb[b], bsg[br][jk]], [bacc[tb]])
                                    elif br == 1:
                                        TT(tmp[:, t0:t0 + n], bank(b)[:, 0:n], sg, ALU.mult, [pb[b], bsg[br][jk]], [btmp[tb]])
                                        TT(acc[:, t0:t0 + n], acc[:, t0:t0 + n], tmp[:, t0:t0 + n], ALU.add, [bacc[tb], btmp[tb]], [bacc[tb]])
                                    else:
                                        TT(tmp[:, t0:t0 + n], bank(b)[:, 0:n], sg, ALU.mult, [pb[b], bsg[br][jk]], [btmp[tb]])
                                        TT(mj[jk][:, t0:t0 + n], acc[:, t0:t0 + n], tmp[:, t0:t0 + n], ALU.add, [bacc[tb], btmp[tb]], [bmj[jk]])
                        S.dma('sp', MT[:, j * NLOC:(j + 1) * NLOC], mj[jk][:], reads=[bmj[jk]])
                S.barrier()

            with ExitStack() as ph:
                mT = T(ph, "mT", [128, 16, NLOC], BF16)
                bm4 = [Buf() for _ in range(6)]
                for q_ in range(6):
                    S.dma('sp', mT[:, :, q_ * 384:(q_ + 1) * 384], MT[:, :].rearrange("p (a t) -> p a t", a=16)[:, :, q_ * 384:(q_ + 1) * 384], writes=[bm4[q_]])
                wo = [T(ph, "wo%d" % i, [128, 16, 512], BF16) for i in range(4)]
                bwo = [Buf() for _ in range(4)]
                for n4 in range(4):
                    S.dma('pool', wo[n4][:], wsrc(w_out[l][:, n4 * 512:(n4 + 1) * 512]), writes=[bwo[n4]])
                gB = [T(ph, "gateB%d" % r, [128, D], F32) for r in range(2)]
                bg = Buf()
                for r in range(2):
                    S.dma('sp', gB[r][:], GATEB[:, (l * 2 + r) * D:(l * 2 + r + 1) * D], writes=[bg])
                if last:
                    fg = T(ph, "fg", [128, D], F32)
                    S.dma('sp', fg[:], fgb[:, :], writes=[bg])
                    junk = T(ph, "junk7", [128, D], BF16)
                    bj = Buf()
                xt = [T(ph, "oxt%d" % i, [128, D], F32) for i in range(2)]
                bx = [Buf(), Buf()]
                ot = [T(ph, "ot%d" % i, [128, D], F32) for i in range(2)]
                bo = [Buf(), Buf()]
                for i in range(16 if last else NTILE):
                    k = i % 2
                    r = 0 if i < 16 else 1
                    S.dma('sp', xt[k][:], xsrc(i), writes=[bx[k]])
                    for n4 in range(4):
                        b = k * 4 + n4
                        for kc in range(16):
                            MM(bank(b)[:, :], mT[:, kc, i * 128:(i + 1) * 128], wo[n4][:, kc, :], kc == 0, kc == 15, [bm4[i // 3], bwo[n4]], [pb[b]])
                    for hf in range(2):
                        P = ps[k * 2 + hf]
                        sl = slice(hf * 1024, (hf + 1) * 1024)
                        TT(ot[k][:, sl], P[:, :], gB[r][:, sl], ALU.mult, [pb[k * 4 + 2 * hf], pb[k * 4 + 2 * hf + 1], bg], [bo[k]])
                        TT(ot[k][:, sl], ot[k][:, sl], xt[k][:, sl], ALU.add, [bo[k], bx[k]], [bo[k]])
                    if not last:
                        S.dma('sp', X1[i * 128:(i + 1) * 128, :], ot[k][:], reads=[bo[k]])
                    else:
                        MS(sst[k][:, 0:1], 0.0, [bss[k]])
                        ACT(junk[:], ot[k][:], AF.Square, [bo[k]], [bj, bss[k]], accum=sst[k][:, 0:1])
                        rstd_from_ss(k, 1.0 / D)
                        STT(ot[k][:], ot[k][:], sst[k][:, 3:4], fg[:], ALU.mult, ALU.mult, [bo[k], bss[k], bg], [bo[k]])
                        S.dma('sp', out[i * 128:(i + 1) * 128, :], ot[k][:], reads=[bo[k]])
                S.barrier()

        phase_mods(0)
        for l in range(n_layers):
            layer(l)
        S.wait_cc()
        S.barrier()
        S.replay(block)
    return nc


def _rope_tables(s):
    pos = (s * NTOK + np.arange(NTOK)).astype(np.float32)
    row = np.floor(pos / 64.0).astype(np.float32)
    col = (pos - row * 64.0).astype(np.float32)

    def tabs(dim):
        half = dim // 4
        inv = (np.float32(THETA) ** (-(np.arange(half, dtype=np.float32)) / np.float32(half))).astype(np.float32)
        cos = np.zeros((128, NTOK), np.float32)
        sin = np.zeros((128, NTOK), np.float32)
        for d in range(dim):
            axis = row if d < dim // 2 else col
            dd = d % (dim // 2)
            ang = (axis * inv[dd % half]).astype(np.float32)
            cos[d] = np.cos(ang)
            sin[d] = np.sin(ang) * (-1.0 if dd < half else 1.0)
        return cos, sin
    c128, s128 = tabs(128)
    c64, s64 = tabs(64)
    return np.ascontiguousarray(np.concatenate([c128, s128, c64, s64], axis=1))


def _perm(dim):
    half = dim // 4
    R = np.zeros((dim, dim), np.float32)
    for m in range(dim):
        dd = m % (dim // 2)
        partner = m + half if dd < half else m - half
        R[partner, m] = 1.0
    return R


def _consts():
    c = np.zeros((128, 580), np.float32)
    c[:, 0:128] = np.eye(128, dtype=np.float32)
    c[:, 128:256] = _perm(128)
    c[0:64, 256:320] = _perm(64)
    c[0, 320:448] = 1.0
    c[1, 448:576] = 1.0
    c[0, 576] = 1.0
    c[1, 577] = 1.0
    return c


def _masks(s):
    j = np.arange(128)[:, None]
    i = np.arange(128)[None, :]
    prev = (j >= i).astype(np.float32)
    nxt = (j <= i).astype(np.float32)
    m = np.zeros((128, 4, 4, 128), np.float32)
    m[:, 0] = prev[:, None, :]
    m[:, 1] = nxt[:, None, :]
    m[:, 2] = 0.0 if s == 0 else prev[:, None, :]
    m[:, 3] = 0.0 if s == 1 else nxt[:, None, :]
    return np.ascontiguousarray(m.reshape(128, 4 * 512))


def make_in_maps(x, c, ctx, c_ctx, ada_w, ada_b, norm_g, w_in, sink_a, mla_gq, mla_gkv, w_uq, w_ukv,
                 sgu_ln_g, sgu_ln_b, sgu_w, sgu_b, w_pa, w_pb, w_pc, w_out, final_g):
    f = lambda a: np.ascontiguousarray(np.asarray(a, dtype=np.float32))
    x, c, ctx, c_ctx = f(x), f(c), f(ctx), f(c_ctx)
    bc = lambda v, n: np.ascontiguousarray(np.broadcast_to(np.asarray(v, np.float32)[:, None, :], (2, 128, n)))
    shared = {
        "ada_w": f(ada_w),
        "ada_b": np.ascontiguousarray(np.broadcast_to(f(ada_b)[:, None, :], (2, 2, 3 * D))),
        "normg": np.ascontiguousarray(f(norm_g).reshape(2, 16, 128).transpose(0, 2, 1)),
        "w_in": f(w_in),
        "sinkb": np.ascontiguousarray(np.broadcast_to(f(sink_a)[:, None, :, None], (2, 128, 8, 128)).reshape(2, 128, 1024)),
        "gqb": bc(mla_gq, 512), "gkvb": bc(mla_gkv, 512),
        "w_uq": f(w_uq), "w_ukv": f(w_ukv),
        "lngb": bc(sgu_ln_g, 1024), "lnbb": bc(sgu_ln_b, 1024),
        "wsT": np.ascontiguousarray(f(sgu_w).transpose(0, 3, 1, 2).reshape(2, 128, 1024)),
        "bsb": np.ascontiguousarray(np.broadcast_to(f(sgu_b).reshape(2, 1, 1024), (2, 128, 1024))),
        "w_pa": f(w_pa), "w_pb": f(w_pb), "w_pc": f(w_pc), "w_out": f(w_out),
        "fgb": np.ascontiguousarray(np.broadcast_to(f(final_g)[None, :], (128, D))),
        "cst": _consts(),
    }
    maps = []
    for core in range(8):
        b, s = core // 2, core % 2
        cv = np.zeros((128, 32), np.float32)
        cv[:, 0::2] = c[b].reshape(16, 128).T
        cv[:, 1::2] = c_ctx.reshape(16, 128).T
        m = dict(shared)
        m["x"] = np.ascontiguousarray(x[b, s * NTOK:(s + 1) * NTOK])
        m["ctx"] = np.ascontiguousarray(ctx[b])
        m["cvec"] = cv
        m["rope"] = _rope_tables(s)
        m["masks"] = _masks(s)
        maps.append(m)
    return maps


def kernel(**inputs):
    nc = build()
    maps = make_in_maps(**inputs)
    res = run_bass_kernel_spmd(nc, maps, core_ids=list(range(8)))
    outp = np.zeros((4, 2 * NTOK, D), np.float32)
    for core in range(8):
        b, s = core // 2, core % 2
        outp[b, s * NTOK:(s + 1) * NTOK] = np.asarray(res.results[core]["out"], dtype=np.float32)
    return outp
```

```python
import numpy as np
from contextlib import ExitStack
import concourse.bass as bass
import concourse.mybir as mybir
from concourse.bass_utils import run_bass_kernel_spmd

F32 = mybir.dt.float32
BF16 = mybir.dt.bfloat16
AF = mybir.ActivationFunctionType
ALU = mybir.AluOpType
AX = mybir.AxisListType

ENGS = ["pe", "act", "dve", "pool", "sp"]
SAME_ENGINE_SYNC = {"pe": False, "act": True, "dve": True, "pool": True, "sp": False}

D = 2048
NTOK = 2048
NCTX = 256
NLOC = NTOK + NCTX
NTILE = NLOC // 128
TBS = [(0, 512), (512, 512), (1024, 512), (1536, 512), (2048, 256)]
GROUPS = [[0, 1, 2], [3, 4]]
IN_COLS = 13888
EPS = 1e-6
THETA = 10000.0


class Buf:
    __slots__ = ("name", "w", "r")

    def __init__(self, name=""):
        self.name = name
        self.w = None
        self.r = []


class Sched:
    def __init__(self, nc, stack, n_dma_sems=56, n_cc=8):
        self.nc = nc
        self.streams = {e: [] for e in ENGS}
        self.esem = {e: stack.enter_context(nc.semaphore("es_" + e)) for e in ENGS}
        self.dsem = [stack.enter_context(nc.semaphore("ds_%d" % i)) for i in range(n_dma_sems)]
        self.csem = [stack.enter_context(nc.semaphore("cs_%d" % i)) for i in range(n_cc)]
        self.psem = [stack.enter_context(nc.semaphore("pf_%d" % i)) for i in range(4)]
        self.puse = [0] * 4
        self.pnext = 0
        self.ncc = 0
        self.duse = [0] * n_dma_sems
        self.dnext = 0
        self.dnext_sw = 0
        self.n_hw = n_dma_sems - 16
        self.seen = {e: {} for e in ENGS}
        self.targets = {e: set() for e in ENGS}
        self.cnt = {e: 0 for e in ENGS}

    def _need(self, eng, tok):
        if tok[0] == 'e' and tok[1] == eng and not SAME_ENGINE_SYNC[eng]:
            return False
        return self.seen[eng].get(tok[0:2], 0) < tok[2]

    def _waits(self, eng, reads, writes, extra=()):
        deps = {}

        def add(tok):
            if tok is None:
                return
            k = tok[0:2]
            if deps.get(k, 0) < tok[2]:
                deps[k] = tok[2]
        for b in reads:
            add(b.w)
        for b in writes:
            add(b.w)
            for t in b.r:
                add(t)
        for t in extra:
            add(t)
        out = []
        for k, v in deps.items():
            tok = (k[0], k[1], v)
            if self._need(eng, tok):
                out.append(tok)
                self.seen[eng][k] = v
                if k[0] == 'e':
                    self.targets[k[1]].add(v)
        return out

    def _mark(self, tok, reads, writes):
        k = tok[0:2]
        for b in writes:
            b.w = tok
            b.r = []
        for b in reads:
            b.r = [t for t in b.r if t[0:2] != k] + [tok]

    def op(self, eng, fn, reads=(), writes=()):
        waits = self._waits(eng, reads, writes)
        self.cnt[eng] += 1
        idx = self.cnt[eng]
        tok = ('e', eng, idx)
        self.streams[eng].append([waits, fn, 'op', idx])
        self._mark(tok, reads, writes)
        return tok

    def dma(self, eng, out_ap, in_ap, reads=(), writes=()):
        if eng == 'pool':
            i = self.n_hw + self.dnext_sw
            self.dnext_sw = (self.dnext_sw + 1) % (len(self.dsem) - self.n_hw)
        else:
            i = self.dnext
            self.dnext = (self.dnext + 1) % self.n_hw
        extra = []
        if self.duse[i] > 0:
            extra.append(('d', i, 16 * self.duse[i]))
        waits = self._waits(eng, reads, writes, extra)
        self.duse[i] += 1
        tok = ('d', i, 16 * self.duse[i])
        self.streams[eng].append([waits, (out_ap, in_ap), 'dma', i])
        self._mark(tok, reads, writes)
        return tok

    def dma_nobar(self, eng, out_ap, in_ap, reads=(), writes=()):
        i = self.pnext
        self.pnext = (self.pnext + 1) % len(self.psem)
        extra = []
        if self.puse[i] > 0:
            extra.append(('p', i, 16 * self.puse[i]))
        waits = self._waits(eng, reads, writes, extra)
        self.puse[i] += 1
        tok = ('p', i, 16 * self.puse[i])
        self.streams[eng].append([waits, (out_ap, in_ap), 'pdma', i])
        self._mark(tok, reads, writes)
        return tok

    def cc(self, fn, reads=(), writes=()):
        i = self.ncc
        self.ncc += 1
        waits = self._waits('pool', reads, writes)
        tok = ('c', i, 1)
        self.streams['pool'].append([waits, fn, 'cc', i])
        self._mark(tok, reads, writes)
        return tok

    def barrier(self):
        cnt = self.cnt
        extra = [('e', e, cnt[e]) for e in ENGS if e != 'sp' and cnt[e] > 0]
        extra += [('d', i, 16 * u) for i, u in enumerate(self.duse) if u > 0]
        waits = self._waits('sp', (), (), extra)
        cnt['sp'] += 1
        idx = cnt['sp']
        self.streams['sp'].append([waits, None, 'inc', idx])
        tok = ('e', 'sp', idx)
        for e in ENGS:
            if e == 'sp':
                continue
            w = self._waits(e, (), (), [tok])
            self.streams[e].append([w, None, 'nop', None])
            for e2 in ENGS:
                self.seen[e][('e', e2)] = max(self.seen[e].get(('e', e2), 0), cnt[e2])
            for i, u in enumerate(self.duse):
                self.seen[e][('d', i)] = 16 * u

    def wait_cc(self):
        extra = [('c', i, 1) for i in range(self.ncc)]
        extra += [('p', i, 16 * u) for i, u in enumerate(self.puse) if u > 0]
        w = self._waits('sp', (), (), extra)
        self.streams['sp'].append([w, None, 'nop', None])

    def replay(self, block):
        vals = {}
        for e in ENGS:
            tg = sorted(self.targets[e])
            vals[e] = {t: n + 1 for n, t in enumerate(tg)}

        def emit_stream(e, eng):
            for waits, fn, kind, info in self.streams[e]:
                for tok in waits:
                    if tok[0] == 'e':
                        eng.wait_ge(self.esem[tok[1]], vals[tok[1]][tok[2]])
                    elif tok[0] == 'd':
                        eng.wait_ge(self.dsem[tok[1]], tok[2])
                    elif tok[0] == 'p':
                        eng.wait_ge(self.psem[tok[1]], tok[2])
                    else:
                        eng.wait_ge(self.csem[tok[1]], tok[2])
                if kind == 'op':
                    ins = fn(eng)
                    if info in vals[e]:
                        ins.then_inc(self.esem[e], 1)
                elif kind == 'dma':
                    o, i_ = fn
                    eng.dma_start(out=o, in_=i_).then_inc(self.dsem[info], 16)
                elif kind == 'pdma':
                    o, i_ = fn
                    eng.dma_start(out=o, in_=i_).then_inc(self.psem[info], 16)
                elif kind == 'cc':
                    fn(eng).then_inc(self.csem[info], 1)
                elif kind == 'inc':
                    if info in vals[e]:
                        eng.sem_inc(self.esem[e], 1)

        @block.tensor
        def _(eng):
            emit_stream('pe', eng)

        @block.scalar
        def _(eng):
            emit_stream('act', eng)

        @block.vector
        def _(eng):
            emit_stream('dve', eng)

        @block.gpsimd
        def _(eng):
            emit_stream('pool', eng)

        @block.sync
        def _(eng):
            emit_stream('sp', eng)


def build(n_layers=2, dbg=()):
    nc = bass.Bass("TRN2", target_bir_lowering=False)

    def din(name, shape, dt=F32):
        return nc.dram_tensor(name, shape, dt, kind="ExternalInput").ap()

    def dint(name, shape, dt):
        return nc.dram_tensor(name, shape, dt, kind="Internal").ap()

    x_in = din("x", [NTOK, D])
    ctx_in = din("ctx", [NCTX, D])
    cvec = din("cvec", [128, 32])
    ada_w = din("ada_w", [2, D, 3 * D])
    ada_b = din("ada_b", [2, 2, 3 * D])
    normg = din("normg", [2, 128, 16])
    w_in = din("w_in", [2, D, IN_COLS])
    sinkb = din("sinkb", [2, 128, 8 * 128])
    gqb = din("gqb", [2, 128, 512])
    gkvb = din("gkvb", [2, 128, 512])
    w_uq = din("w_uq", [2, 512, 1536])
    w_ukv = din("w_ukv", [2, 512, 2048])
    lngb = din("lngb", [2, 128, 1024])
    lnbb = din("lnbb", [2, 128, 1024])
    wsT = din("wsT", [2, 128, 1024])
    bsb = din("bsb", [2, 128, 1024])
    w_p = [din("w_pa", [2, 1024, D]), din("w_pb", [2, 1024, D]), din("w_pc", [2, 1024, D])]
    w_out = din("w_out", [2, D, D])
    fgb = din("fgb", [128, D])
    rope = din("rope", [128, 4 * NTOK])
    cst = din("cst", [128, 580])
    masks_in = din("masks", [128, 4 * 512])
    out = nc.dram_tensor("out", [NTOK, D], F32, kind="ExternalOutput").ap()

    X1 = dint("X1", [NLOC, D], F32)
    SG = dint("SG", [128, 48 * NLOC], BF16)
    SZB = dint("SZB", [128, 8 * NLOC], BF16)
    CQN = dint("CQN", [128, 4 * NLOC], BF16)
    CKVC = dint("CKVC", [128, 4 * NCTX], BF16)
    KRC = dint("KRC", [64, NCTX], BF16)
    SND = [dint("SND0", [128, 4096], BF16), dint("SND1", [128, 4096], BF16), dint("SND2", [128, 3072], BF16)]
    GTH = [[dint("GTH%d_%d" % (l, i), [256, 4096 if i < 2 else 3072], BF16) for i in range(3)] for l in range(2)]
    KA = dint("KA", [128, 2 * NLOC], BF16)
    VA = dint("VA", [128, NTILE * 256], BF16)
    Y = dint("Y", [128, 24 * NLOC], BF16)
    MT = dint("MT", [128, 16 * NLOC], BF16)
    GATEB = dint("GATEB", [128, 2 * 2 * D], F32)
    dbg_out = {}
    for name, shape in dbg:
        dbg_out[name] = nc.dram_tensor("dbg_" + name, shape, F32, kind="ExternalOutput").ap()

    with ExitStack() as st:
        S = Sched(nc, st)

        uid = [0]

        def T(stack, name, shape, dt):
            uid[0] += 1
            return stack.enter_context(nc.sbuf_tensor("%s_u%d" % (name, uid[0]), shape, dt))

        cs = T(st, "cs", [128, 580], F32)
        ident = cs[:, 0:128]
        R128 = cs[:, 128:256]
        R64 = cs[0:64, 256:320]
        sel = cs[0:2, 320:576]
        I2 = cs[0:2, 576:578]
        ones = T(st, "ones", [128, 128], BF16)
        sact = T(st, "sact", [128, 32], BF16)
        cvt = T(st, "cvt", [128, 32], F32)
        G1 = [T(st, "G1_%d" % l, [128, 32], F32) for l in range(2)]
        S1 = [T(st, "S1_%d" % l, [128, 32], F32) for l in range(2)]
        sst = [T(st, "sst%d" % i, [128, 16], F32) for i in range(4)]
        ones32 = T(st, "ones32", [128, 128], F32)
        epst = T(st, "epst", [128, 1], F32)
        ps = [st.enter_context(nc.psum_tensor("ps%d" % i, [128, 1024], F32)) for i in range(4)]
        block = st.enter_context(nc.Block())

        def bank(b):
            return ps[b // 2][:, (b % 2) * 512:(b % 2) * 512 + 512]
        pb = [Buf("bank%d" % i) for i in range(8)]
        bcs, bones, bsact = Buf(), Buf(), Buf()
        bmods = [Buf(), Buf()]
        bss = [Buf(), Buf(), Buf(), Buf()]

        def pipeline(steps, depth=1):
            n = len(steps)
            for i in range(min(depth, n)):
                steps[i][0]()
            for i in range(n):
                if i + depth < n:
                    steps[i + depth][0]()
                steps[i][1]()

        def MM(o, lhsT, rhs, start, stop, r, w):
            S.op('pe', lambda e: e.matmul(o, lhsT, rhs, start=start, stop=stop), r, w)

        def TR(o, i_, r, w):
            S.op('pe', lambda e: e.transpose(out=o, in_=i_, identity=ident), list(r) + [bcs], w)

        def ACT(o, i_, func, r, w, scale=None, bias=None, accum=None):
            kw = {}
            if scale is not None:
                kw['scale'] = scale
            if bias is not None:
                kw['bias'] = bias
            if accum is not None:
                kw['accum_out'] = accum
            S.op('act', lambda e: e.activation(out=o, in_=i_, func=func, **kw), r, w)

        def TT(o, a, b, op, r, w, eng='dve'):
            S.op(eng, lambda e: e.tensor_tensor(out=o, in0=a, in1=b, op=op), r, w)

        def TS(o, a, s1, s2, op0, op1, r, w, eng='dve'):
            if s2 is None:
                S.op(eng, lambda e: e.tensor_scalar(out=o, in0=a, scalar1=s1, scalar2=None, op0=op0), r, w)
            else:
                S.op(eng, lambda e: e.tensor_scalar(out=o, in0=a, scalar1=s1, scalar2=s2, op0=op0, op1=op1), r, w)

        def STT(o, a, s, b, op0, op1, r, w, eng='dve'):
            S.op(eng, lambda e: e.scalar_tensor_tensor(out=o, in0=a, scalar=s, in1=b, op0=op0, op1=op1), r, w)

        def CP(o, i_, r, w, eng='dve'):
            if eng == 'act':
                S.op('act', lambda e: e.activation(out=o, in_=i_, func=AF.Copy), r, w)
            else:
                S.op(eng, lambda e: e.tensor_copy(out=o, in_=i_), r, w)

        def MS(o, v, w, eng='dve'):
            S.op(eng, lambda e: e.memset(o, v), (), w)

        def RCP(o, i_, r, w):
            S.op('dve', lambda e: e.reciprocal(out=o, in_=i_), r, w)

        def rstd_from_ss(k, n_inv, r_extra=()):
            t = sst[k]
            ACT(t[:, 2:3], t[:, 0:1], AF.Sqrt, [bss[k], bones], [bss[k]], scale=n_inv, bias=epst[:, 0:1])
            RCP(t[:, 3:4], t[:, 2:3], [bss[k]], [bss[k]])

        def wsrc(ap2d):
            return ap2d.rearrange("(k p) n -> p k n", p=128)

        S.dma('sp', cs[:], cst[:, :], writes=[bcs])
        S.dma('sp', cvt[:], cvec[:, :], writes=[bsact])
        MS(ones[:], 1.0, [bones])
        MS(ones32[:], 1.0, [bones])
        MS(epst[:], EPS, [bones])
        ACT(sact[:], cvt[:], AF.Silu, [bsact], [bsact])

        def mods_setup(l, ph):
            st_ = {}
            st_['wb'] = [T(ph, "mw%d" % i, [128, 16, 512], BF16) for i in range(2)]
            st_['bw'] = [Buf(), Buf()]
            st_['mrow'] = T(ph, "mrow", [2, 3 * D], F32)
            st_['brow'] = T(ph, "brow", [2, 3 * D], F32)
            st_['ngt'] = T(ph, "ngt", [128, 16], F32)
            st_['mcol'] = T(ph, "mcol", [128, 96], F32)
            st_['gt'] = T(ph, "gt", [128, D], F32)
            for nm in ('bmrow', 'bbrow', 'bng', 'bmcol', 'bgt'):
                st_[nm] = Buf()
            st_['l'] = l
            S.dma('sp', st_['brow'][:], ada_b[l], writes=[st_['bbrow']])
            S.dma('sp', st_['ngt'][:], normg[l], writes=[st_['bng']])
            return st_

        def mods_load(st_, nt):
            l = st_['l']
            S.dma('pool', st_['wb'][nt % 2][:], wsrc(ada_w[l][:, nt * 512:(nt + 1) * 512]), writes=[st_['bw'][nt % 2]])

        def mods_tile(st_, nt):
            wb, bw, mrow, brow = st_['wb'], st_['bw'], st_['mrow'], st_['brow']
            for kc in range(16):
                MM(bank(7)[0:2, :], sact[:, 2 * kc:2 * kc + 2], wb[nt % 2][:, kc, :], kc == 0, kc == 15,
                   [bsact, bw[nt % 2]], [pb[7]])
            TT(mrow[0:2, nt * 512:(nt + 1) * 512], bank(7)[0:2, :], brow[0:2, nt * 512:(nt + 1) * 512], ALU.add,
               [pb[7], st_['bbrow']], [st_['bmrow']])

        def mods_finish(st_):
            l = st_['l']
            mrow, ngt, mcol, gt = st_['mrow'], st_['ngt'], st_['mcol'], st_['gt']
            bmrow, bng, bmcol, bgt = st_['bmrow'], st_['bng'], st_['bmcol'], st_['bgt']
            for f in range(48):
                MM(bank(6)[:, 2 * f:2 * f + 2], mrow[0:2, f * 128:(f + 1) * 128], I2, True, True, [bmrow, bcs], [pb[6]])
            CP(mcol[:], bank(6)[:, 0:96], [pb[6]], [bmcol])
            mc3 = mcol[:].rearrange("p (f r) -> p f r", r=2)
            for r in range(2):
                STT(G1[l][:, r * 16:(r + 1) * 16], mc3[:, 16:32, r], 1.0, ngt[:], ALU.add, ALU.mult, [bmcol, bng], [bmods[l]])
                CP(S1[l][:, r * 16:(r + 1) * 16], mc3[:, 0:16, r], [bmcol], [bmods[l]])
            for r in range(2):
                for n in range(4):
                    MM(bank(5)[:, :], sel[:, r * 128:(r + 1) * 128], mrow[0:2, 2 * D + n * 512:2 * D + (n + 1) * 512], True, True,
                       [bmrow, bcs], [pb[5]])
                    CP(gt[:, n * 512:(n + 1) * 512], bank(5)[:, :], [pb[5]], [bgt], eng='act')
                S.dma('sp', GATEB[:, (l * 2 + r) * D:(l * 2 + r + 1) * D], gt[:], reads=[bgt])

        def phase_mods(l):
            with ExitStack() as ph:
                st_ = mods_setup(l, ph)
                for nt in range(12):
                    mods_load(st_, nt)
                    mods_tile(st_, nt)
                mods_finish(st_)
                S.barrier()

        rope_pend = []

        def rope_flush():
            cur = rope_pend[:]
            del rope_pend[:]
            for f in cur:
                f()

        def gemm_fm(wt, bw, c0, M, nk, rhs_of, rhs_bufs, consumer, tbs=(0, 1, 2, 3, 4)):
            for grp in GROUPS:
                g = [tb for tb in grp if tb in tbs]
                if not g:
                    continue
                for kc in range(nk):
                    for tb in g:
                        t0, n = TBS[tb]
                        MM(bank(tb)[0:M, 0:n], wt[:, kc, c0:c0 + M], rhs_of(kc, t0, n), kc == 0, kc == nk - 1,
                           [bw] + rhs_bufs(tb), [pb[tb]])
                rope_flush()
                for tb in g:
                    t0, n = TBS[tb]
                    consumer(tb, t0, n, bank(tb)[0:M, 0:n], pb[tb])

        def layer(l):
            last = (l == n_layers - 1)
            with ExitStack() as lay:
                hT = T(lay, "hT", [128, 16, NLOC], BF16)
                bhT = [Buf("hT%d" % i) for i in range(NTILE)]
                bgth = [Buf(), Buf(), Buf()]
                wl = ExitStack()
                wbL = [T(wl, "wbL%d" % i, [128, 16, 512], BF16) for i in range(2)]
                bwL = [Buf(), Buf()]
                pref = set()

                def wload(k, c0, ncol=512):
                    if (k, c0) in pref:
                        pref.discard((k, c0))
                        return
                    S.dma('pool', wbL[k][:, :, 0:ncol], wsrc(w_in[l][:, c0:c0 + ncol]), writes=[bwL[k]])

                def wprefetch(k, c0, ncol=512):
                    S.dma_nobar('pool', wbL[k][:, :, 0:ncol], wsrc(w_in[l][:, c0:c0 + ncol]), writes=[bwL[k]])
                    pref.add((k, c0))

                def h_rhs(kc, t0, n):
                    return hT[:, kc, t0:t0 + n]

                def h_bufs(tb):
                    t0, n = TBS[tb]
                    return bhT[t0 // 128:(t0 + n) // 128]

                def xsrc(i):
                    if l == 0:
                        return x_in[i * 128:(i + 1) * 128, :] if i < 16 else ctx_in[(i - 16) * 128:(i - 15) * 128, :]
                    return X1[i * 128:(i + 1) * 128, :]

                with ExitStack() as ph:
                    xt = [T(ph, "xt%d" % i, [128, D], F32) for i in range(4)]
                    junk = T(ph, "junk", [128, D], BF16)
                    bx = [Buf(), Buf(), Buf(), Buf()]
                    bj = Buf()

                    def p1A(i):
                        k = i % 4
                        S.dma('sp', xt[k][:], xsrc(i), writes=[bx[k]])
                        MS(sst[k][:, 0:1], 0.0, [bss[k]])
                        ACT(junk[:], xt[k][:], AF.Square, [bx[k]], [bj, bss[k]], accum=sst[k][:, 0:1])
                        rstd_from_ss(k, 1.0 / D)
                        TS(xt[k][:], xt[k][:], sst[k][:, 3:4], None, ALU.mult, None, [bss[k], bx[k]], [bx[k]])

                    def p1B(i):
                        k = i % 4
                        r = 0 if i < 16 else 1
                        for q in range(4):
                            b = 4 + q
                            for j in range(4):
                                kc = q * 4 + j
                                TR(bank(b)[:, j * 128:(j + 1) * 128], xt[k][:, kc * 128:(kc + 1) * 128], [bx[k]], [pb[b]])
                            for j in range(4):
                                kc = q * 4 + j
                                o = hT[:, kc, i * 128:(i + 1) * 128]
                                src = bank(b)[:, j * 128:(j + 1) * 128]
                                gc = G1[l][:, r * 16 + kc:r * 16 + kc + 1]
                                sc = S1[l][:, r * 16 + kc:r * 16 + kc + 1]
                                if j % 2 == 0:
                                    ACT(o, src, AF.Identity, [pb[b], bmods[l]], [bhT[i]], scale=gc, bias=sc)
                                else:
                                    TS(o, src, gc, sc, ALU.mult, ALU.add, [pb[b], bmods[l]], [bhT[i]])
                    pipeline([(lambda i=i: p1A(i), lambda i=i: p1B(i)) for i in range(NTILE)], depth=2)
                    wprefetch(0, 7744)
                    S.barrier()
                if 'hT' in dbg_out:
                    with ExitStack() as ph:
                        tmp = T(ph, "dbgt", [128, NLOC], F32)
                        bt = Buf()
                        for kc in range(16):
                            CP(tmp[:], hT[:, kc, :], bhT, [bt])
                            S.dma('sp', dbg_out['hT'][l * 16 + kc], tmp[:], reads=[bt])
                        S.barrier()

                def fm_to_dram(jobs, side_l=None):
                    with ExitStack() as ph:
                        wb = wbL
                        bw = bwL
                        stg = [T(ph, "stg%d" % i, [128, NLOC], BF16) for i in range(2)]
                        bstg = [Buf(), Buf()]
                        mst = mods_setup(side_l, ph) if side_l is not None else None
                        tiles = []
                        for (c0, nblk, func, dst) in jobs:
                            for wi in range((nblk + 3) // 4):
                                tiles.append((c0, nblk, func, dst, wi))
                        for ti, (c0, nblk, func, dst, wi) in enumerate(tiles):
                            ncol = min(512, nblk * 128 - wi * 512)
                            k = ti % 2
                            wload(k, c0 + wi * 512, ncol)
                            if mst is not None and ti < 12:
                                mods_load(mst, ti)
                                if ti >= 1:
                                    mods_tile(mst, ti - 1)
                            if mst is not None and ti == 12:
                                mods_tile(mst, 11)
                                mods_finish(mst)
                            for j in range(ncol // 128):
                                blk = wi * 4 + j
                                sk = (ti * 4 + j) % 2

                                def cons(tb, t0, n, pap, pbuf, sk=sk, func=func):
                                    ACT(stg[sk][:, t0:t0 + n], pap, func, [pbuf], [bstg[sk]])
                                gemm_fm(wb[k], bw[k], j * 128, 128, 16, h_rhs, h_bufs, cons)
                                S.dma('sp', dst[:, blk * NLOC:(blk + 1) * NLOC], stg[sk][:], reads=[bstg[sk]])
                        wprefetch(0, 2560)
                        wprefetch(1, 3072)
                        S.barrier()

                fm_to_dram([(7744, 48, AF.Sigmoid, SG), (3648, 8, AF.Silu, SZB)], side_l=(l + 1 if l + 1 < n_layers else None))

                def rope_fm(M, pap, pbuf, rawf, braw, Rm, cos, sin, tmpa, btmp, o, bo, pbank, btab):
                    CP(rawf[0:M, :], pap, [pbuf], [braw], eng='act')

                    def stage2():
                        MM(bank(pbank)[0:M, 0:512], Rm, rawf[0:M, :], True, True, [braw, bcs], [pb[pbank]])
                        TT(tmpa[0:M, :], rawf[0:M, :], cos, ALU.mult, [braw, btab], [btmp])
                        TT(rawf[0:M, :], bank(pbank)[0:M, 0:512], sin, ALU.mult, [pb[pbank], braw, btab], [braw])
                        TT(o, tmpa[0:M, :], rawf[0:M, :], ALU.add, [btmp, braw], [bo])
                    rope_pend.append(stage2)

                with ExitStack() as ph:
                    wb = wbL
                    bw = bwL
                    gB = [T(ph, "gB%d" % i, [128, 512], F32) for i in range(2)]
                    bgB = Buf()
                    tab = T(ph, "tab", [128, 4, NTOK], F32)
                    btab = Buf()
                    latT1 = T(ph, "latT", [128, 4, NLOC], BF16)
                    latT = [latT1, latT1]
                    blat1 = Buf()
                    blat = [blat1, blat1]
                    nrm = [T(ph, "nrm%d" % i, [128, 512], F32) for i in range(3)]
                    bnrm = [Buf(), Buf(), Buf()]
                    junk = T(ph, "junk2", [128, 512], BF16)
                    bj = Buf()
                    krT = T(ph, "krT", [128, NLOC], BF16)
                    bkr = Buf()
                    MS(krT[64:128, :], 0.0, [bkr])
                    akT = T(ph, "akT", [128, 2, NLOC], BF16)
                    bak = Buf()
                    avt = T(ph, "avt", [128, NTILE, 256], BF16)
                    bav = Buf()
                    rawf = [T(ph, "rawf%d" % i, [128, 512], F32) for i in range(3)]
                    braw = [Buf(), Buf(), Buf()]
                    tmpa = [T(ph, "tmpa%d" % i, [128, 512], F32) for i in range(3)]
                    btmp = [Buf(), Buf(), Buf()]
                    S.dma('sp', gB[0][:], gqb[l], writes=[bgB])
                    S.dma('sp', gB[1][:], gkvb[l], writes=[bgB])
                    S.dma('sp', tab[:].rearrange("p a t -> p (a t)"), rope[:, :], writes=[btab])
                    bsnd = [Buf(), Buf(), Buf()]
                    for which in range(2):
                        c0 = 2560 + which * 512
                        wload(which, c0)
                        def p2A(i, which=which):
                            k = i % 3
                            b = k
                            for kc in range(16):
                                MM(bank(b)[:, :], hT[:, kc, i * 128:(i + 1) * 128], wb[which][:, kc, :], kc == 0, kc == 15,
                                   [bhT[i], bw[which]], [pb[b]])
                            MS(sst[k][:, 0:1], 0.0, [bss[k]])
                            ACT(junk[:], bank(b)[:, :], AF.Square, [pb[b]], [bj, bss[k]], accum=sst[k][:, 0:1])
                            rstd_from_ss(k, 1.0 / 512)
                            STT(nrm[k][:], bank(b)[:, :], sst[k][:, 3:4], gB[which][:], ALU.mult, ALU.mult,
                                [pb[b], bss[k], bgB], [bnrm[k]])

                        def p2B(i, which=which):
                            k = i % 3
                            tbk = 3 + i % 2
                            for c in range(4):
                                TR(bank(tbk)[:, c * 128:(c + 1) * 128], nrm[k][:, c * 128:(c + 1) * 128], [bnrm[k]], [pb[tbk]])
                            CP(latT[which][:, :, i * 128:(i + 1) * 128], bank(tbk)[:, :].rearrange("p (c t) -> p c t", c=4),
                               [pb[tbk]], [blat[which]], eng='act')
                        pipeline([(lambda i=i: p2A(i), lambda i=i: p2B(i)) for i in range(NTILE)], depth=2)
                        if which == 0:
                            S.dma('sp', CQN[:, :].rearrange("p (c t) -> p c t", c=4), latT[0][:], reads=[blat[0]])
                    S.dma('sp', SND[0][:, :].rearrange("p (c t) -> p c t", c=2), latT[1][:, 0:2, 0:NTOK], reads=[blat[1]], writes=[bsnd[0]])
                    S.dma('sp', SND[1][:, :].rearrange("p (c t) -> p c t", c=2), latT[1][:, 2:4, 0:NTOK], reads=[blat[1]], writes=[bsnd[1]])
                    S.dma('sp', CKVC[:, :].rearrange("p (c t) -> p c t", c=4), latT[1][:, :, NTOK:NLOC], reads=[blat[1]])
                    wload(0, 3584, 64)

                    def cons_kr(tb, t0, n, pap, pbuf):
                        if tb < 4:
                            k = tb % 3
                            rope_fm(64, pap, pbuf, rawf[k], braw[k], R64, tab[0:64, 2, t0:t0 + n], tab[0:64, 3, t0:t0 + n],
                                    tmpa[k], btmp[k], krT[0:64, t0:t0 + n], bkr, 5 + k, btab)
                        else:
                            CP(krT[0:64, t0:t0 + n], pap, [pbuf], [bkr], eng='act')
                    gemm_fm(wb[0], bw[0], 0, 64, 16, h_rhs, h_bufs, cons_kr)
                    rope_flush()
                    S.dma('sp', SND[2][:, 0:NTOK], krT[:, 0:NTOK], reads=[bkr], writes=[bsnd[2]])
                    S.dma('sp', KRC[:, :], krT[0:64, NTOK:NLOC], reads=[bkr])
                    wload(1, 1024)
                    for h in range(2):
                        def cons_ak(tb, t0, n, pap, pbuf, h=h):
                            if tb < 4:
                                k = tb % 3
                                rope_fm(128, pap, pbuf, rawf[k], braw[k], R128, tab[:, 0, t0:t0 + n], tab[:, 1, t0:t0 + n],
                                        tmpa[k], btmp[k], akT[:, h, t0:t0 + n], bak, 5 + k, btab)
                            else:
                                CP(akT[:, h, t0:t0 + n], pap, [pbuf], [bak], eng='act')
                        gemm_fm(wb[1], bw[1], h * 128, 128, 16, h_rhs, h_bufs, cons_ak)
                    rope_flush()
                    for i in range(NTILE):
                        b = 5 + i % 2
                        for kc in range(16):
                            MM(bank(b)[:, 0:256], hT[:, kc, i * 128:(i + 1) * 128], wb[1][:, kc, 256:512], kc == 0, kc == 15,
                               [bhT[i], bw[1]], [pb[b]])
                        CP(avt[:, i, :], bank(b)[:, 0:256], [pb[b]], [bav], eng='act' if i % 2 else 'dve')
                    S.dma('sp', KA[:, :].rearrange("p (h t) -> p h t", h=2), akT[:], reads=[bak])
                    S.dma('sp', VA[:, :].rearrange("p (i c) -> p i c", c=256), avt[:], reads=[bav])
                    for which, t0 in ((0, 0), (1, NTOK - 128)):
                        S.dma('sp', SND[2][:, 2048 + which * 256:2048 + (which + 1) * 256].rearrange("p (h t) -> p h t", h=2),
                              akT[:, :, t0:t0 + 128], reads=[bak], writes=[bsnd[2]])
                        S.dma('sp', SND[2][:, 2560 + which * 256:2560 + (which + 1) * 256], avt[:, t0 // 128, :],
                              reads=[bav], writes=[bsnd[2]])
                    for i in range(3):
                        def ccf(e, i=i):
                            return e.collective_compute("AllGather", ALU.bypass, replica_groups=[[0, 1], [2, 3], [4, 5], [6, 7]],
                                                        ins=[SND[i][:, :]], outs=[GTH[l][i][:, :]])
                        S.cc(ccf, reads=[bsnd[i]], writes=[bgth[i]])
                    wprefetch(0, 5696)
                    wprefetch(1, 6208)
                    S.barrier()

                with ExitStack() as ph:
                    wb = wbL
                    bw = bwL
                    lnG = T(ph, "lnG", [128, 1024], F32)
                    lnB = T(ph, "lnB", [128, 1024], F32)
                    BS = T(ph, "BS", [128, 1024], F32)
                    wst = T(ph, "wst", [128, 8, 128], BF16)
                    bc3 = Buf()
                    MIX = T(ph, "MIX", [128, 8, NLOC], BF16)
                    bmix = [Buf() for _ in range(8)]
                    cvf = [T(ph, "cvf%d" % i, [128, 1024], F32) for i in range(2)]
                    bcvf = [Buf(), Buf()]
                    vn = [T(ph, "vn%d" % i, [128, 1024], BF16) for i in range(2)]
                    bvn = [Buf(), Buf()]
                    junk = T(ph, "junk3", [128, 1024], BF16)
                    bj = Buf()
                    szt = [T(ph, "szt%d" % i, [128, NLOC], BF16) for i in range(2)]
                    bsz = [Buf(), Buf()]
                    S.dma('sp', lnG[:], lngb[l], writes=[bc3])
                    S.dma('sp', lnB[:], lnbb[l], writes=[bc3])
                    S.dma('sp', BS[:], bsb[l], writes=[bc3])
                    S.dma('pool', wst[:].rearrange("p g q -> p (g q)"), wsT[l], writes=[bc3])
                    for hf in range(2):
                        wload(hf, 5696 + hf * 512)
                    def p3A(i):
                        k = i % 2
                        P = ps[k]
                        for hf in range(2):
                            for kc in range(16):
                                MM(P[:, hf * 512:(hf + 1) * 512], hT[:, kc, i * 128:(i + 1) * 128], wb[hf][:, kc, :], kc == 0, kc == 15,
                                   [bhT[i], bw[hf]], [pb[2 * k + hf]])
                        pbs = [pb[2 * k], pb[2 * k + 1]]
                        t = sst[k]
                        MS(t[:, 0:1], 0.0, [bss[k]])
                        S.op('dve', lambda e, t=t, P=P: e.reduce_sum(out=t[:, 4:5], in_=P[:, :], axis=AX.X), pbs, [bss[k]])
                        ACT(junk[:], P[:, :], AF.Square, pbs, [bj, bss[k]], accum=t[:, 0:1])
                        TS(t[:, 5:6], t[:, 4:5], 1.0 / 1024, None, ALU.mult, None, [bss[k]], [bss[k]])
                        TT(t[:, 6:7], t[:, 5:6], t[:, 5:6], ALU.mult, [bss[k]], [bss[k]])
                        TS(t[:, 7:8], t[:, 0:1], 1.0 / 1024, None, ALU.mult, None, [bss[k]], [bss[k]])
                        TT(t[:, 7:8], t[:, 7:8], t[:, 6:7], ALU.subtract, [bss[k]], [bss[k]])
                        TS(t[:, 1:2], t[:, 7:8], EPS, None, ALU.add, None, [bss[k]], [bss[k]])
                        ACT(t[:, 2:3], t[:, 1:2], AF.Sqrt, [bss[k]], [bss[k]])
                        RCP(t[:, 3:4], t[:, 2:3], [bss[k]], [bss[k]])
                        STT(t[:, 8:9], t[:, 5:6], -1.0, t[:, 3:4], ALU.mult, ALU.mult, [bss[k]], [bss[k]])
                        ACT(cvf[k][:], P[:, :], AF.Identity, pbs + [bss[k]], [bcvf[k]], scale=t[:, 3:4], bias=t[:, 8:9])
                        TT(cvf[k][:], cvf[k][:], lnG[:], ALU.mult, [bcvf[k], bc3], [bcvf[k]])
                        TT(vn[k][:], cvf[k][:], lnB[:], ALU.add, [bcvf[k], bc3], [bvn[k]])

                    def p3B(i):
                        k = i % 2
                        Pm = ps[2 + k]
                        for g in range(8):
                            MM(Pm[:, g * 128:(g + 1) * 128], vn[k][:, g * 128:(g + 1) * 128], wst[:, g, :], True, True,
                               [bvn[k], bc3], [pb[4 + 2 * k + g // 4]])
                        TT(MIX[:, :, i * 128:(i + 1) * 128], Pm[:, :].rearrange("p (g q) -> p g q", g=8),
                           BS[:].rearrange("p (g q) -> p g q", g=8), ALU.add, [pb[4 + 2 * k], pb[5 + 2 * k], bc3], bmix)
                    pipeline([(lambda i=i: p3A(i), lambda i=i: p3B(i)) for i in range(NTILE)])
                    for wi in range(2):
                        wload(wi, 6720 + wi * 512)
                        for j in range(4):
                            g = wi * 4 + j
                            sk = g % 2

                            def cons_cz(tb, t0, n, pap, pbuf, sk=sk):
                                ACT(szt[sk][:, t0:t0 + n], pap, AF.Silu, [pbuf], [bsz[sk]])
                            gemm_fm(wb[wi], bw[wi], j * 128, 128, 16, h_rhs, h_bufs, cons_cz)
                            TT(MIX[:, g, :], MIX[:, g, :], szt[sk][:], ALU.mult, [bmix[g], bsz[sk]], [bmix[g]])
                    for wi in range(2):
                        wload(wi, 4672 + wi * 512)
                        for j in range(4):
                            g = wi * 4 + j

                            def cons_cu(tb, t0, n, pap, pbuf, g=g):
                                TT(MIX[:, g, t0:t0 + n], pap, MIX[:, g, t0:t0 + n], ALU.mult, [pbuf, bmix[g]], [bmix[g]])
                            gemm_fm(wb[wi], bw[wi], j * 128, 128, 16, h_rhs, h_bufs, cons_cu)
                            S.dma('sp', Y[:, (16 + g) * NLOC:(17 + g) * NLOC], MIX[:, g, :], reads=[bmix[g]])
                    wprefetch(0, 0)
                    wprefetch(1, 1536)
                    S.barrier()

                with ExitStack() as ph:
                    wb = wbL
                    bw = bwL
                    tab = T(ph, "tab128", [128, 2, NTOK], F32)
                    mk = T(ph, "mk", [128, 4, 512], BF16)
                    skb = T(ph, "skb", [128, 1024], F32)
                    ESB = T(ph, "ESB", [128, 1024], F32)
                    bc5 = Buf()
                    S.dma('sp', tab[:], rope[:, 0:2 * NTOK].rearrange("p (a t) -> p a t", a=2), writes=[bc5])
                    S.dma('pool', mk[:].rearrange("p a t -> p (a t)"), masks_in[:, :], writes=[bc5])
                    S.dma('sp', skb[:], sinkb[l], writes=[bc5])
                    ACT(ESB[:], skb[:], AF.Exp, [bc5], [bc5])
                    KAT = T(ph, "KAT", [128, 20 * 128], BF16)
                    VAT = T(ph, "VAT", [128, 20, 128], BF16)
                    bkvl = [Buf() for _ in range(8)]
                    QT = T(ph, "QT", [128, 4, NLOC], BF16)
                    bqt = Buf()
                    SZ = T(ph, "SZ", [128, 4, NLOC], BF16)
                    bsz = Buf()
                    rawf = [T(ph, "arawf%d" % i, [128, 512], F32) for i in range(3)]
                    braw = [Buf(), Buf(), Buf()]
                    tmpa = [T(ph, "atmpa%d" % i, [128, 512], F32) for i in range(3)]
                    btmp = [Buf(), Buf(), Buf()]
                    PT = [T(ph, "aPT%d" % i, [128, 512], BF16) for i in range(3)]
                    bpt = [Buf(), Buf(), Buf()]
                    lt = [T(ph, "lt%d" % i, [128, 512], F32) for i in range(2)]
                    blt = [Buf(), Buf()]
                    of = [T(ph, "aof%d" % i, [128, 512], F32) for i in range(2)]
                    bof = [Buf(), Buf()]
                    scale_a = 128 ** -0.5
                    gi5 = [0]
                    G2 = GTH[l][2]
                    for g in range(2):
                        S.dma('sp', KAT[:, 128:128 + NTOK], KA[:, g * NLOC:g * NLOC + NTOK], writes=[bkvl[0]])
                        S.dma('sp', KAT[:, 18 * 128:20 * 128], KA[:, g * NLOC + NTOK:(g + 1) * NLOC], writes=[bkvl[1]])
                        S.dma('sp', KAT[:, 0:128], G2[0:128, 2048 + 256 + g * 128:2048 + 256 + (g + 1) * 128], reads=[bgth[2]], writes=[bkvl[2]])
                        S.dma('sp', KAT[:, 17 * 128:18 * 128], G2[128:256, 2048 + g * 128:2048 + (g + 1) * 128], reads=[bgth[2]], writes=[bkvl[3]])
                        va3 = VA[:, :].rearrange("p (i c) -> p i c", c=256)
                        S.dma('sp', VAT[:, 1:17, :], va3[:, 0:16, g * 128:(g + 1) * 128], writes=[bkvl[4]])
                        S.dma('sp', VAT[:, 18:20, :], va3[:, 16:18, g * 128:(g + 1) * 128], writes=[bkvl[5]])
                        S.dma('sp', VAT[:, 0, :], G2[0:128, 2560 + 256 + g * 128:2560 + 256 + (g + 1) * 128], reads=[bgth[2]], writes=[bkvl[6]])
                        S.dma('sp', VAT[:, 17, :], G2[128:256, 2560 + g * 128:2560 + (g + 1) * 128], reads=[bgth[2]], writes=[bkvl[7]])
                        wload(0, g * 512)
                        wload(1, 1536 + g * 512)
                        for hh in range(4):
                            def cons_q(tb, t0, n, pap, pbuf, hh=hh):
                                if tb < 4:
                                    k = tb % 3
                                    rope_fm(128, pap, pbuf, rawf[k], braw[k], R128, tab[:, 0, t0:t0 + n], tab[:, 1, t0:t0 + n],
                                            tmpa[k], btmp[k], QT[:, hh, t0:t0 + n], bqt, 5 + k, bc5)
                                else:
                                    CP(QT[:, hh, t0:t0 + n], pap, [pbuf], [bqt], eng='act')
                            gemm_fm(wb[0], bw[0], hh * 128, 128, 16, h_rhs, h_bufs, cons_q)
                        for hh in range(4):
                            def cons_z(tb, t0, n, pap, pbuf, hh=hh):
                                ACT(SZ[:, hh, t0:t0 + n], pap, AF.Silu, [pbuf], [bsz])
                            gemm_fm(wb[1], bw[1], hh * 128, 128, 16, h_rhs, h_bufs, cons_z)
                        rope_flush()
                        nqb = 16 if last else 18
                        steps = []
                        for qb in range(nqb):
                            k = qb % 2
                            q0 = qb * 128
                            if qb < 16:
                                keys = [(qb, 2 if qb == 0 else 0), (qb + 1, None), (qb + 2, 3 if qb == 15 else 1), (18, None), (19, None)]
                            else:
                                keys = [(18, None), (19, None)]
                            for idx, (kt, m) in enumerate(keys):
                                g_ = gi5[0]
                                gi5[0] += 1

                                def A(kt=kt, m=m, q0=q0, g_=g_):
                                    sb = 4 + g_ % 2
                                    p3 = g_ % 3
                                    MM(bank(sb)[:, :].rearrange("p (a q) -> p a q", a=4), KAT[:, kt * 128:(kt + 1) * 128], QT[:, :, q0:q0 + 128],
                                       True, True, bkvl + [bqt], [pb[sb]])
                                    ACT(PT[p3][:], bank(sb)[:, :], AF.Exp, [pb[sb]], [bpt[p3]], scale=scale_a)
                                    if m is not None:
                                        TT(PT[p3][:], PT[p3][:], mk[:, m, :], ALU.mult, [bpt[p3], bc5], [bpt[p3]])

                                def B(kt=kt, k=k, q0=q0, g_=g_, idx=idx, nk_=len(keys), g=g):
                                    p3 = g_ % 3
                                    bO = 0 + k
                                    bL = 2 + k
                                    MM(bank(bO)[:, :], VAT[:, kt, :], PT[p3][:], idx == 0, idx == nk_ - 1, bkvl + [bpt[p3]], [pb[bO]])
                                    MM(bank(bL)[:, :], ones[:, :], PT[p3][:], idx == 0, idx == nk_ - 1, [bones, bpt[p3]], [pb[bL]])
                                    if idx == nk_ - 1:
                                        TT(lt[k][:], bank(bL)[:, :], ESB[:, g * 512:(g + 1) * 512], ALU.add, [pb[bL], bc5], [blt[k]])
                                        ACT(lt[k][:], lt[k][:], AF.Ln, [blt[k]], [blt[k]])
                                        ACT(lt[k][:], lt[k][:], AF.Exp, [blt[k]], [blt[k]], scale=-1.0)
                                        TT(of[k][:], bank(bO)[:, :], lt[k][:], ALU.mult, [pb[bO], blt[k]], [bof[k]])
                                        TT(SZ[:, :, q0:q0 + 128], of[k][:].rearrange("p (a q) -> p a q", a=4), SZ[:, :, q0:q0 + 128], ALU.mult,
                                           [bof[k], bsz], [bsz])
                                steps.append((A, B))
                        pipeline(steps)
                        S.dma('sp', Y[:, (g * 4) * NLOC:(g * 4 + 4) * NLOC].rearrange("p (a t) -> p a t", a=4), SZ[:], reads=[bsz])
                    S.barrier()
                wl.close()
                with ExitStack() as ph:
                    NK = 2 * NTOK + NCTX
                    cqs = [T(ph, "cqs%d" % i, [128, 4, 512], BF16) for i in range(2)]
                    bcqs = [Buf(), Buf()]
                    CQN3 = CQN[:, :].rearrange("p (c t) -> p c t", c=4)
                    ckT = T(ph, "ckT", [128, 4, NK], BF16)
                    krF = T(ph, "krF", [128, NK], BF16)
                    tab = T(ph, "tab64", [64, 2, NTOK], F32)
                    bldl = [Buf() for _ in range(9)]
                    bld = bldl[0]
                    ii = 0
                    for gi in range(2):
                        for half in range(2):
                            S.dma('sp', ckT[:, 2 * gi:2 * gi + 2, half * NTOK:(half + 1) * NTOK],
                                  GTH[l][gi][half * 128:(half + 1) * 128, :].rearrange("p (c t) -> p c t", c=2), reads=[bgth[gi]], writes=[bldl[ii]])
                            ii += 1
                    S.dma('sp', ckT[:, :, 2 * NTOK:NK], CKVC[:, :].rearrange("p (c t) -> p c t", c=4), writes=[bldl[4]])
                    for half in range(2):
                        S.dma('sp', krF[0:64, half * NTOK:(half + 1) * NTOK], GTH[l][2][half * 128:half * 128 + 64, 0:NTOK], reads=[bgth[2]], writes=[bldl[5 + half]])
                    S.dma('sp', krF[0:64, 2 * NTOK:NK], KRC[:, :], writes=[bldl[7]])
                    S.dma('sp', tab[:], rope[0:64, 2 * NTOK:4 * NTOK].rearrange("p (a t) -> p a t", a=2), writes=[bldl[8]])
                    MS(krF[64:128, :], 0.0, [bldl[7]])
                    wq = [T(ph, "wq%d" % i, [128, 4, 192], BF16) for i in range(2)]
                    wkv = [T(ph, "wkv%d" % i, [128, 4, 256], BF16) for i in range(2)]
                    bwh = [Buf(), Buf()]
                    KnT = T(ph, "KnT", [128, NK], BF16)
                    bkn = Buf()
                    Vh = T(ph, "Vh", [128, NK // 128, 128], BF16)
                    bvh = Buf()
                    qn = [T(ph, "qn%d" % i, [128, 512], BF16) for i in range(2)]
                    qr = [T(ph, "qr%d" % i, [128, 512], BF16) for i in range(2)]
                    bq = [Buf(), Buf()]
                    for i_ in range(2):
                        MS(qr[i_][64:128, :], 0.0, [bq[i_]])
                    rawf = T(ph, "mrawf", [64, 512], F32)
                    braw = Buf()
                    tmpa = T(ph, "mtmpa", [64, 512], F32)
                    btmp = Buf()
                    PT = [T(ph, "PT%d" % i, [128, 2, 512], BF16) for i in range(3)]
                    bpt = [Buf(), Buf(), Buf()]
                    rinv = [T(ph, "rinv%d" % i, [128, 512], F32) for i in range(2)]
                    brinv = [Buf(), Buf()]
                    accD = [T(ph, "accD%d" % i, [128, 2, 512], F32) for i in range(2)]
                    accP = [T(ph, "accP%d" % i, [128, 2, 512], F32) for i in range(2)]
                    baccD = [Buf(), Buf()]
                    baccP = [Buf(), Buf()]
                    szb = [T(ph, "szb%d" % i, [128, 512], BF16) for i in range(3)]
                    bszb = [Buf(), Buf(), Buf()]
                    ybt = [T(ph, "ybt%d" % i, [128, 512], BF16) for i in range(2)]
                    bybt = [Buf(), Buf()]
                    scale_b = (128 + 64) ** -0.5
                    NKT = NK // 128

                    def load_w(h):
                        hk = h % 2
                        S.dma('pool', wq[hk][:], wsrc(w_uq[l][:, h * 192:(h + 1) * 192]), writes=[bwh[hk]])
                        S.dma('pool', wkv[hk][:], wsrc(w_ukv[l][:, h * 256:(h + 1) * 256]), writes=[bwh[hk]])

                    def kv_proj(h):
                        hk = h % 2
                        for kb in range((NK + 511) // 512):
                            k0 = kb * 512
                            n = min(512, NK - k0)
                            b = kb % 2
                            for kc in range(4):
                                MM(bank(b)[:, 0:n], wkv[hk][:, kc, 0:128], ckT[:, kc, k0:k0 + n], kc == 0, kc == 3, [bwh[hk]] + bldl, [pb[b]])
                            CP(KnT[:, k0:k0 + n], bank(b)[:, 0:n], [pb[b]], [bkn], eng='act')
                        for kt in range(NKT):
                            b = (kt // 4) % 2
                            j = kt % 4
                            for kc in range(4):
                                MM(bank(b)[:, j * 128:(j + 1) * 128], ckT[:, kc, kt * 128:(kt + 1) * 128], wkv[hk][:, kc, 128:256],
                                   kc == 0, kc == 3, [bwh[hk]] + bldl, [pb[b]])
                            if j == 3 or kt == NKT - 1:
                                k0 = kt - j
                                CP(Vh[:, k0:kt + 1, :], bank(b)[:, 0:(j + 1) * 128].rearrange("p (a d) -> p a d", d=128), [pb[b]], [bvh],
                                   eng='act' if (kt // 4) % 2 else 'dve')

                    def prologue(h, tb, k, part, z=0):
                        hk = h % 2
                        t0, n = TBS[tb]
                        if part == 0:
                            for kc in range(4):
                                MM(bank(0)[:, 0:n], wq[hk][:, kc, 0:128], cqs[k][:, kc, 0:n], kc == 0, kc == 3, [bwh[hk], bcqs[k]], [pb[0]])
                            CP(qn[k][:, 0:n], bank(0)[:, 0:n], [pb[0]], [bq[k]], eng='act')
                            for kc in range(4):
                                MM(bank(1)[0:64, 0:n], wq[hk][:, kc, 128:192], cqs[k][:, kc, 0:n], kc == 0, kc == 3, [bwh[hk], bcqs[k]], [pb[1]])
                            if tb < 4:
                                CP(rawf[0:64, :], bank(1)[0:64, 0:n], [pb[1]], [braw], eng='act')
                            else:
                                CP(qr[k][0:64, 0:n], bank(1)[0:64, 0:n], [pb[1]], [bq[k]], eng='dve')
                            S.dma('sp', szb[z][:, 0:n], SZB[:, h * NLOC + t0:h * NLOC + t0 + n], writes=[bszb[z]])
                        elif tb < 4:
                            MM(bank(1)[0:64, 0:512], R64, rawf[0:64, :], True, True, [braw, bcs], [pb[1]])
                            TT(tmpa[0:64, :], rawf[0:64, :], tab[0:64, 0, t0:t0 + n], ALU.mult, [braw, bldl[8]], [btmp])
                            TT(rawf[0:64, :], bank(1)[0:64, 0:512], tab[0:64, 1, t0:t0 + n], ALU.mult, [pb[1], braw, bldl[8]], [braw])
                            TT(qr[k][0:64, 0:n], tmpa[0:64, :], rawf[0:64, :], ALU.add, [btmp, braw], [bq[k]])

                    def load_cq(tb, k):
                        t0, n = TBS[tb]
                        S.dma('sp', cqs[k][:, :, 0:n], CQN3[:, :, t0:t0 + n], writes=[bcqs[k]])

                    def finalize(h, tb, k, usedP, z):
                        t0, n = TBS[tb]
                        bO = 2 + k
                        if usedP:
                            TT(accD[k][:, :, 0:n], accD[k][:, :, 0:n], accP[k][:, :, 0:n], ALU.add, [baccD[k], baccP[k]], [baccD[k]])
                        TT(rinv[k][:, 0:n], accD[k][:, 0, 0:n], accD[k][:, 1, 0:n], ALU.add, [baccD[k]], [brinv[k]])

                        def later():
                            MM(bank(1)[:, 0:n], ones32[:, :], rinv[k][:, 0:n], True, True, [bones, brinv[k]], [pb[1]])
                            ACT(rinv[k][:, 0:n], bank(1)[:, 0:n], AF.Ln, [pb[1]], [brinv[k]])
                            ACT(rinv[k][:, 0:n], rinv[k][:, 0:n], AF.Exp, [brinv[k]], [brinv[k]], scale=-1.0)
                            TT(rinv[k][:, 0:n], bank(bO)[:, 0:n], rinv[k][:, 0:n], ALU.mult, [pb[bO], brinv[k]], [brinv[k]])
                            TT(ybt[k][:, 0:n], rinv[k][:, 0:n], szb[z][:, 0:n], ALU.mult, [brinv[k], bszb[z]], [bybt[k]])
                            S.dma('sp', Y[:, (8 + h) * NLOC + t0:(8 + h) * NLOC + t0 + n], ybt[k][:, 0:n], reads=[bybt[k]])
                        deferred.append((unit_of[0], later))

                    def flush_deferred(upto=None):
                        while deferred and (upto is None or deferred[0][0] <= upto):
                            deferred.pop(0)[1]()

                    deferred = []
                    unit_of = [0]
                    units = [(h, tb) for h in range(8) for tb in range(5) if not (tb == 4 and last)]
                    gp = [0]
                    load_w(0)
                    load_cq(units[0][1], 0)
                    prologue(0, units[0][1], 0, 0, 0)
                    prologue(0, units[0][1], 0, 1, 0)
                    ui = 0
                    for h in range(8):
                        if h + 1 < 8:
                            load_w(h + 1)
                        kv_proj(h)
                        steps = []
                        while ui < len(units) and units[ui][0] == h:
                            _, tb = units[ui]
                            k = ui % 2
                            t0, n = TBS[tb]
                            kts = list(range(NKT)) if tb < 4 else [32, 33]
                            pairs = [(kts[2 * j], kts[2 * j + 1]) for j in range(len(kts) // 2)]
                            nxt = units[ui + 1] if ui + 1 < len(units) else None
                            npair = len(pairs)
                            hook0 = min(2, npair - 1)
                            hook1 = min(4, npair - 1)
                            hookf = min(5, npair - 1)
                            for j, (ka, kb_) in enumerate(pairs):
                                g_ = gp[0]
                                gp[0] += 1

                                def A(k=k, n=n, j=j, ka=ka, kb_=kb_, g_=g_, nxt=nxt, hf=(j == hookf), hk0=(j == hook0), hk1=(j == hook1), ui=ui, npair=npair):
                                    pp = 2 + g_ % 2
                                    p3 = g_ % 3
                                    for t, kt in enumerate((ka, kb_)):
                                        sb = 2 * pp + t
                                        MM(bank(sb)[:, 0:n], KnT[:, kt * 128:(kt + 1) * 128], qn[k][:, 0:n], True, False, [bkn, bq[k]], [pb[sb]])
                                        MM(bank(sb)[:, 0:n], krF[:, kt * 128:(kt + 1) * 128], qr[k][:, 0:n], False, True, bldl + [bq[k]], [pb[sb]])
                                    ACT(PT[p3][:, :, 0:n], ps[pp][:, :].rearrange("p (t q) -> p t q", t=2)[:, :, 0:n], AF.Exp,
                                        [pb[2 * pp], pb[2 * pp + 1]], [bpt[p3]], scale=scale_b)
                                    if j == 0 and nxt is not None:
                                        load_cq(nxt[1], (ui + 1) % 2)
                                    if j == min(1, npair - 1):
                                        flush_deferred(ui - 2)
                                    if hf:
                                        flush_deferred()
                                    if hk0 and nxt is not None:
                                        prologue(nxt[0], nxt[1], (ui + 1) % 2, 0, (ui + 1) % 3)
                                    if hk1 and nxt is not None:
                                        prologue(nxt[0], nxt[1], (ui + 1) % 2, 1, (ui + 1) % 3)

                                def B(h=h, tb=tb, k=k, n=n, j=j, ka=ka, kb_=kb_, g_=g_, npair=npair, ui=ui):
                                    p3 = g_ % 3
                                    bO = 2 + k
                                    for t, kt in enumerate((ka, kb_)):
                                        MM(bank(bO)[:, 0:n], Vh[:, kt, :], PT[p3][:, t, 0:n], j == 0 and t == 0, j == npair - 1 and t == 1,
                                           [bvh, bpt[p3]], [pb[bO]])
                                    if j % 2 == 1:
                                        eng, acc, bacc, first = 'pool', accP[k], baccP[k], (j == 1)
                                    else:
                                        eng, acc, bacc, first = 'dve', accD[k], baccD[k], (j == 0)
                                    if first:
                                        CP(acc[:, :, 0:n], PT[p3][:, :, 0:n], [bpt[p3]], [bacc], eng=eng)
                                    else:
                                        TT(acc[:, :, 0:n], acc[:, :, 0:n], PT[p3][:, :, 0:n], ALU.add, [bacc, bpt[p3]], [bacc], eng=eng)
                                    if j == npair - 1:
                                        unit_of[0] = ui
                                        finalize(h, tb, k, npair > 1, ui % 3)
                                steps.append((A, B))
                            ui += 1
                        pipeline(steps)
                    flush_deferred()
                    S.barrier()

            with ExitStack() as ph:
                YT = T(ph, "YT", [128, 24, NLOC], BF16)
                by2 = [[Buf() for _ in range(5)] for _ in range(3)]
                for tb_ in (0, 1, 2, 3, 4):
                    t0_, n_ = TBS[tb_]
                    for br in range(3):
                        S.dma('sp', YT[:, br * 8:(br + 1) * 8, t0_:t0_ + n_],
                              Y[:, br * 8 * NLOC:(br + 1) * 8 * NLOC].rearrange("p (a t) -> p a t", a=8)[:, :, t0_:t0_ + n_], writes=[by2[br][tb_]])
                wp = [[T(ph, "wp%d_%d" % (br, i), [128, 8, 256], BF16) for i in range(2)] for br in range(3)]
                bwp = [[Buf(), Buf()] for _ in range(3)]
                sgt = [[T(ph, "sgt%d_%d" % (br, i), [128, NLOC], BF16) for i in range(2)] for br in range(3)]
                bsg = [[Buf(), Buf()] for _ in range(3)]
                acc = T(ph, "acc", [128, NLOC], F32)
                tmp = T(ph, "mtmp", [128, NLOC], F32)
                bacc = [Buf() for _ in range(5)]
                btmp = [Buf() for _ in range(5)]
                mj = [T(ph, "mj%d" % i, [128, NLOC], BF16) for i in range(2)]
                bmj = [Buf(), Buf()]
                rot = [0]

                def nb(n):
                    r = [(rot[0] + i) % 8 for i in range(n)]
                    rot[0] = (rot[0] + n) % 8
                    return r
                for cg in range(8):
                    ck = cg % 2
                    for br in range(3):
                        S.dma('pool', wp[br][ck][:], wsrc(w_p[br][l][:, cg * 256:(cg + 1) * 256]), writes=[bwp[br][ck]])
                    for jj in range(2):
                        j = cg * 2 + jj
                        jk = j % 2
                        for br in range(3):
                            S.dma('sp', sgt[br][jk][:], SG[:, (br * 16 + j) * NLOC:(br * 16 + j + 1) * NLOC], writes=[bsg[br][jk]])
                        for grp in GROUPS:
                            for br in range(3):
                                bks = nb(len(grp))
                                for kc in range(8):
                                    for tb, b in zip(grp, bks):
                                        t0, n = TBS[tb]
                                        MM(bank(b)[:, 0:n], wp[br][ck][:, kc, jj * 128:(jj + 1) * 128], YT[:, br * 8 + kc, t0:t0 + n],
                                           kc == 0, kc == 7, [bwp[br][ck], by2[br][tb]], [pb[b]])
                                for tb, b in zip(grp, bks):
                                    t0, n = TBS[tb]
                                    sg = sgt[br][jk][:, t0:t0 + n]
                                    if br == 0:
                                        TT(acc[:, t0:t0 + n], bank(b)[:, 0:n], sg, ALU.mult, [pb[b], bsg[br][jk]], [bacc[tb]])
                                    elif br == 1:
                                        TT(tmp[:, t0:t0 + n], bank(b)[:, 0:n], sg, ALU.mult, [pb[b], bsg[br][jk]], [btmp[tb]])
                                        TT(acc[:, t0:t0 + n], acc[:, t0:t0 + n], tmp[:, t0:t0 + n], ALU.add, [bacc[tb], btmp[tb]], [bacc[tb]])
                                    else:
                                        TT(tmp[:, t0:t0 + n], bank(b)[:, 0:n], sg, ALU.mult, [pb[b], bsg[br][jk]], [btmp[tb]])
                                        TT(mj[jk][:, t0:t0 + n], acc[:, t0:t0 + n], tmp[:, t0:t0 + n], ALU.add, [bacc[tb], btmp[tb]], [bmj[jk]])
                        S.dma('sp', MT[:, j * NLOC:(j + 1) * NLOC], mj[jk][:], reads=[bmj[jk]])
                S.barrier()

            with ExitStack() as ph:
                mT = T(ph, "mT", [128, 16, NLOC], BF16)
                bm4 = [Buf() for _ in range(6)]
                for q_ in range(6):
                    S.dma('sp', mT[:, :, q_ * 384:(q_ + 1) * 384], MT[:, :].rearrange("p (a t) -> p a t", a=16)[:, :, q_ * 384:(q_ + 1) * 384], writes=[bm4[q_]])
                wo = [T(ph, "wo%d" % i, [128, 16, 512], BF16) for i in range(4)]
                bwo = [Buf() for _ in range(4)]
                for n4 in range(4):
                    S.dma('pool', wo[n4][:], wsrc(w_out[l][:, n4 * 512:(n4 + 1) * 512]), writes=[bwo[n4]])
                gB = [T(ph, "gateB%d" % r, [128, D], F32) for r in range(2)]
                bg = Buf()
                for r in range(2):
                    S.dma('sp', gB[r][:], GATEB[:, (l * 2 + r) * D:(l * 2 + r + 1) * D], writes=[bg])
                if last:
                    fg = T(ph, "fg", [128, D], F32)
                    S.dma('sp', fg[:], fgb[:, :], writes=[bg])
                    junk = T(ph, "junk7", [128, D], BF16)
                    bj = Buf()
                xt = [T(ph, "oxt%d" % i, [128, D], F32) for i in range(2)]
                bx = [Buf(), Buf()]
                ot = [T(ph, "ot%d" % i, [128, D], F32) for i in range(2)]
                bo = [Buf(), Buf()]
                for i in range(16 if last else NTILE):
                    k = i % 2
                    r = 0 if i < 16 else 1
                    S.dma('sp', xt[k][:], xsrc(i), writes=[bx[k]])
                    for n4 in range(4):
                        b = k * 4 + n4
                        for kc in range(16):
                            MM(bank(b)[:, :], mT[:, kc, i * 128:(i + 1) * 128], wo[n4][:, kc, :], kc == 0, kc == 15, [bm4[i // 3], bwo[n4]], [pb[b]])
                    for hf in range(2):
                        P = ps[k * 2 + hf]
                        sl = slice(hf * 1024, (hf + 1) * 1024)
                        TT(ot[k][:, sl], P[:, :], gB[r][:, sl], ALU.mult, [pb[k * 4 + 2 * hf], pb[k * 4 + 2 * hf + 1], bg], [bo[k]])
                        TT(ot[k][:, sl], ot[k][:, sl], xt[k][:, sl], ALU.add, [bo[k], bx[k]], [bo[k]])
                    if not last:
                        S.dma('sp', X1[i * 128:(i + 1) * 128, :], ot[k][:], reads=[bo[k]])
                    else:
                        MS(sst[k][:, 0:1], 0.0, [bss[k]])
                        ACT(junk[:], ot[k][:], AF.Square, [bo[k]], [bj, bss[k]], accum=sst[k][:, 0:1])
                        rstd_from_ss(k, 1.0 / D)
                        STT(ot[k][:], ot[k][:], sst[k][:, 3:4], fg[:], ALU.mult, ALU.mult, [bo[k], bss[k], bg], [bo[k]])
                        S.dma('sp', out[i * 128:(i + 1) * 128, :], ot[k][:], reads=[bo[k]])
                S.barrier()

        phase_mods(0)
        for l in range(n_layers):
            layer(l)
        S.wait_cc()
        S.barrier()
        S.replay(block)
    return nc


def _rope_tables(s):
    pos = (s * NTOK + np.arange(NTOK)).astype(np.float32)
    row = np.floor(pos / 64.0).astype(np.float32)
    col = (pos - row * 64.0).astype(np.float32)

    def tabs(dim):
        half = dim // 4
        inv = (np.float32(THETA) ** (-(np.arange(half, dtype=np.float32)) / np.float32(half))).astype(np.float32)
        cos = np.zeros((128, NTOK), np.float32)
        sin = np.zeros((128, NTOK), np.float32)
        for d in range(dim):
            axis = row if d < dim // 2 else col
            dd = d % (dim // 2)
            ang = (axis * inv[dd % half]).astype(np.float32)
            cos[d] = np.cos(ang)
            sin[d] = np.sin(ang) * (-1.0 if dd < half else 1.0)
        return cos, sin
    c128, s128 = tabs(128)
    c64, s64 = tabs(64)
    return np.ascontiguousarray(np.concatenate([c128, s128, c64, s64], axis=1))


def _perm(dim):
    half = dim // 4
    R = np.zeros((dim, dim), np.float32)
    for m in range(dim):
        dd = m % (dim // 2)
        partner = m + half if dd < half else m - half
        R[partner, m] = 1.0
    return R


def _consts():
    c = np.zeros((128, 580), np.float32)
    c[:, 0:128] = np.eye(128, dtype=np.float32)
    c[:, 128:256] = _perm(128)
    c[0:64, 256:320] = _perm(64)
    c[0, 320:448] = 1.0
    c[1, 448:576] = 1.0
    c[0, 576] = 1.0
    c[1, 577] = 1.0
    return c


def _masks(s):
    j = np.arange(128)[:, None]
    i = np.arange(128)[None, :]
    prev = (j >= i).astype(np.float32)
    nxt = (j <= i).astype(np.float32)
    m = np.zeros((128, 4, 4, 128), np.float32)
    m[:, 0] = prev[:, None, :]
    m[:, 1] = nxt[:, None, :]
    m[:, 2] = 0.0 if s == 0 else prev[:, None, :]
    m[:, 3] = 0.0 if s == 1 else nxt[:, None, :]
    return np.ascontiguousarray(m.reshape(128, 4 * 512))


def make_in_maps(x, c, ctx, c_ctx, ada_w, ada_b, norm_g, w_in, sink_a, mla_gq, mla_gkv, w_uq, w_ukv,
                 sgu_ln_g, sgu_ln_b, sgu_w, sgu_b, w_pa, w_pb, w_pc, w_out, final_g):
    f = lambda a: np.ascontiguousarray(np.asarray(a, dtype=np.float32))
    x, c, ctx, c_ctx = f(x), f(c), f(ctx), f(c_ctx)
    bc = lambda v, n: np.ascontiguousarray(np.broadcast_to(np.asarray(v, np.float32)[:, None, :], (2, 128, n)))
    shared = {
        "ada_w": f(ada_w),
        "ada_b": np.ascontiguousarray(np.broadcast_to(f(ada_b)[:, None, :], (2, 2, 3 * D))),
        "normg": np.ascontiguousarray(f(norm_g).reshape(2, 16, 128).transpose(0, 2, 1)),
        "w_in": f(w_in),
        "sinkb": np.ascontiguousarray(np.broadcast_to(f(sink_a)[:, None, :, None], (2, 128, 8, 128)).reshape(2, 128, 1024)),
        "gqb": bc(mla_gq, 512), "gkvb": bc(mla_gkv, 512),
        "w_uq": f(w_uq), "w_ukv": f(w_ukv),
        "lngb": bc(sgu_ln_g, 1024), "lnbb": bc(sgu_ln_b, 1024),
        "wsT": np.ascontiguousarray(f(sgu_w).transpose(0, 3, 1, 2).reshape(2, 128, 1024)),
        "bsb": np.ascontiguousarray(np.broadcast_to(f(sgu_b).reshape(2, 1, 1024), (2, 128, 1024))),
        "w_pa": f(w_pa), "w_pb": f(w_pb), "w_pc": f(w_pc), "w_out": f(w_out),
        "fgb": np.ascontiguousarray(np.broadcast_to(f(final_g)[None, :], (128, D))),
        "cst": _consts(),
    }
    maps = []
    for core in range(8):
        b, s = core // 2, core % 2
        cv = np.zeros((128, 32), np.float32)
        cv[:, 0::2] = c[b].reshape(16, 128).T
        cv[:, 1::2] = c_ctx.reshape(16, 128).T
        m = dict(shared)
        m["x"] = np.ascontiguousarray(x[b, s * NTOK:(s + 1) * NTOK])
        m["ctx"] = np.ascontiguousarray(ctx[b])
        m["cvec"] = cv
        m["rope"] = _rope_tables(s)
        m["masks"] = _masks(s)
        maps.append(m)
    return maps


def kernel(**inputs):
    nc = build()
    maps = make_in_maps(**inputs)
    res = run_bass_kernel_spmd(nc, maps, core_ids=list(range(8)))
    outp = np.zeros((4, 2 * NTOK, D), np.float32)
    for core in range(8):
        b, s = core // 2, core % 2
        outp[b, s * NTOK:(s + 1) * NTOK] = np.asarray(res.results[core]["out"], dtype=np.float32)
    return outp
```

```python
import numpy as np
from contextlib import ExitStack
import concourse.bass as bass
import concourse.mybir as mybir
from concourse.bass_utils import run_bass_kernel_spmd

F32 = mybir.dt.float32
BF16 = mybir.dt.bfloat16
AF = mybir.ActivationFunctionType
ALU = mybir.AluOpType
AX = mybir.AxisListType

ENGS = ["pe", "act", "dve", "pool", "sp"]
SAME_ENGINE_SYNC = {"pe": False, "act": True, "dve": True, "pool": True, "sp": False}

D = 2048
NTOK = 2048
NCTX = 256
NLOC = NTOK + NCTX
NTILE = NLOC // 128
TBS = [(0, 512), (512, 512), (1024, 512), (1536, 512), (2048, 256)]
GROUPS = [[0, 1, 2], [3, 4]]
IN_COLS = 13888
EPS = 1e-6
THETA = 10000.0


class Buf:
    __slots__ = ("name", "w", "r")

    def __init__(self, name=""):
        self.name = name
        self.w = None
        self.r = []


class Sched:
    def __init__(self, nc, stack, n_dma_sems=56, n_cc=8):
        self.nc = nc
        self.streams = {e: [] for e in ENGS}
        self.esem = {e: stack.enter_context(nc.semaphore("es_" + e)) for e in ENGS}
        self.dsem = [stack.enter_context(nc.semaphore("ds_%d" % i)) for i in range(n_dma_sems)]
        self.csem = [stack.enter_context(nc.semaphore("cs_%d" % i)) for i in range(n_cc)]
        self.ncc = 0
        self.duse = [0] * n_dma_sems
        self.dnext = 0
        self.dnext_sw = 0
        self.n_hw = n_dma_sems - 16
        self.seen = {e: {} for e in ENGS}
        self.targets = {e: set() for e in ENGS}
        self.cnt = {e: 0 for e in ENGS}

    def _need(self, eng, tok):
        if tok[0] == 'e' and tok[1] == eng and not SAME_ENGINE_SYNC[eng]:
            return False
        return self.seen[eng].get(tok[0:2], 0) < tok[2]

    def _waits(self, eng, reads, writes, extra=()):
        deps = {}

        def add(tok):
            if tok is None:
                return
            k = tok[0:2]
            if deps.get(k, 0) < tok[2]:
                deps[k] = tok[2]
        for b in reads:
            add(b.w)
        for b in writes:
            add(b.w)
            for t in b.r:
                add(t)
        for t in extra:
            add(t)
        out = []
        for k, v in deps.items():
            tok = (k[0], k[1], v)
            if self._need(eng, tok):
                out.append(tok)
                self.seen[eng][k] = v
                if k[0] == 'e':
                    self.targets[k[1]].add(v)
        return out

    def _mark(self, tok, reads, writes):
        k = tok[0:2]
        for b in writes:
            b.w = tok
            b.r = []
        for b in reads:
            b.r = [t for t in b.r if t[0:2] != k] + [tok]

    def op(self, eng, fn, reads=(), writes=()):
        waits = self._waits(eng, reads, writes)
        self.cnt[eng] += 1
        idx = self.cnt[eng]
        tok = ('e', eng, idx)
        self.streams[eng].append([waits, fn, 'op', idx])
        self._mark(tok, reads, writes)
        return tok

    def dma(self, eng, out_ap, in_ap, reads=(), writes=()):
        if eng == 'pool':
            i = self.n_hw + self.dnext_sw
            self.dnext_sw = (self.dnext_sw + 1) % (len(self.dsem) - self.n_hw)
        else:
            i = self.dnext
            self.dnext = (self.dnext + 1) % self.n_hw
        extra = []
        if self.duse[i] > 0:
            extra.append(('d', i, 16 * self.duse[i]))
        waits = self._waits(eng, reads, writes, extra)
        self.duse[i] += 1
        tok = ('d', i, 16 * self.duse[i])
        self.streams[eng].append([waits, (out_ap, in_ap), 'dma', i])
        self._mark(tok, reads, writes)
        return tok

    def cc(self, fn, reads=(), writes=()):
        i = self.ncc
        self.ncc += 1
        waits = self._waits('pool', reads, writes)
        tok = ('c', i, 1)
        self.streams['pool'].append([waits, fn, 'cc', i])
        self._mark(tok, reads, writes)
        return tok

    def barrier(self):
        cnt = self.cnt
        extra = [('e', e, cnt[e]) for e in ENGS if e != 'sp' and cnt[e] > 0]
        extra += [('d', i, 16 * u) for i, u in enumerate(self.duse) if u > 0]
        waits = self._waits('sp', (), (), extra)
        cnt['sp'] += 1
        idx = cnt['sp']
        self.streams['sp'].append([waits, None, 'inc', idx])
        tok = ('e', 'sp', idx)
        for e in ENGS:
            if e == 'sp':
                continue
            w = self._waits(e, (), (), [tok])
            self.streams[e].append([w, None, 'nop', None])
            for e2 in ENGS:
                self.seen[e][('e', e2)] = max(self.seen[e].get(('e', e2), 0), cnt[e2])
            for i, u in enumerate(self.duse):
                self.seen[e][('d', i)] = 16 * u

    def wait_cc(self):
        extra = [('c', i, 1) for i in range(self.ncc)]
        w = self._waits('sp', (), (), extra)
        self.streams['sp'].append([w, None, 'nop', None])

    def replay(self, block):
        vals = {}
        for e in ENGS:
            tg = sorted(self.targets[e])
            vals[e] = {t: n + 1 for n, t in enumerate(tg)}

        def emit_stream(e, eng):
            for waits, fn, kind, info in self.streams[e]:
                for tok in waits:
                    if tok[0] == 'e':
                        eng.wait_ge(self.esem[tok[1]], vals[tok[1]][tok[2]])
                    elif tok[0] == 'd':
                        eng.wait_ge(self.dsem[tok[1]], tok[2])
                    else:
                        eng.wait_ge(self.csem[tok[1]], tok[2])
                if kind == 'op':
                    ins = fn(eng)
                    if info in vals[e]:
                        ins.then_inc(self.esem[e], 1)
                elif kind == 'dma':
                    o, i_ = fn
                    eng.dma_start(out=o, in_=i_).then_inc(self.dsem[info], 16)
                elif kind == 'cc':
                    fn(eng).then_inc(self.csem[info], 1)
                elif kind == 'inc':
                    if info in vals[e]:
                        eng.sem_inc(self.esem[e], 1)

        @block.tensor
        def _(eng):
            emit_stream('pe', eng)

        @block.scalar
        def _(eng):
            emit_stream('act', eng)

        @block.vector
        def _(eng):
            emit_stream('dve', eng)

        @block.gpsimd
        def _(eng):
            emit_stream('pool', eng)

        @block.sync
        def _(eng):
            emit_stream('sp', eng)


def build(n_layers=2, dbg=()):
    nc = bass.Bass("TRN2", target_bir_lowering=False)

    def din(name, shape, dt=F32):
        return nc.dram_tensor(name, shape, dt, kind="ExternalInput").ap()

    def dint(name, shape, dt):
        return nc.dram_tensor(name, shape, dt, kind="Internal").ap()

    x_in = din("x", [NTOK, D])
    ctx_in = din("ctx", [NCTX, D])
    cvec = din("cvec", [128, 32])
    ada_w = din("ada_w", [2, D, 3 * D])
    ada_b = din("ada_b", [2, 2, 3 * D])
    normg = din("normg", [2, 128, 16])
    w_in = din("w_in", [2, D, IN_COLS])
    sinkb = din("sinkb", [2, 128, 8 * 128])
    gqb = din("gqb", [2, 128, 512])
    gkvb = din("gkvb", [2, 128, 512])
    w_uq = din("w_uq", [2, 512, 1536])
    w_ukv = din("w_ukv", [2, 512, 2048])
    lngb = din("lngb", [2, 128, 1024])
    lnbb = din("lnbb", [2, 128, 1024])
    wsT = din("wsT", [2, 128, 1024])
    bsb = din("bsb", [2, 128, 1024])
    w_p = [din("w_pa", [2, 1024, D]), din("w_pb", [2, 1024, D]), din("w_pc", [2, 1024, D])]
    w_out = din("w_out", [2, D, D])
    fgb = din("fgb", [128, D])
    rope = din("rope", [128, 4 * NTOK])
    cst = din("cst", [128, 580])
    masks_in = din("masks", [128, 4 * 512])
    out = nc.dram_tensor("out", [NTOK, D], F32, kind="ExternalOutput").ap()

    X1 = dint("X1", [NLOC, D], F32)
    SG = dint("SG", [128, 48 * NLOC], BF16)
    SZB = dint("SZB", [128, 8 * NLOC], BF16)
    CQN = dint("CQN", [128, 4 * NLOC], BF16)
    CKVC = dint("CKVC", [128, 4 * NCTX], BF16)
    KRC = dint("KRC", [64, NCTX], BF16)
    SND = [dint("SND0", [128, 4096], BF16), dint("SND1", [128, 4096], BF16), dint("SND2", [128, 3072], BF16)]
    GTH = [[dint("GTH%d_%d" % (l, i), [256, 4096 if i < 2 else 3072], BF16) for i in range(3)] for l in range(2)]
    KA = dint("KA", [128, 2 * NLOC], BF16)
    VA = dint("VA", [128, NTILE * 256], BF16)
    Y = dint("Y", [128, 24 * NLOC], BF16)
    MT = dint("MT", [128, 16 * NLOC], BF16)
    GATEB = dint("GATEB", [128, 2 * 2 * D], F32)
    dbg_out = {}
    for name, shape in dbg:
        dbg_out[name] = nc.dram_tensor("dbg_" + name, shape, F32, kind="ExternalOutput").ap()

    with ExitStack() as st:
        S = Sched(nc, st)

        uid = [0]

        def T(stack, name, shape, dt):
            uid[0] += 1
            return stack.enter_context(nc.sbuf_tensor("%s_u%d" % (name, uid[0]), shape, dt))

        cs = T(st, "cs", [128, 580], F32)
        ident = cs[:, 0:128]
        R128 = cs[:, 128:256]
        R64 = cs[0:64, 256:320]
        sel = cs[0:2, 320:576]
        I2 = cs[0:2, 576:578]
        ones = T(st, "ones", [128, 128], BF16)
        sact = T(st, "sact", [128, 32], BF16)
        cvt = T(st, "cvt", [128, 32], F32)
        G1 = [T(st, "G1_%d" % l, [128, 32], F32) for l in range(2)]
        S1 = [T(st, "S1_%d" % l, [128, 32], F32) for l in range(2)]
        sst = [T(st, "sst%d" % i, [128, 16], F32) for i in range(4)]
        ones32 = T(st, "ones32", [128, 128], F32)
        ps = [st.enter_context(nc.psum_tensor("ps%d" % i, [128, 1024], F32)) for i in range(4)]
        block = st.enter_context(nc.Block())

        def bank(b):
            return ps[b // 2][:, (b % 2) * 512:(b % 2) * 512 + 512]
        pb = [Buf("bank%d" % i) for i in range(8)]
        bcs, bones, bsact = Buf(), Buf(), Buf()
        bmods = [Buf(), Buf()]
        bss = [Buf(), Buf(), Buf(), Buf()]

        def pipeline(steps, depth=1):
            n = len(steps)
            for i in range(min(depth, n)):
                steps[i][0]()
            for i in range(n):
                if i + depth < n:
                    steps[i + depth][0]()
                steps[i][1]()

        def MM(o, lhsT, rhs, start, stop, r, w):
            S.op('pe', lambda e: e.matmul(o, lhsT, rhs, start=start, stop=stop), r, w)

        def TR(o, i_, r, w):
            S.op('pe', lambda e: e.transpose(out=o, in_=i_, identity=ident), list(r) + [bcs], w)

        def ACT(o, i_, func, r, w, scale=None, bias=None, accum=None):
            kw = {}
            if scale is not None:
                kw['scale'] = scale
            if bias is not None:
                kw['bias'] = bias
            if accum is not None:
                kw['accum_out'] = accum
            S.op('act', lambda e: e.activation(out=o, in_=i_, func=func, **kw), r, w)

        def TT(o, a, b, op, r, w, eng='dve'):
            S.op(eng, lambda e: e.tensor_tensor(out=o, in0=a, in1=b, op=op), r, w)

        def TS(o, a, s1, s2, op0, op1, r, w, eng='dve'):
            if s2 is None:
                S.op(eng, lambda e: e.tensor_scalar(out=o, in0=a, scalar1=s1, scalar2=None, op0=op0), r, w)
            else:
                S.op(eng, lambda e: e.tensor_scalar(out=o, in0=a, scalar1=s1, scalar2=s2, op0=op0, op1=op1), r, w)

        def STT(o, a, s, b, op0, op1, r, w, eng='dve'):
            S.op(eng, lambda e: e.scalar_tensor_tensor(out=o, in0=a, scalar=s, in1=b, op0=op0, op1=op1), r, w)

        def CP(o, i_, r, w, eng='dve'):
            if eng == 'act':
                S.op('act', lambda e: e.activation(out=o, in_=i_, func=AF.Copy), r, w)
            else:
                S.op(eng, lambda e: e.tensor_copy(out=o, in_=i_), r, w)

        def MS(o, v, w, eng='dve'):
            S.op(eng, lambda e: e.memset(o, v), (), w)

        def RCP(o, i_, r, w):
            S.op('dve', lambda e: e.reciprocal(out=o, in_=i_), r, w)

        def rstd_from_ss(k, n_inv, r_extra=()):
            t = sst[k]
            TS(t[:, 1:2], t[:, 0:1], n_inv, EPS, ALU.mult, ALU.add, [bss[k]], [bss[k]])
            ACT(t[:, 2:3], t[:, 1:2], AF.Sqrt, [bss[k]], [bss[k]])
            RCP(t[:, 3:4], t[:, 2:3], [bss[k]], [bss[k]])

        def wsrc(ap2d):
            return ap2d.rearrange("(k p) n -> p k n", p=128)

        S.dma('sp', cs[:], cst[:, :], writes=[bcs])
        S.dma('sp', cvt[:], cvec[:, :], writes=[bsact])
        MS(ones[:], 1.0, [bones])
        MS(ones32[:], 1.0, [bones])
        ACT(sact[:], cvt[:], AF.Silu, [bsact], [bsact])

        def mods_setup(l, ph):
            st_ = {}
            st_['wb'] = [T(ph, "mw%d" % i, [128, 16, 512], BF16) for i in range(2)]
            st_['bw'] = [Buf(), Buf()]
            st_['mrow'] = T(ph, "mrow", [2, 3 * D], F32)
            st_['brow'] = T(ph, "brow", [2, 3 * D], F32)
            st_['ngt'] = T(ph, "ngt", [128, 16], F32)
            st_['mcol'] = T(ph, "mcol", [128, 96], F32)
            st_['gt'] = T(ph, "gt", [128, D], F32)
            for nm in ('bmrow', 'bbrow', 'bng', 'bmcol', 'bgt'):
                st_[nm] = Buf()
            st_['l'] = l
            S.dma('sp', st_['brow'][:], ada_b[l], writes=[st_['bbrow']])
            S.dma('sp', st_['ngt'][:], normg[l], writes=[st_['bng']])
            return st_

        def mods_load(st_, nt):
            l = st_['l']
            S.dma('pool', st_['wb'][nt % 2][:], wsrc(ada_w[l][:, nt * 512:(nt + 1) * 512]), writes=[st_['bw'][nt % 2]])

        def mods_tile(st_, nt):
            wb, bw, mrow, brow = st_['wb'], st_['bw'], st_['mrow'], st_['brow']
            for kc in range(16):
                MM(bank(7)[0:2, :], sact[:, 2 * kc:2 * kc + 2], wb[nt % 2][:, kc, :], kc == 0, kc == 15,
                   [bsact, bw[nt % 2]], [pb[7]])
            TT(mrow[0:2, nt * 512:(nt + 1) * 512], bank(7)[0:2, :], brow[0:2, nt * 512:(nt + 1) * 512], ALU.add,
               [pb[7], st_['bbrow']], [st_['bmrow']])

        def mods_finish(st_):
            l = st_['l']
            mrow, ngt, mcol, gt = st_['mrow'], st_['ngt'], st_['mcol'], st_['gt']
            bmrow, bng, bmcol, bgt = st_['bmrow'], st_['bng'], st_['bmcol'], st_['bgt']
            for f in range(48):
                MM(bank(6)[:, 2 * f:2 * f + 2], mrow[0:2, f * 128:(f + 1) * 128], I2, True, True, [bmrow, bcs], [pb[6]])
            CP(mcol[:], bank(6)[:, 0:96], [pb[6]], [bmcol])
            mc3 = mcol[:].rearrange("p (f r) -> p f r", r=2)
            for r in range(2):
                STT(G1[l][:, r * 16:(r + 1) * 16], mc3[:, 16:32, r], 1.0, ngt[:], ALU.add, ALU.mult, [bmcol, bng], [bmods[l]])
                CP(S1[l][:, r * 16:(r + 1) * 16], mc3[:, 0:16, r], [bmcol], [bmods[l]])
            for r in range(2):
                for n in range(4):
                    MM(bank(5)[:, :], sel[:, r * 128:(r + 1) * 128], mrow[0:2, 2 * D + n * 512:2 * D + (n + 1) * 512], True, True,
                       [bmrow, bcs], [pb[5]])
                    CP(gt[:, n * 512:(n + 1) * 512], bank(5)[:, :], [pb[5]], [bgt], eng='act')
                S.dma('sp', GATEB[:, (l * 2 + r) * D:(l * 2 + r + 1) * D], gt[:], reads=[bgt])

        def phase_mods(l):
            with ExitStack() as ph:
                st_ = mods_setup(l, ph)
                for nt in range(12):
                    mods_load(st_, nt)
                    mods_tile(st_, nt)
                mods_finish(st_)
                S.barrier()

        rope_pend = []

        def rope_flush():
            cur = rope_pend[:]
            del rope_pend[:]
            for f in cur:
                f()

        def gemm_fm(wt, bw, c0, M, nk, rhs_of, rhs_bufs, consumer, tbs=(0, 1, 2, 3, 4)):
            for grp in GROUPS:
                g = [tb for tb in grp if tb in tbs]
                if not g:
                    continue
                for kc in range(nk):
                    for tb in g:
                        t0, n = TBS[tb]
                        MM(bank(tb)[0:M, 0:n], wt[:, kc, c0:c0 + M], rhs_of(kc, t0, n), kc == 0, kc == nk - 1,
                           [bw] + rhs_bufs(tb), [pb[tb]])
                rope_flush()
                for tb in g:
                    t0, n = TBS[tb]
                    consumer(tb, t0, n, bank(tb)[0:M, 0:n], pb[tb])

        def layer(l):
            last = (l == n_layers - 1)
            with ExitStack() as lay:
                hT = T(lay, "hT", [128, 16, NLOC], BF16)
                bhT = [Buf("hT%d" % i) for i in range(2 * NTILE)]
                bgth = [Buf(), Buf(), Buf()]

                def h_rhs(kc, t0, n):
                    return hT[:, kc, t0:t0 + n]

                def h_bufs(tb):
                    t0, n = TBS[tb]
                    return bhT[2 * (t0 // 128):2 * ((t0 + n) // 128)]

                def xsrc(i):
                    if l == 0:
                        return x_in[i * 128:(i + 1) * 128, :] if i < 16 else ctx_in[(i - 16) * 128:(i - 15) * 128, :]
                    return X1[i * 128:(i + 1) * 128, :]

                with ExitStack() as ph:
                    xt = [T(ph, "xt%d" % i, [128, D], F32) for i in range(4)]
                    junk = T(ph, "junk", [128, D], BF16)
                    bx = [Buf(), Buf(), Buf(), Buf()]
                    bj = Buf()

                    def p1A(i):
                        k = i % 4
                        S.dma('sp', xt[k][:], xsrc(i), writes=[bx[k]])
                        MS(sst[k][:, 0:1], 0.0, [bss[k]])
                        ACT(junk[:], xt[k][:], AF.Square, [bx[k]], [bj, bss[k]], accum=sst[k][:, 0:1])
                        rstd_from_ss(k, 1.0 / D)
                        TS(xt[k][:], xt[k][:], sst[k][:, 3:4], None, ALU.mult, None, [bss[k], bx[k]], [bx[k]])

                    def p1B(i):
                        k = i % 4
                        r = 0 if i < 16 else 1
                        for q in range(4):
                            b = 4 + q
                            for j in range(4):
                                kc = q * 4 + j
                                TR(bank(b)[:, j * 128:(j + 1) * 128], xt[k][:, kc * 128:(kc + 1) * 128], [bx[k]], [pb[b]])
                            for j in range(4):
                                kc = q * 4 + j
                                o = hT[:, kc, i * 128:(i + 1) * 128]
                                src = bank(b)[:, j * 128:(j + 1) * 128]
                                gc = G1[l][:, r * 16 + kc:r * 16 + kc + 1]
                                sc = S1[l][:, r * 16 + kc:r * 16 + kc + 1]
                                if j % 2 == 0:
                                    ACT(o, src, AF.Identity, [pb[b], bmods[l]], [bhT[2 * i]], scale=gc, bias=sc)
                                else:
                                    TS(o, src, gc, sc, ALU.mult, ALU.add, [pb[b], bmods[l]], [bhT[2 * i + 1]])
                    pipeline([(lambda i=i: p1A(i), lambda i=i: p1B(i)) for i in range(NTILE)], depth=2)
                    S.barrier()
                if 'hT' in dbg_out:
                    with ExitStack() as ph:
                        tmp = T(ph, "dbgt", [128, NLOC], F32)
                        bt = Buf()
                        for kc in range(16):
                            CP(tmp[:], hT[:, kc, :], bhT, [bt])
                            S.dma('sp', dbg_out['hT'][l * 16 + kc], tmp[:], reads=[bt])
                        S.barrier()

                def fm_to_dram(jobs, side_l=None):
                    with ExitStack() as ph:
                        wb = [T(ph, "fw%d" % i, [128, 16, 512], BF16) for i in range(2)]
                        bw = [Buf(), Buf()]
                        stg = [T(ph, "stg%d" % i, [128, NLOC], BF16) for i in range(2)]
                        bstg = [Buf(), Buf()]
                        mst = mods_setup(side_l, ph) if side_l is not None else None
                        tiles = []
                        for (c0, nblk, func, dst) in jobs:
                            for wi in range((nblk + 3) // 4):
                                tiles.append((c0, nblk, func, dst, wi))
                        for ti, (c0, nblk, func, dst, wi) in enumerate(tiles):
                            ncol = min(512, nblk * 128 - wi * 512)
                            k = ti % 2
                            S.dma('pool', wb[k][:, :, 0:ncol], wsrc(w_in[l][:, c0 + wi * 512:c0 + wi * 512 + ncol]), writes=[bw[k]])
                            if mst is not None and ti < 12:
                                mods_load(mst, ti)
                                if ti >= 1:
                                    mods_tile(mst, ti - 1)
                            if mst is not None and ti == 12:
                                mods_tile(mst, 11)
                                mods_finish(mst)
                            for j in range(ncol // 128):
                                blk = wi * 4 + j
                                sk = (ti * 4 + j) % 2

                                def cons(tb, t0, n, pap, pbuf, sk=sk, func=func):
                                    ACT(stg[sk][:, t0:t0 + n], pap, func, [pbuf], [bstg[sk]])
                                gemm_fm(wb[k], bw[k], j * 128, 128, 16, h_rhs, h_bufs, cons)
                                S.dma('sp', dst[:, blk * NLOC:(blk + 1) * NLOC], stg[sk][:], reads=[bstg[sk]])
                        S.barrier()

                fm_to_dram([(7744, 48, AF.Sigmoid, SG), (3648, 8, AF.Silu, SZB)], side_l=(l + 1 if l + 1 < n_layers else None))

                def rope_fm(M, pap, pbuf, rawf, braw, Rm, cos, sin, tmpa, btmp, o, bo, pbank, btab):
                    CP(rawf[0:M, :], pap, [pbuf], [braw], eng='act')

                    def stage2():
                        MM(bank(pbank)[0:M, 0:512], Rm, rawf[0:M, :], True, True, [braw, bcs], [pb[pbank]])
                        TT(tmpa[0:M, :], rawf[0:M, :], cos, ALU.mult, [braw, btab], [btmp])
                        TT(rawf[0:M, :], bank(pbank)[0:M, 0:512], sin, ALU.mult, [pb[pbank], braw, btab], [braw])
                        TT(o, tmpa[0:M, :], rawf[0:M, :], ALU.add, [btmp, braw], [bo])
                    rope_pend.append(stage2)

                with ExitStack() as ph:
                    wb = [T(ph, "lw%d" % i, [128, 16, 512], BF16) for i in range(2)]
                    bw = [Buf(), Buf()]
                    gB = [T(ph, "gB%d" % i, [128, 512], F32) for i in range(2)]
                    bgB = Buf()
                    tab = T(ph, "tab", [128, 4, NTOK], F32)
                    btab = Buf()
                    latT1 = T(ph, "latT", [128, 4, NLOC], BF16)
                    latT = [latT1, latT1]
                    blat1 = Buf()
                    blat = [blat1, blat1]
                    nrm = [T(ph, "nrm%d" % i, [128, 512], F32) for i in range(3)]
                    bnrm = [Buf(), Buf(), Buf()]
                    junk = T(ph, "junk2", [128, 512], BF16)
                    bj = Buf()
                    krT = T(ph, "krT", [128, NLOC], BF16)
                    bkr = Buf()
                    MS(krT[64:128, :], 0.0, [bkr])
                    akT = T(ph, "akT", [128, 2, NLOC], BF16)
                    bak = Buf()
                    avt = T(ph, "avt", [128, NTILE, 256], BF16)
                    bav = Buf()
                    rawf = [T(ph, "rawf%d" % i, [128, 512], F32) for i in range(3)]
                    braw = [Buf(), Buf(), Buf()]
                    tmpa = [T(ph, "tmpa%d" % i, [128, 512], F32) for i in range(3)]
                    btmp = [Buf(), Buf(), Buf()]
                    S.dma('sp', gB[0][:], gqb[l], writes=[bgB])
                    S.dma('sp', gB[1][:], gkvb[l], writes=[bgB])
                    S.dma('sp', tab[:].rearrange("p a t -> p (a t)"), rope[:, :], writes=[btab])
                    bsnd = [Buf(), Buf(), Buf()]
                    for which in range(2):
                        c0 = 2560 + which * 512
                        S.dma('pool', wb[which][:], wsrc(w_in[l][:, c0:c0 + 512]), writes=[bw[which]])
                        def p2A(i, which=which):
                            k = i % 3
                            b = k
                            for kc in range(16):
                                MM(bank(b)[:, :], hT[:, kc, i * 128:(i + 1) * 128], wb[which][:, kc, :], kc == 0, kc == 15,
                                   [bhT[2 * i], bhT[2 * i + 1], bw[which]], [pb[b]])
                            MS(sst[k][:, 0:1], 0.0, [bss[k]])
                            ACT(junk[:], bank(b)[:, :], AF.Square, [pb[b]], [bj, bss[k]], accum=sst[k][:, 0:1])
                            rstd_from_ss(k, 1.0 / 512)
                            STT(nrm[k][:], bank(b)[:, :], sst[k][:, 3:4], gB[which][:], ALU.mult, ALU.mult,
                                [pb[b], bss[k], bgB], [bnrm[k]])

                        def p2B(i, which=which):
                            k = i % 3
                            tbk = 3 + i % 2
                            for c in range(4):
                                TR(bank(tbk)[:, c * 128:(c + 1) * 128], nrm[k][:, c * 128:(c + 1) * 128], [bnrm[k]], [pb[tbk]])
                            CP(latT[which][:, :, i * 128:(i + 1) * 128], bank(tbk)[:, :].rearrange("p (c t) -> p c t", c=4),
                               [pb[tbk]], [blat[which]], eng='act')
                        pipeline([(lambda i=i: p2A(i), lambda i=i: p2B(i)) for i in range(NTILE)], depth=2)
                        if which == 0:
                            S.dma('sp', CQN[:, :].rearrange("p (c t) -> p c t", c=4), latT[0][:], reads=[blat[0]])
                    S.dma('sp', SND[0][:, :].rearrange("p (c t) -> p c t", c=2), latT[1][:, 0:2, 0:NTOK], reads=[blat[1]], writes=[bsnd[0]])
                    S.dma('sp', SND[1][:, :].rearrange("p (c t) -> p c t", c=2), latT[1][:, 2:4, 0:NTOK], reads=[blat[1]], writes=[bsnd[1]])
                    S.dma('sp', CKVC[:, :].rearrange("p (c t) -> p c t", c=4), latT[1][:, :, NTOK:NLOC], reads=[blat[1]])
                    S.dma('pool', wb[0][:, :, 0:64], wsrc(w_in[l][:, 3584:3648]), writes=[bw[0]])

                    def cons_kr(tb, t0, n, pap, pbuf):
                        if tb < 4:
                            k = tb % 3
                            rope_fm(64, pap, pbuf, rawf[k], braw[k], R64, tab[0:64, 2, t0:t0 + n], tab[0:64, 3, t0:t0 + n],
                                    tmpa[k], btmp[k], krT[0:64, t0:t0 + n], bkr, 5 + k, btab)
                        else:
                            CP(krT[0:64, t0:t0 + n], pap, [pbuf], [bkr], eng='act')
                    gemm_fm(wb[0], bw[0], 0, 64, 16, h_rhs, h_bufs, cons_kr)
                    rope_flush()
                    S.dma('sp', SND[2][:, 0:NTOK], krT[:, 0:NTOK], reads=[bkr], writes=[bsnd[2]])
                    S.dma('sp', KRC[:, :], krT[0:64, NTOK:NLOC], reads=[bkr])
                    S.dma('pool', wb[1][:], wsrc(w_in[l][:, 1024:1536]), writes=[bw[1]])
                    for h in range(2):
                        def cons_ak(tb, t0, n, pap, pbuf, h=h):
                            if tb < 4:
                                k = tb % 3
                                rope_fm(128, pap, pbuf, rawf[k], braw[k], R128, tab[:, 0, t0:t0 + n], tab[:, 1, t0:t0 + n],
                                        tmpa[k], btmp[k], akT[:, h, t0:t0 + n], bak, 5 + k, btab)
                            else:
                                CP(akT[:, h, t0:t0 + n], pap, [pbuf], [bak], eng='act')
                        gemm_fm(wb[1], bw[1], h * 128, 128, 16, h_rhs, h_bufs, cons_ak)
                    rope_flush()
                    for i in range(NTILE):
                        b = 5 + i % 2
                        for kc in range(16):
                            MM(bank(b)[:, 0:256], hT[:, kc, i * 128:(i + 1) * 128], wb[1][:, kc, 256:512], kc == 0, kc == 15,
                               [bhT[2 * i], bhT[2 * i + 1], bw[1]], [pb[b]])
                        CP(avt[:, i, :], bank(b)[:, 0:256], [pb[b]], [bav], eng='act' if i % 2 else 'dve')
                    S.dma('sp', KA[:, :].rearrange("p (h t) -> p h t", h=2), akT[:], reads=[bak])
                    S.dma('sp', VA[:, :].rearrange("p (i c) -> p i c", c=256), avt[:], reads=[bav])
                    for which, t0 in ((0, 0), (1, NTOK - 128)):
                        S.dma('sp', SND[2][:, 2048 + which * 256:2048 + (which + 1) * 256].rearrange("p (h t) -> p h t", h=2),
                              akT[:, :, t0:t0 + 128], reads=[bak], writes=[bsnd[2]])
                        S.dma('sp', SND[2][:, 2560 + which * 256:2560 + (which + 1) * 256], avt[:, t0 // 128, :],
                              reads=[bav], writes=[bsnd[2]])
                    for i in range(3):
                        def ccf(e, i=i):
                            return e.collective_compute("AllGather", ALU.bypass, replica_groups=[[0, 1], [2, 3], [4, 5], [6, 7]],
                                                        ins=[SND[i][:, :]], outs=[GTH[l][i][:, :]])
                        S.cc(ccf, reads=[bsnd[i]], writes=[bgth[i]])
                    S.barrier()

                with ExitStack() as ph:
                    wb = [T(ph, "sw%d" % i, [128, 16, 512], BF16) for i in range(2)]
                    bw = [Buf(), Buf()]
                    lnG = T(ph, "lnG", [128, 1024], F32)
                    lnB = T(ph, "lnB", [128, 1024], F32)
                    BS = T(ph, "BS", [128, 1024], F32)
                    wst = T(ph, "wst", [128, 8, 128], BF16)
                    bc3 = Buf()
                    MIX = T(ph, "MIX", [128, 8, NLOC], BF16)
                    bmix = [Buf() for _ in range(8)]
                    cvf = [T(ph, "cvf%d" % i, [128, 1024], F32) for i in range(2)]
                    bcvf = [Buf(), Buf()]
                    vn = [T(ph, "vn%d" % i, [128, 1024], BF16) for i in range(2)]
                    bvn = [Buf(), Buf()]
                    junk = T(ph, "junk3", [128, 1024], BF16)
                    bj = Buf()
                    szt = [T(ph, "szt%d" % i, [128, NLOC], BF16) for i in range(2)]
                    bsz = [Buf(), Buf()]
                    S.dma('sp', lnG[:], lngb[l], writes=[bc3])
                    S.dma('sp', lnB[:], lnbb[l], writes=[bc3])
                    S.dma('sp', BS[:], bsb[l], writes=[bc3])
                    S.dma('pool', wst[:].rearrange("p g q -> p (g q)"), wsT[l], writes=[bc3])
                    for hf in range(2):
                        S.dma('pool', wb[hf][:], wsrc(w_in[l][:, 5696 + hf * 512:5696 + (hf + 1) * 512]), writes=[bw[hf]])
                    def p3A(i):
                        k = i % 2
                        P = ps[k]
                        for hf in range(2):
                            for kc in range(16):
                                MM(P[:, hf * 512:(hf + 1) * 512], hT[:, kc, i * 128:(i + 1) * 128], wb[hf][:, kc, :], kc == 0, kc == 15,
                                   [bhT[2 * i], bhT[2 * i + 1], bw[hf]], [pb[2 * k + hf]])
                        pbs = [pb[2 * k], pb[2 * k + 1]]
                        t = sst[k]
                        MS(t[:, 0:1], 0.0, [bss[k]])
                        S.op('dve', lambda e, t=t, P=P: e.reduce_sum(out=t[:, 4:5], in_=P[:, :], axis=AX.X), pbs, [bss[k]])
                        ACT(junk[:], P[:, :], AF.Square, pbs, [bj, bss[k]], accum=t[:, 0:1])
                        TS(t[:, 5:6], t[:, 4:5], 1.0 / 1024, None, ALU.mult, None, [bss[k]], [bss[k]])
                        TT(t[:, 6:7], t[:, 5:6], t[:, 5:6], ALU.mult, [bss[k]], [bss[k]])
                        TS(t[:, 7:8], t[:, 0:1], 1.0 / 1024, None, ALU.mult, None, [bss[k]], [bss[k]])
                        TT(t[:, 7:8], t[:, 7:8], t[:, 6:7], ALU.subtract, [bss[k]], [bss[k]])
                        TS(t[:, 1:2], t[:, 7:8], EPS, None, ALU.add, None, [bss[k]], [bss[k]])
                        ACT(t[:, 2:3], t[:, 1:2], AF.Sqrt, [bss[k]], [bss[k]])
                        RCP(t[:, 3:4], t[:, 2:3], [bss[k]], [bss[k]])
                        STT(t[:, 8:9], t[:, 5:6], -1.0, t[:, 3:4], ALU.mult, ALU.mult, [bss[k]], [bss[k]])
                        ACT(cvf[k][:], P[:, :], AF.Identity, pbs + [bss[k]], [bcvf[k]], scale=t[:, 3:4], bias=t[:, 8:9])
                        TT(cvf[k][:], cvf[k][:], lnG[:], ALU.mult, [bcvf[k], bc3], [bcvf[k]])
                        TT(vn[k][:], cvf[k][:], lnB[:], ALU.add, [bcvf[k], bc3], [bvn[k]])

                    def p3B(i):
                        k = i % 2
                        Pm = ps[2 + k]
                        for g in range(8):
                            MM(Pm[:, g * 128:(g + 1) * 128], vn[k][:, g * 128:(g + 1) * 128], wst[:, g, :], True, True,
                               [bvn[k], bc3], [pb[4 + 2 * k + g // 4]])
                        TT(MIX[:, :, i * 128:(i + 1) * 128], Pm[:, :].rearrange("p (g q) -> p g q", g=8),
                           BS[:].rearrange("p (g q) -> p g q", g=8), ALU.add, [pb[4 + 2 * k], pb[5 + 2 * k], bc3], bmix)
                    pipeline([(lambda i=i: p3A(i), lambda i=i: p3B(i)) for i in range(NTILE)])
                    for wi in range(2):
                        S.dma('pool', wb[wi][:], wsrc(w_in[l][:, 6720 + wi * 512:6720 + (wi + 1) * 512]), writes=[bw[wi]])
                        for j in range(4):
                            g = wi * 4 + j
                            sk = g % 2

                            def cons_cz(tb, t0, n, pap, pbuf, sk=sk):
                                ACT(szt[sk][:, t0:t0 + n], pap, AF.Silu, [pbuf], [bsz[sk]])
                            gemm_fm(wb[wi], bw[wi], j * 128, 128, 16, h_rhs, h_bufs, cons_cz)
                            TT(MIX[:, g, :], MIX[:, g, :], szt[sk][:], ALU.mult, [bmix[g], bsz[sk]], [bmix[g]])
                    for wi in range(2):
                        S.dma('pool', wb[wi][:], wsrc(w_in[l][:, 4672 + wi * 512:4672 + (wi + 1) * 512]), writes=[bw[wi]])
                        for j in range(4):
                            g = wi * 4 + j

                            def cons_cu(tb, t0, n, pap, pbuf, g=g):
                                TT(MIX[:, g, t0:t0 + n], pap, MIX[:, g, t0:t0 + n], ALU.mult, [pbuf, bmix[g]], [bmix[g]])
                            gemm_fm(wb[wi], bw[wi], j * 128, 128, 16, h_rhs, h_bufs, cons_cu)
                            S.dma('sp', Y[:, (16 + g) * NLOC:(17 + g) * NLOC], MIX[:, g, :], reads=[bmix[g]])
                    S.barrier()

                with ExitStack() as ph:
                    NK = 2 * NTOK + NCTX
                    cqs = [T(ph, "cqs%d" % i, [128, 4, 512], BF16) for i in range(2)]
                    bcqs = [Buf(), Buf()]
                    CQN3 = CQN[:, :].rearrange("p (c t) -> p c t", c=4)
                    ckT = T(ph, "ckT", [128, 4, NK], BF16)
                    krF = T(ph, "krF", [128, NK], BF16)
                    tab = T(ph, "tab64", [64, 2, NTOK], F32)
                    bldl = [Buf() for _ in range(9)]
                    bld = bldl[0]
                    ii = 0
                    for gi in range(2):
                        for half in range(2):
                            S.dma('sp', ckT[:, 2 * gi:2 * gi + 2, half * NTOK:(half + 1) * NTOK],
                                  GTH[l][gi][half * 128:(half + 1) * 128, :].rearrange("p (c t) -> p c t", c=2), reads=[bgth[gi]], writes=[bldl[ii]])
                            ii += 1
                    S.dma('sp', ckT[:, :, 2 * NTOK:NK], CKVC[:, :].rearrange("p (c t) -> p c t", c=4), writes=[bldl[4]])
                    for half in range(2):
                        S.dma('sp', krF[0:64, half * NTOK:(half + 1) * NTOK], GTH[l][2][half * 128:half * 128 + 64, 0:NTOK], reads=[bgth[2]], writes=[bldl[5 + half]])
                    S.dma('sp', krF[0:64, 2 * NTOK:NK], KRC[:, :], writes=[bldl[7]])
                    S.dma('sp', tab[:], rope[0:64, 2 * NTOK:4 * NTOK].rearrange("p (a t) -> p a t", a=2), writes=[bldl[8]])
                    MS(krF[64:128, :], 0.0, [bldl[7]])
                    wq = [T(ph, "wq%d" % i, [128, 4, 192], BF16) for i in range(2)]
                    wkv = [T(ph, "wkv%d" % i, [128, 4, 256], BF16) for i in range(2)]
                    bwh = [Buf(), Buf()]
                    KnT = T(ph, "KnT", [128, NK], BF16)
                    bkn = Buf()
                    Vh = T(ph, "Vh", [128, NK // 128, 128], BF16)
                    bvh = Buf()
                    qn = [T(ph, "qn%d" % i, [128, 512], BF16) for i in range(2)]
                    qr = [T(ph, "qr%d" % i, [128, 512], BF16) for i in range(2)]
                    bq = [Buf(), Buf()]
                    for i_ in range(2):
                        MS(qr[i_][64:128, :], 0.0, [bq[i_]])
                    rawf = T(ph, "mrawf", [64, 512], F32)
                    braw = Buf()
                    tmpa = T(ph, "mtmpa", [64, 512], F32)
                    btmp = Buf()
                    PT = [T(ph, "PT%d" % i, [128, 2, 512], BF16) for i in range(3)]
                    bpt = [Buf(), Buf(), Buf()]
                    rinv = [T(ph, "rinv%d" % i, [128, 512], F32) for i in range(2)]
                    brinv = [Buf(), Buf()]
                    accD = [T(ph, "accD%d" % i, [128, 2, 512], F32) for i in range(2)]
                    accP = [T(ph, "accP%d" % i, [128, 2, 512], F32) for i in range(2)]
                    baccD = [Buf(), Buf()]
                    baccP = [Buf(), Buf()]
                    szb = [T(ph, "szb%d" % i, [128, 512], BF16) for i in range(3)]
                    bszb = [Buf(), Buf(), Buf()]
                    ybt = [T(ph, "ybt%d" % i, [128, 512], BF16) for i in range(2)]
                    bybt = [Buf(), Buf()]
                    scale_b = (128 + 64) ** -0.5
                    NKT = NK // 128

                    def load_w(h):
                        hk = h % 2
                        S.dma('pool', wq[hk][:], wsrc(w_uq[l][:, h * 192:(h + 1) * 192]), writes=[bwh[hk]])
                        S.dma('pool', wkv[hk][:], wsrc(w_ukv[l][:, h * 256:(h + 1) * 256]), writes=[bwh[hk]])

                    def kv_proj(h):
                        hk = h % 2
                        for kb in range((NK + 511) // 512):
                            k0 = kb * 512
                            n = min(512, NK - k0)
                            b = kb % 2
                            for kc in range(4):
                                MM(bank(b)[:, 0:n], wkv[hk][:, kc, 0:128], ckT[:, kc, k0:k0 + n], kc == 0, kc == 3, [bwh[hk]] + bldl, [pb[b]])
                            CP(KnT[:, k0:k0 + n], bank(b)[:, 0:n], [pb[b]], [bkn], eng='act')
                        for kt in range(NKT):
                            b = (kt // 4) % 2
                            j = kt % 4
                            for kc in range(4):
                                MM(bank(b)[:, j * 128:(j + 1) * 128], ckT[:, kc, kt * 128:(kt + 1) * 128], wkv[hk][:, kc, 128:256],
                                   kc == 0, kc == 3, [bwh[hk]] + bldl, [pb[b]])
                            if j == 3 or kt == NKT - 1:
                                k0 = kt - j
                                CP(Vh[:, k0:kt + 1, :], bank(b)[:, 0:(j + 1) * 128].rearrange("p (a d) -> p a d", d=128), [pb[b]], [bvh],
                                   eng='act' if (kt // 4) % 2 else 'dve')

                    def prologue(h, tb, k, part, z=0):
                        hk = h % 2
                        t0, n = TBS[tb]
                        if part == 0:
                            for kc in range(4):
                                MM(bank(0)[:, 0:n], wq[hk][:, kc, 0:128], cqs[k][:, kc, 0:n], kc == 0, kc == 3, [bwh[hk], bcqs[k]], [pb[0]])
                            CP(qn[k][:, 0:n], bank(0)[:, 0:n], [pb[0]], [bq[k]], eng='act')
                            for kc in range(4):
                                MM(bank(1)[0:64, 0:n], wq[hk][:, kc, 128:192], cqs[k][:, kc, 0:n], kc == 0, kc == 3, [bwh[hk], bcqs[k]], [pb[1]])
                            if tb < 4:
                                CP(rawf[0:64, :], bank(1)[0:64, 0:n], [pb[1]], [braw], eng='act')
                            else:
                                CP(qr[k][0:64, 0:n], bank(1)[0:64, 0:n], [pb[1]], [bq[k]], eng='dve')
                            S.dma('sp', szb[z][:, 0:n], SZB[:, h * NLOC + t0:h * NLOC + t0 + n], writes=[bszb[z]])
                        elif tb < 4:
                            MM(bank(1)[0:64, 0:512], R64, rawf[0:64, :], True, True, [braw, bcs], [pb[1]])
                            TT(tmpa[0:64, :], rawf[0:64, :], tab[0:64, 0, t0:t0 + n], ALU.mult, [braw, bldl[8]], [btmp])
                            TT(rawf[0:64, :], bank(1)[0:64, 0:512], tab[0:64, 1, t0:t0 + n], ALU.mult, [pb[1], braw, bldl[8]], [braw])
                            TT(qr[k][0:64, 0:n], tmpa[0:64, :], rawf[0:64, :], ALU.add, [btmp, braw], [bq[k]])

                    def load_cq(tb, k):
                        t0, n = TBS[tb]
                        S.dma('sp', cqs[k][:, :, 0:n], CQN3[:, :, t0:t0 + n], writes=[bcqs[k]])

                    def finalize(h, tb, k, usedP, z):
                        t0, n = TBS[tb]
                        bO = 2 + k
                        if usedP:
                            TT(accD[k][:, :, 0:n], accD[k][:, :, 0:n], accP[k][:, :, 0:n], ALU.add, [baccD[k], baccP[k]], [baccD[k]])
                        TT(rinv[k][:, 0:n], accD[k][:, 0, 0:n], accD[k][:, 1, 0:n], ALU.add, [baccD[k]], [brinv[k]])

                        def later():
                            MM(bank(1)[:, 0:n], ones32[:, :], rinv[k][:, 0:n], True, True, [bones, brinv[k]], [pb[1]])
                            ACT(rinv[k][:, 0:n], bank(1)[:, 0:n], AF.Ln, [pb[1]], [brinv[k]])
                            ACT(rinv[k][:, 0:n], rinv[k][:, 0:n], AF.Exp, [brinv[k]], [brinv[k]], scale=-1.0)
                            TT(rinv[k][:, 0:n], bank(bO)[:, 0:n], rinv[k][:, 0:n], ALU.mult, [pb[bO], brinv[k]], [brinv[k]])
                            TT(ybt[k][:, 0:n], rinv[k][:, 0:n], szb[z][:, 0:n], ALU.mult, [brinv[k], bszb[z]], [bybt[k]])
                            S.dma('sp', Y[:, (8 + h) * NLOC + t0:(8 + h) * NLOC + t0 + n], ybt[k][:, 0:n], reads=[bybt[k]])
                        deferred.append((unit_of[0], later))

                    def flush_deferred(upto=None):
                        while deferred and (upto is None or deferred[0][0] <= upto):
                            deferred.pop(0)[1]()

                    deferred = []
                    unit_of = [0]
                    units = [(h, tb) for h in range(8) for tb in range(5) if not (tb == 4 and last)]
                    gp = [0]
                    load_w(0)
                    load_cq(units[0][1], 0)
                    prologue(0, units[0][1], 0, 0, 0)
                    prologue(0, units[0][1], 0, 1, 0)
                    ui = 0
                    for h in range(8):
                        if h + 1 < 8:
                            load_w(h + 1)
                        kv_proj(h)
                        steps = []
                        while ui < len(units) and units[ui][0] == h:
                            _, tb = units[ui]
                            k = ui % 2
                            t0, n = TBS[tb]
                            kts = list(range(NKT)) if tb < 4 else [32, 33]
                            pairs = [(kts[2 * j], kts[2 * j + 1]) for j in range(len(kts) // 2)]
                            nxt = units[ui + 1] if ui + 1 < len(units) else None
                            npair = len(pairs)
                            hook0 = min(2, npair - 1)
                            hook1 = min(4, npair - 1)
                            hookf = min(5, npair - 1)
                            for j, (ka, kb_) in enumerate(pairs):
                                g_ = gp[0]
                                gp[0] += 1

                                def A(k=k, n=n, j=j, ka=ka, kb_=kb_, g_=g_, nxt=nxt, hf=(j == hookf), hk0=(j == hook0), hk1=(j == hook1), ui=ui, npair=npair):
                                    pp = 2 + g_ % 2
                                    p3 = g_ % 3
                                    for t, kt in enumerate((ka, kb_)):
                                        sb = 2 * pp + t
                                        MM(bank(sb)[:, 0:n], KnT[:, kt * 128:(kt + 1) * 128], qn[k][:, 0:n], True, False, [bkn, bq[k]], [pb[sb]])
                                        MM(bank(sb)[:, 0:n], krF[:, kt * 128:(kt + 1) * 128], qr[k][:, 0:n], False, True, bldl + [bq[k]], [pb[sb]])
                                    ACT(PT[p3][:, :, 0:n], ps[pp][:, :].rearrange("p (t q) -> p t q", t=2)[:, :, 0:n], AF.Exp,
                                        [pb[2 * pp], pb[2 * pp + 1]], [bpt[p3]], scale=scale_b)
                                    if j == 0 and nxt is not None:
                                        load_cq(nxt[1], (ui + 1) % 2)
                                    if j == min(1, npair - 1):
                                        flush_deferred(ui - 2)
                                    if hf:
                                        flush_deferred()
                                    if hk0 and nxt is not None:
                                        prologue(nxt[0], nxt[1], (ui + 1) % 2, 0, (ui + 1) % 3)
                                    if hk1 and nxt is not None:
                                        prologue(nxt[0], nxt[1], (ui + 1) % 2, 1, (ui + 1) % 3)

                                def B(h=h, tb=tb, k=k, n=n, j=j, ka=ka, kb_=kb_, g_=g_, npair=npair, ui=ui):
                                    p3 = g_ % 3
                                    bO = 2 + k
                                    for t, kt in enumerate((ka, kb_)):
                                        MM(bank(bO)[:, 0:n], Vh[:, kt, :], PT[p3][:, t, 0:n], j == 0 and t == 0, j == npair - 1 and t == 1,
                                           [bvh, bpt[p3]], [pb[bO]])
                                    if j % 2 == 1:
                                        eng, acc, bacc, first = 'pool', accP[k], baccP[k], (j == 1)
                                    else:
                                        eng, acc, bacc, first = 'dve', accD[k], baccD[k], (j == 0)
                                    if first:
                                        CP(acc[:, :, 0:n], PT[p3][:, :, 0:n], [bpt[p3]], [bacc], eng=eng)
                                    else:
                                        TT(acc[:, :, 0:n], acc[:, :, 0:n], PT[p3][:, :, 0:n], ALU.add, [bacc, bpt[p3]], [bacc], eng=eng)
                                    if j == npair - 1:
                                        unit_of[0] = ui
                                        finalize(h, tb, k, npair > 1, ui % 3)
                                steps.append((A, B))
                            ui += 1
                        pipeline(steps)
                    flush_deferred()
                    S.barrier()

                with ExitStack() as ph:
                    wb = [T(ph, "aw%d" % i, [128, 16, 512], BF16) for i in range(2)]
                    bw = [Buf(), Buf()]
                    tab = T(ph, "tab128", [128, 2, NTOK], F32)
                    mk = T(ph, "mk", [128, 4, 512], BF16)
                    skb = T(ph, "skb", [128, 1024], F32)
                    ESB = T(ph, "ESB", [128, 1024], F32)
                    bc5 = Buf()
                    S.dma('sp', tab[:], rope[:, 0:2 * NTOK].rearrange("p (a t) -> p a t", a=2), writes=[bc5])
                    S.dma('pool', mk[:].rearrange("p a t -> p (a t)"), masks_in[:, :], writes=[bc5])
                    S.dma('sp', skb[:], sinkb[l], writes=[bc5])
                    ACT(ESB[:], skb[:], AF.Exp, [bc5], [bc5])
                    KAT = T(ph, "KAT", [128, 20 * 128], BF16)
                    VAT = T(ph, "VAT", [128, 20, 128], BF16)
                    bkvl = [Buf() for _ in range(8)]
                    QT = T(ph, "QT", [128, 4, NLOC], BF16)
                    bqt = Buf()
                    SZ = T(ph, "SZ", [128, 4, NLOC], BF16)
                    bsz = Buf()
                    rawf = [T(ph, "arawf%d" % i, [128, 512], F32) for i in range(3)]
                    braw = [Buf(), Buf(), Buf()]
                    tmpa = [T(ph, "atmpa%d" % i, [128, 512], F32) for i in range(3)]
                    btmp = [Buf(), Buf(), Buf()]
                    PT = [T(ph, "aPT%d" % i, [128, 512], BF16) for i in range(3)]
                    bpt = [Buf(), Buf(), Buf()]
                    lt = [T(ph, "lt%d" % i, [128, 512], F32) for i in range(2)]
                    blt = [Buf(), Buf()]
                    of = [T(ph, "aof%d" % i, [128, 512], F32) for i in range(2)]
                    bof = [Buf(), Buf()]
                    scale_a = 128 ** -0.5
                    gi5 = [0]
                    G2 = GTH[l][2]
                    for g in range(2):
                        S.dma('sp', KAT[:, 128:128 + NTOK], KA[:, g * NLOC:g * NLOC + NTOK], writes=[bkvl[0]])
                        S.dma('sp', KAT[:, 18 * 128:20 * 128], KA[:, g * NLOC + NTOK:(g + 1) * NLOC], writes=[bkvl[1]])
                        S.dma('sp', KAT[:, 0:128], G2[0:128, 2048 + 256 + g * 128:2048 + 256 + (g + 1) * 128], reads=[bgth[2]], writes=[bkvl[2]])
                        S.dma('sp', KAT[:, 17 * 128:18 * 128], G2[128:256, 2048 + g * 128:2048 + (g + 1) * 128], reads=[bgth[2]], writes=[bkvl[3]])
                        va3 = VA[:, :].rearrange("p (i c) -> p i c", c=256)
                        S.dma('sp', VAT[:, 1:17, :], va3[:, 0:16, g * 128:(g + 1) * 128], writes=[bkvl[4]])
                        S.dma('sp', VAT[:, 18:20, :], va3[:, 16:18, g * 128:(g + 1) * 128], writes=[bkvl[5]])
                        S.dma('sp', VAT[:, 0, :], G2[0:128, 2560 + 256 + g * 128:2560 + 256 + (g + 1) * 128], reads=[bgth[2]], writes=[bkvl[6]])
                        S.dma('sp', VAT[:, 17, :], G2[128:256, 2560 + g * 128:2560 + (g + 1) * 128], reads=[bgth[2]], writes=[bkvl[7]])
                        S.dma('pool', wb[0][:], wsrc(w_in[l][:, g * 512:(g + 1) * 512]), writes=[bw[0]])
                        S.dma('pool', wb[1][:], wsrc(w_in[l][:, 1536 + g * 512:1536 + (g + 1) * 512]), writes=[bw[1]])
                        for hh in range(4):
                            def cons_q(tb, t0, n, pap, pbuf, hh=hh):
                                if tb < 4:
                                    k = tb % 3
                                    rope_fm(128, pap, pbuf, rawf[k], braw[k], R128, tab[:, 0, t0:t0 + n], tab[:, 1, t0:t0 + n],
                                            tmpa[k], btmp[k], QT[:, hh, t0:t0 + n], bqt, 5 + k, bc5)
                                else:
                                    CP(QT[:, hh, t0:t0 + n], pap, [pbuf], [bqt], eng='act')
                            gemm_fm(wb[0], bw[0], hh * 128, 128, 16, h_rhs, h_bufs, cons_q)
                        for hh in range(4):
                            def cons_z(tb, t0, n, pap, pbuf, hh=hh):
                                ACT(SZ[:, hh, t0:t0 + n], pap, AF.Silu, [pbuf], [bsz])
                            gemm_fm(wb[1], bw[1], hh * 128, 128, 16, h_rhs, h_bufs, cons_z)
                        rope_flush()
                        nqb = 16 if last else 18
                        steps = []
                        for qb in range(nqb):
                            k = qb % 2
                            q0 = qb * 128
                            if qb < 16:
                                keys = [(qb, 2 if qb == 0 else 0), (qb + 1, None), (qb + 2, 3 if qb == 15 else 1), (18, None), (19, None)]
                            else:
                                keys = [(18, None), (19, None)]
                            for idx, (kt, m) in enumerate(keys):
                                g_ = gi5[0]
                                gi5[0] += 1

                                def A(kt=kt, m=m, q0=q0, g_=g_):
                                    sb = 4 + g_ % 2
                                    p3 = g_ % 3
                                    MM(bank(sb)[:, :].rearrange("p (a q) -> p a q", a=4), KAT[:, kt * 128:(kt + 1) * 128], QT[:, :, q0:q0 + 128],
                                       True, True, bkvl + [bqt], [pb[sb]])
                                    ACT(PT[p3][:], bank(sb)[:, :], AF.Exp, [pb[sb]], [bpt[p3]], scale=scale_a)
                                    if m is not None:
                                        TT(PT[p3][:], PT[p3][:], mk[:, m, :], ALU.mult, [bpt[p3], bc5], [bpt[p3]])

                                def B(kt=kt, k=k, q0=q0, g_=g_, idx=idx, nk_=len(keys), g=g):
                                    p3 = g_ % 3
                                    bO = 0 + k
                                    bL = 2 + k
                                    MM(bank(bO)[:, :], VAT[:, kt, :], PT[p3][:], idx == 0, idx == nk_ - 1, bkvl + [bpt[p3]], [pb[bO]])
                                    MM(bank(bL)[:, :], ones[:, :], PT[p3][:], idx == 0, idx == nk_ - 1, [bones, bpt[p3]], [pb[bL]])
                                    if idx == nk_ - 1:
                                        TT(lt[k][:], bank(bL)[:, :], ESB[:, g * 512:(g + 1) * 512], ALU.add, [pb[bL], bc5], [blt[k]])
                                        ACT(lt[k][:], lt[k][:], AF.Ln, [blt[k]], [blt[k]])
                                        ACT(lt[k][:], lt[k][:], AF.Exp, [blt[k]], [blt[k]], scale=-1.0)
                                        TT(of[k][:], bank(bO)[:, :], lt[k][:], ALU.mult, [pb[bO], blt[k]], [bof[k]])
                                        TT(SZ[:, :, q0:q0 + 128], of[k][:].rearrange("p (a q) -> p a q", a=4), SZ[:, :, q0:q0 + 128], ALU.mult,
                                           [bof[k], bsz], [bsz])
                                steps.append((A, B))
                        pipeline(steps)
                        S.dma('sp', Y[:, (g * 4) * NLOC:(g * 4 + 4) * NLOC].rearrange("p (a t) -> p a t", a=4), SZ[:], reads=[bsz])
                    S.barrier()
            with ExitStack() as ph:
                YT = T(ph, "YT", [128, 24, NLOC], BF16)
                by = Buf()
                for br in range(3):
                    S.dma('sp', YT[:, br * 8:(br + 1) * 8, :], Y[:, br * 8 * NLOC:(br + 1) * 8 * NLOC].rearrange("p (a t) -> p a t", a=8), writes=[by])
                wp = [[T(ph, "wp%d_%d" % (br, i), [128, 8, 256], BF16) for i in range(2)] for br in range(3)]
                bwp = [[Buf(), Buf()] for _ in range(3)]
                sgt = [[T(ph, "sgt%d_%d" % (br, i), [128, NLOC], BF16) for i in range(2)] for br in range(3)]
                bsg = [[Buf(), Buf()] for _ in range(3)]
                acc = T(ph, "acc", [128, NLOC], F32)
                tmp = T(ph, "mtmp", [128, NLOC], F32)
                bacc = [Buf() for _ in range(5)]
                btmp = [Buf() for _ in range(5)]
                mj = [T(ph, "mj%d" % i, [128, NLOC], BF16) for i in range(2)]
                bmj = [Buf(), Buf()]
                rot = [0]

                def nb(n):
                    r = [(rot[0] + i) % 8 for i in range(n)]
                    rot[0] = (rot[0] + n) % 8
                    return r
                for cg in range(8):
                    ck = cg % 2
                    for br in range(3):
                        S.dma('pool', wp[br][ck][:], wsrc(w_p[br][l][:, cg * 256:(cg + 1) * 256]), writes=[bwp[br][ck]])
                    for jj in range(2):
                        j = cg * 2 + jj
                        jk = j % 2
                        for br in range(3):
                            S.dma('sp', sgt[br][jk][:], SG[:, (br * 16 + j) * NLOC:(br * 16 + j + 1) * NLOC], writes=[bsg[br][jk]])
                        for grp in GROUPS:
                            for br in range(3):
                                bks = nb(len(grp))
                                for kc in range(8):
                                    for tb, b in zip(grp, bks):
                                        t0, n = TBS[tb]
                                        MM(bank(b)[:, 0:n], wp[br][ck][:, kc, jj * 128:(jj + 1) * 128], YT[:, br * 8 + kc, t0:t0 + n],
                                           kc == 0, kc == 7, [bwp[br][ck], by], [pb[b]])
                                for tb, b in zip(grp, bks):
                                    t0, n = TBS[tb]
                                    sg = sgt[br][jk][:, t0:t0 + n]
                                    if br == 0:
                                        TT(acc[:, t0:t0 + n], bank(b)[:, 0:n], sg, ALU.mult, [pb[b], bsg[br][jk]], [bacc[tb]])
                                    elif br == 1:
                                        TT(tmp[:, t0:t0 + n], bank(b)[:, 0:n], sg, ALU.mult, [pb[b], bsg[br][jk]], [btmp[tb]])
                                        TT(acc[:, t0:t0 + n], acc[:, t0:t0 + n], tmp[:, t0:t0 + n], ALU.add, [bacc[tb], btmp[tb]], [bacc[tb]])
                                    else:
                                        TT(tmp[:, t0:t0 + n], bank(b)[:, 0:n], sg, ALU.mult, [pb[b], bsg[br][jk]], [btmp[tb]])
                                        TT(mj[jk][:, t0:t0 + n], acc[:, t0:t0 + n], tmp[:, t0:t0 + n], ALU.add, [bacc[tb], btmp[tb]], [bmj[jk]])
                        S.dma('sp', MT[:, j * NLOC:(j + 1) * NLOC], mj[jk][:], reads=[bmj[jk]])
                S.barrier()

            with ExitStack() as ph:
                mT = T(ph, "mT", [128, 16, NLOC], BF16)
                bm = Buf()
                S.dma('sp', mT[:], MT[:, :].rearrange("p (a t) -> p a t", a=16), writes=[bm])
                wo = [T(ph, "wo%d" % i, [128, 16, 512], BF16) for i in range(4)]
                bwo = [Buf() for _ in range(4)]
                for n4 in range(4):
                    S.dma('pool', wo[n4][:], wsrc(w_out[l][:, n4 * 512:(n4 + 1) * 512]), writes=[bwo[n4]])
                gB = [T(ph, "gateB%d" % r, [128, D], F32) for r in range(2)]
                bg = Buf()
                for r in range(2):
                    S.dma('sp', gB[r][:], GATEB[:, (l * 2 + r) * D:(l * 2 + r + 1) * D], writes=[bg])
                if last:
                    fg = T(ph, "fg", [128, D], F32)
                    S.dma('sp', fg[:], fgb[:, :], writes=[bg])
                    junk = T(ph, "junk7", [128, D], BF16)
                    bj = Buf()
                xt = [T(ph, "oxt%d" % i, [128, D], F32) for i in range(2)]
                bx = [Buf(), Buf()]
                ot = [T(ph, "ot%d" % i, [128, D], F32) for i in range(2)]
                bo = [Buf(), Buf()]
                for i in range(16 if last else NTILE):
                    k = i % 2
                    r = 0 if i < 16 else 1
                    S.dma('sp', xt[k][:], xsrc(i), writes=[bx[k]])
                    for n4 in range(4):
                        b = k * 4 + n4
                        for kc in range(16):
                            MM(bank(b)[:, :], mT[:, kc, i * 128:(i + 1) * 128], wo[n4][:, kc, :], kc == 0, kc == 15, [bm, bwo[n4]], [pb[b]])
                    for hf in range(2):
                        P = ps[k * 2 + hf]
                        sl = slice(hf * 1024, (hf + 1) * 1024)
                        TT(ot[k][:, sl], P[:, :], gB[r][:, sl], ALU.mult, [pb[k * 4 + 2 * hf], pb[k * 4 + 2 * hf + 1], bg], [bo[k]])
                        TT(ot[k][:, sl], ot[k][:, sl], xt[k][:, sl], ALU.add, [bo[k], bx[k]], [bo[k]])
                    if not last:
                        S.dma('sp', X1[i * 128:(i + 1) * 128, :], ot[k][:], reads=[bo[k]])
                    else:
                        MS(sst[k][:, 0:1], 0.0, [bss[k]])
                        ACT(junk[:], ot[k][:], AF.Square, [bo[k]], [bj, bss[k]], accum=sst[k][:, 0:1])
                        rstd_from_ss(k, 1.0 / D)
                        STT(ot[k][:], ot[k][:], sst[k][:, 3:4], fg[:], ALU.mult, ALU.mult, [bo[k], bss[k], bg], [bo[k]])
                        S.dma('sp', out[i * 128:(i + 1) * 128, :], ot[k][:], reads=[bo[k]])
                S.barrier()

        phase_mods(0)
        for l in range(n_layers):
            layer(l)
        S.wait_cc()
        S.barrier()
        S.replay(block)
    return nc


def _rope_tables(s):
    pos = (s * NTOK + np.arange(NTOK)).astype(np.float32)
    row = np.floor(pos / 64.0).astype(np.float32)
    col = (pos - row * 64.0).astype(np.float32)

    def tabs(dim):
        half = dim // 4
        inv = (np.float32(THETA) ** (-(np.arange(half, dtype=np.float32)) / np.float32(half))).astype(np.float32)
        cos = np.zeros((128, NTOK), np.float32)
        sin = np.zeros((128, NTOK), np.float32)
        for d in range(dim):
            axis = row if d < dim // 2 else col
            dd = d % (dim // 2)
            ang = (axis * inv[dd % half]).astype(np.float32)
            cos[d] = np.cos(ang)
            sin[d] = np.sin(ang) * (-1.0 if dd < half else 1.0)
        return cos, sin
    c128, s128 = tabs(128)
    c64, s64 = tabs(64)
    return np.ascontiguousarray(np.concatenate([c128, s128, c64, s64], axis=1))


def _perm(dim):
    half = dim // 4
    R = np.zeros((dim, dim), np.float32)
    for m in range(dim):
        dd = m % (dim // 2)
        partner = m + half if dd < half else m - half
        R[partner, m] = 1.0
    return R


def _consts():
    c = np.zeros((128, 580), np.float32)
    c[:, 0:128] = np.eye(128, dtype=np.float32)
    c[:, 128:256] = _perm(128)
    c[0:64, 256:320] = _perm(64)
    c[0, 320:448] = 1.0
    c[1, 448:576] = 1.0
    c[0, 576] = 1.0
    c[1, 577] = 1.0
    return c


def _masks(s):
    j = np.arange(128)[:, None]
    i = np.arange(128)[None, :]
    prev = (j >= i).astype(np.float32)
    nxt = (j <= i).astype(np.float32)
    m = np.zeros((128, 4, 4, 128), np.float32)
    m[:, 0] = prev[:, None, :]
    m[:, 1] = nxt[:, None, :]
    m[:, 2] = 0.0 if s == 0 else prev[:, None, :]
    m[:, 3] = 0.0 if s == 1 else nxt[:, None, :]
    return np.ascontiguousarray(m.reshape(128, 4 * 512))


def make_in_maps(x, c, ctx, c_ctx, ada_w, ada_b, norm_g, w_in, sink_a, mla_gq, mla_gkv, w_uq, w_ukv,
                 sgu_ln_g, sgu_ln_b, sgu_w, sgu_b, w_pa, w_pb, w_pc, w_out, final_g):
    f = lambda a: np.ascontiguousarray(np.asarray(a, dtype=np.float32))
    x, c, ctx, c_ctx = f(x), f(c), f(ctx), f(c_ctx)
    bc = lambda v, n: np.ascontiguousarray(np.broadcast_to(np.asarray(v, np.float32)[:, None, :], (2, 128, n)))
    shared = {
        "ada_w": f(ada_w),
        "ada_b": np.ascontiguousarray(np.broadcast_to(f(ada_b)[:, None, :], (2, 2, 3 * D))),
        "normg": np.ascontiguousarray(f(norm_g).reshape(2, 16, 128).transpose(0, 2, 1)),
        "w_in": f(w_in),
        "sinkb": np.ascontiguousarray(np.broadcast_to(f(sink_a)[:, None, :, None], (2, 128, 8, 128)).reshape(2, 128, 1024)),
        "gqb": bc(mla_gq, 512), "gkvb": bc(mla_gkv, 512),
        "w_uq": f(w_uq), "w_ukv": f(w_ukv),
        "lngb": bc(sgu_ln_g, 1024), "lnbb": bc(sgu_ln_b, 1024),
        "wsT": np.ascontiguousarray(f(sgu_w).transpose(0, 3, 1, 2).reshape(2, 128, 1024)),
        "bsb": np.ascontiguousarray(np.broadcast_to(f(sgu_b).reshape(2, 1, 1024), (2, 128, 1024))),
        "w_pa": f(w_pa), "w_pb": f(w_pb), "w_pc": f(w_pc), "w_out": f(w_out),
        "fgb": np.ascontiguousarray(np.broadcast_to(f(final_g)[None, :], (128, D))),
        "cst": _consts(),
    }
    maps = []
    for core in range(8):
        b, s = core // 2, core % 2
        cv = np.zeros((128, 32), np.float32)
        cv[:, 0::2] = c[b].reshape(16, 128).T
        cv[:, 1::2] = c_ctx.reshape(16, 128).T
        m = dict(shared)
        m["x"] = np.ascontiguousarray(x[b, s * NTOK:(s + 1) * NTOK])
        m["ctx"] = np.ascontiguousarray(ctx[b])
        m["cvec"] = cv
        m["rope"] = _rope_tables(s)
        m["masks"] = _masks(s)
        maps.append(m)
    return maps


def kernel(**inputs):
    nc = build()
    maps = make_in_maps(**inputs)
    res = run_bass_kernel_spmd(nc, maps, core_ids=list(range(8)))
    outp = np.zeros((4, 2 * NTOK, D), np.float32)
    for core in range(8):
        b, s = core // 2, core % 2
        outp[b, s * NTOK:(s + 1) * NTOK] = np.asarray(res.results[core]["out"], dtype=np.float32)
    return outp
```

```python
import numpy as np
from contextlib import ExitStack
import concourse.bass as bass
import concourse.mybir as mybir
from concourse.bass_utils import run_bass_kernel_spmd

F32 = mybir.dt.float32
BF16 = mybir.dt.bfloat16
AF = mybir.ActivationFunctionType
ALU = mybir.AluOpType
AX = mybir.AxisListType

ENGS = ["pe", "act", "dve", "pool", "sp"]
SAME_ENGINE_SYNC = {"pe": False, "act": True, "dve": True, "pool": True, "sp": False}

D = 2048
NTOK = 2048
NCTX = 256
NLOC = NTOK + NCTX
NTILE = NLOC // 128
TBS = [(0, 512), (512, 512), (1024, 512), (1536, 512), (2048, 256)]
GROUPS = [[0, 1, 2], [3, 4]]
IN_COLS = 13888
EPS = 1e-6
THETA = 10000.0


class Buf:
    __slots__ = ("name", "w", "r")

    def __init__(self, name=""):
        self.name = name
        self.w = None
        self.r = []


class Sched:
    def __init__(self, nc, stack, n_dma_sems=56, n_cc=8):
        self.nc = nc
        self.streams = {e: [] for e in ENGS}
        self.esem = {e: stack.enter_context(nc.semaphore("es_" + e)) for e in ENGS}
        self.dsem = [stack.enter_context(nc.semaphore("ds_%d" % i)) for i in range(n_dma_sems)]
        self.csem = [stack.enter_context(nc.semaphore("cs_%d" % i)) for i in range(n_cc)]
        self.psem = [stack.enter_context(nc.semaphore("pf_%d" % i)) for i in range(4)]
        self.puse = [0] * 4
        self.pnext = 0
        self.ncc = 0
        self.duse = [0] * n_dma_sems
        self.dnext = 0
        self.dnext_sw = 0
        self.n_hw = n_dma_sems - 16
        self.seen = {e: {} for e in ENGS}
        self.targets = {e: set() for e in ENGS}
        self.cnt = {e: 0 for e in ENGS}

    def _need(self, eng, tok):
        if tok[0] == 'e' and tok[1] == eng and not SAME_ENGINE_SYNC[eng]:
            return False
        return self.seen[eng].get(tok[0:2], 0) < tok[2]

    def _waits(self, eng, reads, writes, extra=()):
        deps = {}

        def add(tok):
            if tok is None:
                return
            k = tok[0:2]
            if deps.get(k, 0) < tok[2]:
                deps[k] = tok[2]
        for b in reads:
            add(b.w)
        for b in writes:
            add(b.w)
            for t in b.r:
                add(t)
        for t in extra:
            add(t)
        out = []
        for k, v in deps.items():
            tok = (k[0], k[1], v)
            if self._need(eng, tok):
                out.append(tok)
                self.seen[eng][k] = v
                if k[0] == 'e':
                    self.targets[k[1]].add(v)
        return out

    def _mark(self, tok, reads, writes):
        k = tok[0:2]
        for b in writes:
            b.w = tok
            b.r = []
        for b in reads:
            b.r = [t for t in b.r if t[0:2] != k] + [tok]

    def op(self, eng, fn, reads=(), writes=()):
        waits = self._waits(eng, reads, writes)
        self.cnt[eng] += 1
        idx = self.cnt[eng]
        tok = ('e', eng, idx)
        self.streams[eng].append([waits, fn, 'op', idx])
        self._mark(tok, reads, writes)
        return tok

    def dma(self, eng, out_ap, in_ap, reads=(), writes=()):
        if eng == 'pool':
            i = self.n_hw + self.dnext_sw
            self.dnext_sw = (self.dnext_sw + 1) % (len(self.dsem) - self.n_hw)
        else:
            i = self.dnext
            self.dnext = (self.dnext + 1) % self.n_hw
        extra = []
        if self.duse[i] > 0:
            extra.append(('d', i, 16 * self.duse[i]))
        waits = self._waits(eng, reads, writes, extra)
        self.duse[i] += 1
        tok = ('d', i, 16 * self.duse[i])
        self.streams[eng].append([waits, (out_ap, in_ap), 'dma', i])
        self._mark(tok, reads, writes)
        return tok

    def dma_nobar(self, eng, out_ap, in_ap, reads=(), writes=()):
        i = self.pnext
        self.pnext = (self.pnext + 1) % len(self.psem)
        extra = []
        if self.puse[i] > 0:
            extra.append(('p', i, 16 * self.puse[i]))
        waits = self._waits(eng, reads, writes, extra)
        self.puse[i] += 1
        tok = ('p', i, 16 * self.puse[i])
        self.streams[eng].append([waits, (out_ap, in_ap), 'pdma', i])
        self._mark(tok, reads, writes)
        return tok

    def cc(self, fn, reads=(), writes=()):
        i = self.ncc
        self.ncc += 1
        waits = self._waits('pool', reads, writes)
        tok = ('c', i, 1)
        self.streams['pool'].append([waits, fn, 'cc', i])
        self._mark(tok, reads, writes)
        return tok

    def barrier(self):
        cnt = self.cnt
        extra = [('e', e, cnt[e]) for e in ENGS if e != 'sp' and cnt[e] > 0]
        extra += [('d', i, 16 * u) for i, u in enumerate(self.duse) if u > 0]
        waits = self._waits('sp', (), (), extra)
        cnt['sp'] += 1
        idx = cnt['sp']
        self.streams['sp'].append([waits, None, 'inc', idx])
        tok = ('e', 'sp', idx)
        for e in ENGS:
            if e == 'sp':
                continue
            w = self._waits(e, (), (), [tok])
            self.streams[e].append([w, None, 'nop', None])
            for e2 in ENGS:
                self.seen[e][('e', e2)] = max(self.seen[e].get(('e', e2), 0), cnt[e2])
            for i, u in enumerate(self.duse):
                self.seen[e][('d', i)] = 16 * u

    def wait_cc(self):
        extra = [('c', i, 1) for i in range(self.ncc)]
        extra += [('p', i, 16 * u) for i, u in enumerate(self.puse) if u > 0]
        w = self._waits('sp', (), (), extra)
        self.streams['sp'].append([w, None, 'nop', None])

    def replay(self, block):
        vals = {}
        for e in ENGS:
            tg = sorted(self.targets[e])
            vals[e] = {t: n + 1 for n, t in enumerate(tg)}

        def emit_stream(e, eng):
            for waits, fn, kind, info in self.streams[e]:
                for tok in waits:
                    if tok[0] == 'e':
                        eng.wait_ge(self.esem[tok[1]], vals[tok[1]][tok[2]])
                    elif tok[0] == 'd':
                        eng.wait_ge(self.dsem[tok[1]], tok[2])
                    elif tok[0] == 'p':
                        eng.wait_ge(self.psem[tok[1]], tok[2])
                    else:
                        eng.wait_ge(self.csem[tok[1]], tok[2])
                if kind == 'op':
                    ins = fn(eng)
                    if info in vals[e]:
                        ins.then_inc(self.esem[e], 1)
                elif kind == 'dma':
                    o, i_ = fn
                    eng.dma_start(out=o, in_=i_).then_inc(self.dsem[info], 16)
                elif kind == 'pdma':
                    o, i_ = fn
                    eng.dma_start(out=o, in_=i_).then_inc(self.psem[info], 16)
                elif kind == 'cc':
                    fn(eng).then_inc(self.csem[info], 1)
                elif kind == 'inc':
                    if info in vals[e]:
                        eng.sem_inc(self.esem[e], 1)

        @block.tensor
        def _(eng):
            emit_stream('pe', eng)

        @block.scalar
        def _(eng):
            emit_stream('act', eng)

        @block.vector
        def _(eng):
            emit_stream('dve', eng)

        @block.gpsimd
        def _(eng):
            emit_stream('pool', eng)

        @block.sync
        def _(eng):
            emit_stream('sp', eng)


def build(n_layers=2, dbg=()):
    nc = bass.Bass("TRN2", target_bir_lowering=False)

    def din(name, shape, dt=F32):
        return nc.dram_tensor(name, shape, dt, kind="ExternalInput").ap()

    def dint(name, shape, dt):
        return nc.dram_tensor(name, shape, dt, kind="Internal").ap()

    x_in = din("x", [NTOK, D])
    ctx_in = din("ctx", [NCTX, D])
    cvec = din("cvec", [128, 32])
    ada_w = din("ada_w", [2, D, 3 * D])
    ada_b = din("ada_b", [2, 2, 3 * D])
    normg = din("normg", [2, 128, 16])
    w_in = din("w_in", [2, D, IN_COLS])
    sinkb = din("sinkb", [2, 128, 8 * 128])
    gqb = din("gqb", [2, 128, 512])
    gkvb = din("gkvb", [2, 128, 512])
    w_uq = din("w_uq", [2, 512, 1536])
    w_ukv = din("w_ukv", [2, 512, 2048])
    lngb = din("lngb", [2, 128, 1024])
    lnbb = din("lnbb", [2, 128, 1024])
    wsT = din("wsT", [2, 128, 1024])
    bsb = din("bsb", [2, 128, 1024])
    w_p = [din("w_pa", [2, 1024, D]), din("w_pb", [2, 1024, D]), din("w_pc", [2, 1024, D])]
    w_out = din("w_out", [2, D, D])
    fgb = din("fgb", [128, D])
    rope = din("rope", [128, 4 * NTOK])
    cst = din("cst", [128, 580])
    masks_in = din("masks", [128, 4 * 512])
    out = nc.dram_tensor("out", [NTOK, D], F32, kind="ExternalOutput").ap()

    X1 = dint("X1", [NLOC, D], F32)
    SG = dint("SG", [128, 48 * NLOC], BF16)
    SZB = dint("SZB", [128, 8 * NLOC], BF16)
    CQN = dint("CQN", [128, 4 * NLOC], BF16)
    CKVC = dint("CKVC", [128, 4 * NCTX], BF16)
    KRC = dint("KRC", [64, NCTX], BF16)
    SND = [dint("SND0", [128, 4096], BF16), dint("SND1", [128, 4096], BF16), dint("SND2", [128, 3072], BF16)]
    GTH = [[dint("GTH%d_%d" % (l, i), [256, 4096 if i < 2 else 3072], BF16) for i in range(3)] for l in range(2)]
    KA = dint("KA", [128, 2 * NLOC], BF16)
    VA = dint("VA", [128, NTILE * 256], BF16)
    Y = dint("Y", [128, 24 * NLOC], BF16)
    MT = dint("MT", [128, 16 * NLOC], BF16)
    GATEB = dint("GATEB", [128, 2 * 2 * D], F32)
    dbg_out = {}
    for name, shape in dbg:
        dbg_out[name] = nc.dram_tensor("dbg_" + name, shape, F32, kind="ExternalOutput").ap()

    with ExitStack() as st:
        S = Sched(nc, st)

        uid = [0]

        def T(stack, name, shape, dt):
            uid[0] += 1
            return stack.enter_context(nc.sbuf_tensor("%s_u%d" % (name, uid[0]), shape, dt))

        cs = T(st, "cs", [128, 580], F32)
        ident = cs[:, 0:128]
        R128 = cs[:, 128:256]
        R64 = cs[0:64, 256:320]
        sel = cs[0:2, 320:576]
        I2 = cs[0:2, 576:578]
        ones = T(st, "ones", [128, 128], BF16)
        sact = T(st, "sact", [128, 32], BF16)
        cvt = T(st, "cvt", [128, 32], F32)
        G1 = [T(st, "G1_%d" % l, [128, 32], F32) for l in range(2)]
        S1 = [T(st, "S1_%d" % l, [128, 32], F32) for l in range(2)]
        sst = [T(st, "sst%d" % i, [128, 16], F32) for i in range(4)]
        ones32 = T(st, "ones32", [128, 128], F32)
        epst = T(st, "epst", [128, 1], F32)
        ps = [st.enter_context(nc.psum_tensor("ps%d" % i, [128, 1024], F32)) for i in range(4)]
        block = st.enter_context(nc.Block())

        def bank(b):
            return ps[b // 2][:, (b % 2) * 512:(b % 2) * 512 + 512]
        pb = [Buf("bank%d" % i) for i in range(8)]
        bcs, bones, bsact = Buf(), Buf(), Buf()
        bmods = [Buf(), Buf()]
        bss = [Buf(), Buf(), Buf(), Buf()]

        def pipeline(steps, depth=1):
            n = len(steps)
            for i in range(min(depth, n)):
                steps[i][0]()
            for i in range(n):
                if i + depth < n:
                    steps[i + depth][0]()
                steps[i][1]()

        def MM(o, lhsT, rhs, start, stop, r, w):
            S.op('pe', lambda e: e.matmul(o, lhsT, rhs, start=start, stop=stop), r, w)

        def TR(o, i_, r, w):
            S.op('pe', lambda e: e.transpose(out=o, in_=i_, identity=ident), list(r) + [bcs], w)

        def ACT(o, i_, func, r, w, scale=None, bias=None, accum=None):
            kw = {}
            if scale is not None:
                kw['scale'] = scale
            if bias is not None:
                kw['bias'] = bias
            if accum is not None:
                kw['accum_out'] = accum
            S.op('act', lambda e: e.activation(out=o, in_=i_, func=func, **kw), r, w)

        def TT(o, a, b, op, r, w, eng='dve'):
            S.op(eng, lambda e: e.tensor_tensor(out=o, in0=a, in1=b, op=op), r, w)

        def TS(o, a, s1, s2, op0, op1, r, w, eng='dve'):
            if s2 is None:
                S.op(eng, lambda e: e.tensor_scalar(out=o, in0=a, scalar1=s1, scalar2=None, op0=op0), r, w)
            else:
                S.op(eng, lambda e: e.tensor_scalar(out=o, in0=a, scalar1=s1, scalar2=s2, op0=op0, op1=op1), r, w)

        def STT(o, a, s, b, op0, op1, r, w, eng='dve'):
            S.op(eng, lambda e: e.scalar_tensor_tensor(out=o, in0=a, scalar=s, in1=b, op0=op0, op1=op1), r, w)

        def CP(o, i_, r, w, eng='dve'):
            if eng == 'act':
                S.op('act', lambda e: e.activation(out=o, in_=i_, func=AF.Copy), r, w)
            else:
                S.op(eng, lambda e: e.tensor_copy(out=o, in_=i_), r, w)

        def MS(o, v, w, eng='dve'):
            S.op(eng, lambda e: e.memset(o, v), (), w)

        def RCP(o, i_, r, w):
            S.op('dve', lambda e: e.reciprocal(out=o, in_=i_), r, w)

        def rstd_from_ss(k, n_inv, r_extra=()):
            t = sst[k]
            ACT(t[:, 2:3], t[:, 0:1], AF.Sqrt, [bss[k], bones], [bss[k]], scale=n_inv, bias=epst[:, 0:1])
            RCP(t[:, 3:4], t[:, 2:3], [bss[k]], [bss[k]])

        def wsrc(ap2d):
            return ap2d.rearrange("(k p) n -> p k n", p=128)

        S.dma('sp', cs[:], cst[:, :], writes=[bcs])
        S.dma('sp', cvt[:], cvec[:, :], writes=[bsact])
        MS(ones[:], 1.0, [bones])
        MS(ones32[:], 1.0, [bones])
        MS(epst[:], EPS, [bones])
        ACT(sact[:], cvt[:], AF.Silu, [bsact], [bsact])

        def mods_setup(l, ph):
            st_ = {}
            st_['wb'] = [T(ph, "mw%d" % i, [128, 16, 512], BF16) for i in range(2)]
            st_['bw'] = [Buf(), Buf()]
            st_['mrow'] = T(ph, "mrow", [2, 3 * D], F32)
            st_['brow'] = T(ph, "brow", [2, 3 * D], F32)
            st_['ngt'] = T(ph, "ngt", [128, 16], F32)
            st_['mcol'] = T(ph, "mcol", [128, 96], F32)
            st_['gt'] = T(ph, "gt", [128, D], F32)
            for nm in ('bmrow', 'bbrow', 'bng', 'bmcol', 'bgt'):
                st_[nm] = Buf()
            st_['l'] = l
            S.dma('sp', st_['brow'][:], ada_b[l], writes=[st_['bbrow']])
            S.dma('sp', st_['ngt'][:], normg[l], writes=[st_['bng']])
            return st_

        def mods_load(st_, nt):
            l = st_['l']
            S.dma('pool', st_['wb'][nt % 2][:], wsrc(ada_w[l][:, nt * 512:(nt + 1) * 512]), writes=[st_['bw'][nt % 2]])

        def mods_tile(st_, nt):
            wb, bw, mrow, brow = st_['wb'], st_['bw'], st_['mrow'], st_['brow']
            for kc in range(16):
                MM(bank(7)[0:2, :], sact[:, 2 * kc:2 * kc + 2], wb[nt % 2][:, kc, :], kc == 0, kc == 15,
                   [bsact, bw[nt % 2]], [pb[7]])
            TT(mrow[0:2, nt * 512:(nt + 1) * 512], bank(7)[0:2, :], brow[0:2, nt * 512:(nt + 1) * 512], ALU.add,
               [pb[7], st_['bbrow']], [st_['bmrow']])

        def mods_finish(st_):
            l = st_['l']
            mrow, ngt, mcol, gt = st_['mrow'], st_['ngt'], st_['mcol'], st_['gt']
            bmrow, bng, bmcol, bgt = st_['bmrow'], st_['bng'], st_['bmcol'], st_['bgt']
            for f in range(48):
                MM(bank(6)[:, 2 * f:2 * f + 2], mrow[0:2, f * 128:(f + 1) * 128], I2, True, True, [bmrow, bcs], [pb[6]])
            CP(mcol[:], bank(6)[:, 0:96], [pb[6]], [bmcol])
            mc3 = mcol[:].rearrange("p (f r) -> p f r", r=2)
            for r in range(2):
                STT(G1[l][:, r * 16:(r + 1) * 16], mc3[:, 16:32, r], 1.0, ngt[:], ALU.add, ALU.mult, [bmcol, bng], [bmods[l]])
                CP(S1[l][:, r * 16:(r + 1) * 16], mc3[:, 0:16, r], [bmcol], [bmods[l]])
            for r in range(2):
                for n in range(4):
                    MM(bank(5)[:, :], sel[:, r * 128:(r + 1) * 128], mrow[0:2, 2 * D + n * 512:2 * D + (n + 1) * 512], True, True,
                       [bmrow, bcs], [pb[5]])
                    CP(gt[:, n * 512:(n + 1) * 512], bank(5)[:, :], [pb[5]], [bgt], eng='act')
                S.dma('sp', GATEB[:, (l * 2 + r) * D:(l * 2 + r + 1) * D], gt[:], reads=[bgt])

        def phase_mods(l):
            with ExitStack() as ph:
                st_ = mods_setup(l, ph)
                for nt in range(12):
                    mods_load(st_, nt)
                    mods_tile(st_, nt)
                mods_finish(st_)
                S.barrier()

        rope_pend = []

        def rope_flush():
            cur = rope_pend[:]
            del rope_pend[:]
            for f in cur:
                f()

        def gemm_fm(wt, bw, c0, M, nk, rhs_of, rhs_bufs, consumer, tbs=(0, 1, 2, 3, 4)):
            for grp in GROUPS:
                g = [tb for tb in grp if tb in tbs]
                if not g:
                    continue
                for kc in range(nk):
                    for tb in g:
                        t0, n = TBS[tb]
                        MM(bank(tb)[0:M, 0:n], wt[:, kc, c0:c0 + M], rhs_of(kc, t0, n), kc == 0, kc == nk - 1,
                           [bw] + rhs_bufs(tb), [pb[tb]])
                rope_flush()
                for tb in g:
                    t0, n = TBS[tb]
                    consumer(tb, t0, n, bank(tb)[0:M, 0:n], pb[tb])

        def layer(l):
            last = (l == n_layers - 1)
            with ExitStack() as lay:
                hT = T(lay, "hT", [128, 16, NLOC], BF16)
                bhT = [Buf("hT%d" % i) for i in range(2 * NTILE)]
                bgth = [Buf(), Buf(), Buf()]
                wl = ExitStack()
                wbL = [T(wl, "wbL%d" % i, [128, 16, 512], BF16) for i in range(2)]
                bwL = [Buf(), Buf()]
                pref = set()

                def wload(k, c0, ncol=512):
                    if (k, c0) in pref:
                        pref.discard((k, c0))
                        return
                    S.dma('pool', wbL[k][:, :, 0:ncol], wsrc(w_in[l][:, c0:c0 + ncol]), writes=[bwL[k]])

                def wprefetch(k, c0, ncol=512):
                    S.dma_nobar('pool', wbL[k][:, :, 0:ncol], wsrc(w_in[l][:, c0:c0 + ncol]), writes=[bwL[k]])
                    pref.add((k, c0))

                def h_rhs(kc, t0, n):
                    return hT[:, kc, t0:t0 + n]

                def h_bufs(tb):
                    t0, n = TBS[tb]
                    return bhT[2 * (t0 // 128):2 * ((t0 + n) // 128)]

                def xsrc(i):
                    if l == 0:
                        return x_in[i * 128:(i + 1) * 128, :] if i < 16 else ctx_in[(i - 16) * 128:(i - 15) * 128, :]
                    return X1[i * 128:(i + 1) * 128, :]

                with ExitStack() as ph:
                    xt = [T(ph, "xt%d" % i, [128, D], F32) for i in range(4)]
                    junk = T(ph, "junk", [128, D], BF16)
                    bx = [Buf(), Buf(), Buf(), Buf()]
                    bj = Buf()

                    def p1A(i):
                        k = i % 4
                        S.dma('sp', xt[k][:], xsrc(i), writes=[bx[k]])
                        MS(sst[k][:, 0:1], 0.0, [bss[k]])
                        ACT(junk[:], xt[k][:], AF.Square, [bx[k]], [bj, bss[k]], accum=sst[k][:, 0:1])
                        rstd_from_ss(k, 1.0 / D)
                        TS(xt[k][:], xt[k][:], sst[k][:, 3:4], None, ALU.mult, None, [bss[k], bx[k]], [bx[k]])

                    def p1B(i):
                        k = i % 4
                        r = 0 if i < 16 else 1
                        for q in range(4):
                            b = 4 + q
                            for j in range(4):
                                kc = q * 4 + j
                                TR(bank(b)[:, j * 128:(j + 1) * 128], xt[k][:, kc * 128:(kc + 1) * 128], [bx[k]], [pb[b]])
                            for j in range(4):
                                kc = q * 4 + j
                                o = hT[:, kc, i * 128:(i + 1) * 128]
                                src = bank(b)[:, j * 128:(j + 1) * 128]
                                gc = G1[l][:, r * 16 + kc:r * 16 + kc + 1]
                                sc = S1[l][:, r * 16 + kc:r * 16 + kc + 1]
                                if j % 2 == 0:
                                    ACT(o, src, AF.Identity, [pb[b], bmods[l]], [bhT[2 * i]], scale=gc, bias=sc)
                                else:
                                    TS(o, src, gc, sc, ALU.mult, ALU.add, [pb[b], bmods[l]], [bhT[2 * i + 1]])
                    pipeline([(lambda i=i: p1A(i), lambda i=i: p1B(i)) for i in range(NTILE)], depth=2)
                    wprefetch(0, 7744)
                    S.barrier()
                if 'hT' in dbg_out:
                    with ExitStack() as ph:
                        tmp = T(ph, "dbgt", [128, NLOC], F32)
                        bt = Buf()
                        for kc in range(16):
                            CP(tmp[:], hT[:, kc, :], bhT, [bt])
                            S.dma('sp', dbg_out['hT'][l * 16 + kc], tmp[:], reads=[bt])
                        S.barrier()

                def fm_to_dram(jobs, side_l=None):
                    with ExitStack() as ph:
                        wb = wbL
                        bw = bwL
                        stg = [T(ph, "stg%d" % i, [128, NLOC], BF16) for i in range(2)]
                        bstg = [Buf(), Buf()]
                        mst = mods_setup(side_l, ph) if side_l is not None else None
                        tiles = []
                        for (c0, nblk, func, dst) in jobs:
                            for wi in range((nblk + 3) // 4):
                                tiles.append((c0, nblk, func, dst, wi))
                        for ti, (c0, nblk, func, dst, wi) in enumerate(tiles):
                            ncol = min(512, nblk * 128 - wi * 512)
                            k = ti % 2
                            wload(k, c0 + wi * 512, ncol)
                            if mst is not None and ti < 12:
                                mods_load(mst, ti)
                                if ti >= 1:
                                    mods_tile(mst, ti - 1)
                            if mst is not None and ti == 12:
                                mods_tile(mst, 11)
                                mods_finish(mst)
                            for j in range(ncol // 128):
                                blk = wi * 4 + j
                                sk = (ti * 4 + j) % 2

                                def cons(tb, t0, n, pap, pbuf, sk=sk, func=func):
                                    ACT(stg[sk][:, t0:t0 + n], pap, func, [pbuf], [bstg[sk]])
                                gemm_fm(wb[k], bw[k], j * 128, 128, 16, h_rhs, h_bufs, cons)
                                S.dma('sp', dst[:, blk * NLOC:(blk + 1) * NLOC], stg[sk][:], reads=[bstg[sk]])
                        wprefetch(0, 2560)
                        wprefetch(1, 3072)
                        S.barrier()

                fm_to_dram([(7744, 48, AF.Sigmoid, SG), (3648, 8, AF.Silu, SZB)], side_l=(l + 1 if l + 1 < n_layers else None))

                def rope_fm(M, pap, pbuf, rawf, braw, Rm, cos, sin, tmpa, btmp, o, bo, pbank, btab):
                    CP(rawf[0:M, :], pap, [pbuf], [braw], eng='act')

                    def stage2():
                        MM(bank(pbank)[0:M, 0:512], Rm, rawf[0:M, :], True, True, [braw, bcs], [pb[pbank]])
                        TT(tmpa[0:M, :], rawf[0:M, :], cos, ALU.mult, [braw, btab], [btmp])
                        TT(rawf[0:M, :], bank(pbank)[0:M, 0:512], sin, ALU.mult, [pb[pbank], braw, btab], [braw])
                        TT(o, tmpa[0:M, :], rawf[0:M, :], ALU.add, [btmp, braw], [bo])
                    rope_pend.append(stage2)

                with ExitStack() as ph:
                    wb = wbL
                    bw = bwL
                    gB = [T(ph, "gB%d" % i, [128, 512], F32) for i in range(2)]
                    bgB = Buf()
                    tab = T(ph, "tab", [128, 4, NTOK], F32)
                    btab = Buf()
                    latT1 = T(ph, "latT", [128, 4, NLOC], BF16)
                    latT = [latT1, latT1]
                    blat1 = Buf()
                    blat = [blat1, blat1]
                    nrm = [T(ph, "nrm%d" % i, [128, 512], F32) for i in range(3)]
                    bnrm = [Buf(), Buf(), Buf()]
                    junk = T(ph, "junk2", [128, 512], BF16)
                    bj = Buf()
                    krT = T(ph, "krT", [128, NLOC], BF16)
                    bkr = Buf()
                    MS(krT[64:128, :], 0.0, [bkr])
                    akT = T(ph, "akT", [128, 2, NLOC], BF16)
                    bak = Buf()
                    avt = T(ph, "avt", [128, NTILE, 256], BF16)
                    bav = Buf()
                    rawf = [T(ph, "rawf%d" % i, [128, 512], F32) for i in range(3)]
                    braw = [Buf(), Buf(), Buf()]
                    tmpa = [T(ph, "tmpa%d" % i, [128, 512], F32) for i in range(3)]
                    btmp = [Buf(), Buf(), Buf()]
                    S.dma('sp', gB[0][:], gqb[l], writes=[bgB])
                    S.dma('sp', gB[1][:], gkvb[l], writes=[bgB])
                    S.dma('sp', tab[:].rearrange("p a t -> p (a t)"), rope[:, :], writes=[btab])
                    bsnd = [Buf(), Buf(), Buf()]
                    for which in range(2):
                        c0 = 2560 + which * 512
                        wload(which, c0)
                        def p2A(i, which=which):
                            k = i % 3
                            b = k
                            for kc in range(16):
                                MM(bank(b)[:, :], hT[:, kc, i * 128:(i + 1) * 128], wb[which][:, kc, :], kc == 0, kc == 15,
                                   [bhT[2 * i], bhT[2 * i + 1], bw[which]], [pb[b]])
                            MS(sst[k][:, 0:1], 0.0, [bss[k]])
                            ACT(junk[:], bank(b)[:, :], AF.Square, [pb[b]], [bj, bss[k]], accum=sst[k][:, 0:1])
                            rstd_from_ss(k, 1.0 / 512)
                            STT(nrm[k][:], bank(b)[:, :], sst[k][:, 3:4], gB[which][:], ALU.mult, ALU.mult,
                                [pb[b], bss[k], bgB], [bnrm[k]])

                        def p2B(i, which=which):
                            k = i % 3
                            tbk = 3 + i % 2
                            for c in range(4):
                                TR(bank(tbk)[:, c * 128:(c + 1) * 128], nrm[k][:, c * 128:(c + 1) * 128], [bnrm[k]], [pb[tbk]])
                            CP(latT[which][:, :, i * 128:(i + 1) * 128], bank(tbk)[:, :].rearrange("p (c t) -> p c t", c=4),
                               [pb[tbk]], [blat[which]], eng='act')
                        pipeline([(lambda i=i: p2A(i), lambda i=i: p2B(i)) for i in range(NTILE)], depth=2)
                        if which == 0:
                            S.dma('sp', CQN[:, :].rearrange("p (c t) -> p c t", c=4), latT[0][:], reads=[blat[0]])
                    S.dma('sp', SND[0][:, :].rearrange("p (c t) -> p c t", c=2), latT[1][:, 0:2, 0:NTOK], reads=[blat[1]], writes=[bsnd[0]])
                    S.dma('sp', SND[1][:, :].rearrange("p (c t) -> p c t", c=2), latT[1][:, 2:4, 0:NTOK], reads=[blat[1]], writes=[bsnd[1]])
                    S.dma('sp', CKVC[:, :].rearrange("p (c t) -> p c t", c=4), latT[1][:, :, NTOK:NLOC], reads=[blat[1]])
                    wload(0, 3584, 64)

                    def cons_kr(tb, t0, n, pap, pbuf):
                        if tb < 4:
                            k = tb % 3
                            rope_fm(64, pap, pbuf, rawf[k], braw[k], R64, tab[0:64, 2, t0:t0 + n], tab[0:64, 3, t0:t0 + n],
                                    tmpa[k], btmp[k], krT[0:64, t0:t0 + n], bkr, 5 + k, btab)
                        else:
                            CP(krT[0:64, t0:t0 + n], pap, [pbuf], [bkr], eng='act')
                    gemm_fm(wb[0], bw[0], 0, 64, 16, h_rhs, h_bufs, cons_kr)
                    rope_flush()
                    S.dma('sp', SND[2][:, 0:NTOK], krT[:, 0:NTOK], reads=[bkr], writes=[bsnd[2]])
                    S.dma('sp', KRC[:, :], krT[0:64, NTOK:NLOC], reads=[bkr])
                    wload(1, 1024)
                    for h in range(2):
                        def cons_ak(tb, t0, n, pap, pbuf, h=h):
                            if tb < 4:
                                k = tb % 3
                                rope_fm(128, pap, pbuf, rawf[k], braw[k], R128, tab[:, 0, t0:t0 + n], tab[:, 1, t0:t0 + n],
                                        tmpa[k], btmp[k], akT[:, h, t0:t0 + n], bak, 5 + k, btab)
                            else:
                                CP(akT[:, h, t0:t0 + n], pap, [pbuf], [bak], eng='act')
                        gemm_fm(wb[1], bw[1], h * 128, 128, 16, h_rhs, h_bufs, cons_ak)
                    rope_flush()
                    for i in range(NTILE):
                        b = 5 + i % 2
                        for kc in range(16):
                            MM(bank(b)[:, 0:256], hT[:, kc, i * 128:(i + 1) * 128], wb[1][:, kc, 256:512], kc == 0, kc == 15,
                               [bhT[2 * i], bhT[2 * i + 1], bw[1]], [pb[b]])
                        CP(avt[:, i, :], bank(b)[:, 0:256], [pb[b]], [bav], eng='act' if i % 2 else 'dve')
                    S.dma('sp', KA[:, :].rearrange("p (h t) -> p h t", h=2), akT[:], reads=[bak])
                    S.dma('sp', VA[:, :].rearrange("p (i c) -> p i c", c=256), avt[:], reads=[bav])
                    for which, t0 in ((0, 0), (1, NTOK - 128)):
                        S.dma('sp', SND[2][:, 2048 + which * 256:2048 + (which + 1) * 256].rearrange("p (h t) -> p h t", h=2),
                              akT[:, :, t0:t0 + 128], reads=[bak], writes=[bsnd[2]])
                        S.dma('sp', SND[2][:, 2560 + which * 256:2560 + (which + 1) * 256], avt[:, t0 // 128, :],
                              reads=[bav], writes=[bsnd[2]])
                    for i in range(3):
                        def ccf(e, i=i):
                            return e.collective_compute("AllGather", ALU.bypass, replica_groups=[[0, 1], [2, 3], [4, 5], [6, 7]],
                                                        ins=[SND[i][:, :]], outs=[GTH[l][i][:, :]])
                        S.cc(ccf, reads=[bsnd[i]], writes=[bgth[i]])
                    wprefetch(0, 5696)
                    wprefetch(1, 6208)
                    S.barrier()

                with ExitStack() as ph:
                    wb = wbL
                    bw = bwL
                    lnG = T(ph, "lnG", [128, 1024], F32)
                    lnB = T(ph, "lnB", [128, 1024], F32)
                    BS = T(ph, "BS", [128, 1024], F32)
                    wst = T(ph, "wst", [128, 8, 128], BF16)
                    bc3 = Buf()
                    MIX = T(ph, "MIX", [128, 8, NLOC], BF16)
                    bmix = [Buf() for _ in range(8)]
                    cvf = [T(ph, "cvf%d" % i, [128, 1024], F32) for i in range(2)]
                    bcvf = [Buf(), Buf()]
                    vn = [T(ph, "vn%d" % i, [128, 1024], BF16) for i in range(2)]
                    bvn = [Buf(), Buf()]
                    junk = T(ph, "junk3", [128, 1024], BF16)
                    bj = Buf()
                    szt = [T(ph, "szt%d" % i, [128, NLOC], BF16) for i in range(2)]
                    bsz = [Buf(), Buf()]
                    S.dma('sp', lnG[:], lngb[l], writes=[bc3])
                    S.dma('sp', lnB[:], lnbb[l], writes=[bc3])
                    S.dma('sp', BS[:], bsb[l], writes=[bc3])
                    S.dma('pool', wst[:].rearrange("p g q -> p (g q)"), wsT[l], writes=[bc3])
                    for hf in range(2):
                        wload(hf, 5696 + hf * 512)
                    def p3A(i):
                        k = i % 2
                        P = ps[k]
                        for hf in range(2):
                            for kc in range(16):
                                MM(P[:, hf * 512:(hf + 1) * 512], hT[:, kc, i * 128:(i + 1) * 128], wb[hf][:, kc, :], kc == 0, kc == 15,
                                   [bhT[2 * i], bhT[2 * i + 1], bw[hf]], [pb[2 * k + hf]])
                        pbs = [pb[2 * k], pb[2 * k + 1]]
                        t = sst[k]
                        MS(t[:, 0:1], 0.0, [bss[k]])
                        S.op('dve', lambda e, t=t, P=P: e.reduce_sum(out=t[:, 4:5], in_=P[:, :], axis=AX.X), pbs, [bss[k]])
                        ACT(junk[:], P[:, :], AF.Square, pbs, [bj, bss[k]], accum=t[:, 0:1])
                        TS(t[:, 5:6], t[:, 4:5], 1.0 / 1024, None, ALU.mult, None, [bss[k]], [bss[k]])
                        TT(t[:, 6:7], t[:, 5:6], t[:, 5:6], ALU.mult, [bss[k]], [bss[k]])
                        TS(t[:, 7:8], t[:, 0:1], 1.0 / 1024, None, ALU.mult, None, [bss[k]], [bss[k]])
                        TT(t[:, 7:8], t[:, 7:8], t[:, 6:7], ALU.subtract, [bss[k]], [bss[k]])
                        TS(t[:, 1:2], t[:, 7:8], EPS, None, ALU.add, None, [bss[k]], [bss[k]])
                        ACT(t[:, 2:3], t[:, 1:2], AF.Sqrt, [bss[k]], [bss[k]])
                        RCP(t[:, 3:4], t[:, 2:3], [bss[k]], [bss[k]])
                        STT(t[:, 8:9], t[:, 5:6], -1.0, t[:, 3:4], ALU.mult, ALU.mult, [bss[k]], [bss[k]])
                        ACT(cvf[k][:], P[:, :], AF.Identity, pbs + [bss[k]], [bcvf[k]], scale=t[:, 3:4], bias=t[:, 8:9])
                        TT(cvf[k][:], cvf[k][:], lnG[:], ALU.mult, [bcvf[k], bc3], [bcvf[k]])
                        TT(vn[k][:], cvf[k][:], lnB[:], ALU.add, [bcvf[k], bc3], [bvn[k]])

                    def p3B(i):
                        k = i % 2
                        Pm = ps[2 + k]
                        for g in range(8):
                            MM(Pm[:, g * 128:(g + 1) * 128], vn[k][:, g * 128:(g + 1) * 128], wst[:, g, :], True, True,
                               [bvn[k], bc3], [pb[4 + 2 * k + g // 4]])
                        TT(MIX[:, :, i * 128:(i + 1) * 128], Pm[:, :].rearrange("p (g q) -> p g q", g=8),
                           BS[:].rearrange("p (g q) -> p g q", g=8), ALU.add, [pb[4 + 2 * k], pb[5 + 2 * k], bc3], bmix)
                    pipeline([(lambda i=i: p3A(i), lambda i=i: p3B(i)) for i in range(NTILE)])
                    for wi in range(2):
                        wload(wi, 6720 + wi * 512)
                        for j in range(4):
                            g = wi * 4 + j
                            sk = g % 2

                            def cons_cz(tb, t0, n, pap, pbuf, sk=sk):
                                ACT(szt[sk][:, t0:t0 + n], pap, AF.Silu, [pbuf], [bsz[sk]])
                            gemm_fm(wb[wi], bw[wi], j * 128, 128, 16, h_rhs, h_bufs, cons_cz)
                            TT(MIX[:, g, :], MIX[:, g, :], szt[sk][:], ALU.mult, [bmix[g], bsz[sk]], [bmix[g]])
                    for wi in range(2):
                        wload(wi, 4672 + wi * 512)
                        for j in range(4):
                            g = wi * 4 + j

                            def cons_cu(tb, t0, n, pap, pbuf, g=g):
                                TT(MIX[:, g, t0:t0 + n], pap, MIX[:, g, t0:t0 + n], ALU.mult, [pbuf, bmix[g]], [bmix[g]])
                            gemm_fm(wb[wi], bw[wi], j * 128, 128, 16, h_rhs, h_bufs, cons_cu)
                            S.dma('sp', Y[:, (16 + g) * NLOC:(17 + g) * NLOC], MIX[:, g, :], reads=[bmix[g]])
                    wprefetch(0, 0)
                    wprefetch(1, 1536)
                    S.barrier()

                with ExitStack() as ph:
                    wb = wbL
                    bw = bwL
                    tab = T(ph, "tab128", [128, 2, NTOK], F32)
                    mk = T(ph, "mk", [128, 4, 512], BF16)
                    skb = T(ph, "skb", [128, 1024], F32)
                    ESB = T(ph, "ESB", [128, 1024], F32)
                    bc5 = Buf()
                    S.dma('sp', tab[:], rope[:, 0:2 * NTOK].rearrange("p (a t) -> p a t", a=2), writes=[bc5])
                    S.dma('pool', mk[:].rearrange("p a t -> p (a t)"), masks_in[:, :], writes=[bc5])
                    S.dma('sp', skb[:], sinkb[l], writes=[bc5])
                    ACT(ESB[:], skb[:], AF.Exp, [bc5], [bc5])
                    KAT = T(ph, "KAT", [128, 20 * 128], BF16)
                    VAT = T(ph, "VAT", [128, 20, 128], BF16)
                    bkvl = [Buf() for _ in range(8)]
                    QT = T(ph, "QT", [128, 4, NLOC], BF16)
                    bqt = Buf()
                    SZ = T(ph, "SZ", [128, 4, NLOC], BF16)
                    bsz = Buf()
                    rawf = [T(ph, "arawf%d" % i, [128, 512], F32) for i in range(3)]
                    braw = [Buf(), Buf(), Buf()]
                    tmpa = [T(ph, "atmpa%d" % i, [128, 512], F32) for i in range(3)]
                    btmp = [Buf(), Buf(), Buf()]
                    PT = [T(ph, "aPT%d" % i, [128, 512], BF16) for i in range(3)]
                    bpt = [Buf(), Buf(), Buf()]
                    lt = [T(ph, "lt%d" % i, [128, 512], F32) for i in range(2)]
                    blt = [Buf(), Buf()]
                    of = [T(ph, "aof%d" % i, [128, 512], F32) for i in range(2)]
                    bof = [Buf(), Buf()]
                    scale_a = 128 ** -0.5
                    gi5 = [0]
                    G2 = GTH[l][2]
                    for g in range(2):
                        S.dma('sp', KAT[:, 128:128 + NTOK], KA[:, g * NLOC:g * NLOC + NTOK], writes=[bkvl[0]])
                        S.dma('sp', KAT[:, 18 * 128:20 * 128], KA[:, g * NLOC + NTOK:(g + 1) * NLOC], writes=[bkvl[1]])
                        S.dma('sp', KAT[:, 0:128], G2[0:128, 2048 + 256 + g * 128:2048 + 256 + (g + 1) * 128], reads=[bgth[2]], writes=[bkvl[2]])
                        S.dma('sp', KAT[:, 17 * 128:18 * 128], G2[128:256, 2048 + g * 128:2048 + (g + 1) * 128], reads=[bgth[2]], writes=[bkvl[3]])
                        va3 = VA[:, :].rearrange("p (i c) -> p i c", c=256)
                        S.dma('sp', VAT[:, 1:17, :], va3[:, 0:16, g * 128:(g + 1) * 128], writes=[bkvl[4]])
                        S.dma('sp', VAT[:, 18:20, :], va3[:, 16:18, g * 128:(g + 1) * 128], writes=[bkvl[5]])
                        S.dma('sp', VAT[:, 0, :], G2[0:128, 2560 + 256 + g * 128:2560 + 256 + (g + 1) * 128], reads=[bgth[2]], writes=[bkvl[6]])
                        S.dma('sp', VAT[:, 17, :], G2[128:256, 2560 + g * 128:2560 + (g + 1) * 128], reads=[bgth[2]], writes=[bkvl[7]])
                        wload(0, g * 512)
                        wload(1, 1536 + g * 512)
                        for hh in range(4):
                            def cons_q(tb, t0, n, pap, pbuf, hh=hh):
                                if tb < 4:
                                    k = tb % 3
                                    rope_fm(128, pap, pbuf, rawf[k], braw[k], R128, tab[:, 0, t0:t0 + n], tab[:, 1, t0:t0 + n],
                                            tmpa[k], btmp[k], QT[:, hh, t0:t0 + n], bqt, 5 + k, bc5)
                                else:
                                    CP(QT[:, hh, t0:t0 + n], pap, [pbuf], [bqt], eng='act')
                            gemm_fm(wb[0], bw[0], hh * 128, 128, 16, h_rhs, h_bufs, cons_q)
                        for hh in range(4):
                            def cons_z(tb, t0, n, pap, pbuf, hh=hh):
                                ACT(SZ[:, hh, t0:t0 + n], pap, AF.Silu, [pbuf], [bsz])
                            gemm_fm(wb[1], bw[1], hh * 128, 128, 16, h_rhs, h_bufs, cons_z)
                        rope_flush()
                        nqb = 16 if last else 18
                        steps = []
                        for qb in range(nqb):
                            k = qb % 2
                            q0 = qb * 128
                            if qb < 16:
                                keys = [(qb, 2 if qb == 0 else 0), (qb + 1, None), (qb + 2, 3 if qb == 15 else 1), (18, None), (19, None)]
                            else:
                                keys = [(18, None), (19, None)]
                            for idx, (kt, m) in enumerate(keys):
                                g_ = gi5[0]
                                gi5[0] += 1

                                def A(kt=kt, m=m, q0=q0, g_=g_):
                                    sb = 4 + g_ % 2
                                    p3 = g_ % 3
                                    MM(bank(sb)[:, :].rearrange("p (a q) -> p a q", a=4), KAT[:, kt * 128:(kt + 1) * 128], QT[:, :, q0:q0 + 128],
                                       True, True, bkvl + [bqt], [pb[sb]])
                                    ACT(PT[p3][:], bank(sb)[:, :], AF.Exp, [pb[sb]], [bpt[p3]], scale=scale_a)
                                    if m is not None:
                                        TT(PT[p3][:], PT[p3][:], mk[:, m, :], ALU.mult, [bpt[p3], bc5], [bpt[p3]])

                                def B(kt=kt, k=k, q0=q0, g_=g_, idx=idx, nk_=len(keys), g=g):
                                    p3 = g_ % 3
                                    bO = 0 + k
                                    bL = 2 + k
                                    MM(bank(bO)[:, :], VAT[:, kt, :], PT[p3][:], idx == 0, idx == nk_ - 1, bkvl + [bpt[p3]], [pb[bO]])
                                    MM(bank(bL)[:, :], ones[:, :], PT[p3][:], idx == 0, idx == nk_ - 1, [bones, bpt[p3]], [pb[bL]])
                                    if idx == nk_ - 1:
                                        TT(lt[k][:], bank(bL)[:, :], ESB[:, g * 512:(g + 1) * 512], ALU.add, [pb[bL], bc5], [blt[k]])
                                        ACT(lt[k][:], lt[k][:], AF.Ln, [blt[k]], [blt[k]])
                                        ACT(lt[k][:], lt[k][:], AF.Exp, [blt[k]], [blt[k]], scale=-1.0)
                                        TT(of[k][:], bank(bO)[:, :], lt[k][:], ALU.mult, [pb[bO], blt[k]], [bof[k]])
                                        TT(SZ[:, :, q0:q0 + 128], of[k][:].rearrange("p (a q) -> p a q", a=4), SZ[:, :, q0:q0 + 128], ALU.mult,
                                           [bof[k], bsz], [bsz])
                                steps.append((A, B))
                        pipeline(steps)
                        S.dma('sp', Y[:, (g * 4) * NLOC:(g * 4 + 4) * NLOC].rearrange("p (a t) -> p a t", a=4), SZ[:], reads=[bsz])
                    S.barrier()
                wl.close()
                with ExitStack() as ph:
                    NK = 2 * NTOK + NCTX
                    cqs = [T(ph, "cqs%d" % i, [128, 4, 512], BF16) for i in range(2)]
                    bcqs = [Buf(), Buf()]
                    CQN3 = CQN[:, :].rearrange("p (c t) -> p c t", c=4)
                    ckT = T(ph, "ckT", [128, 4, NK], BF16)
                    krF = T(ph, "krF", [128, NK], BF16)
                    tab = T(ph, "tab64", [64, 2, NTOK], F32)
                    bldl = [Buf() for _ in range(9)]
                    bld = bldl[0]
                    ii = 0
                    for gi in range(2):
                        for half in range(2):
                            S.dma('sp', ckT[:, 2 * gi:2 * gi + 2, half * NTOK:(half + 1) * NTOK],
                                  GTH[l][gi][half * 128:(half + 1) * 128, :].rearrange("p (c t) -> p c t", c=2), reads=[bgth[gi]], writes=[bldl[ii]])
                            ii += 1
                    S.dma('sp', ckT[:, :, 2 * NTOK:NK], CKVC[:, :].rearrange("p (c t) -> p c t", c=4), writes=[bldl[4]])
                    for half in range(2):
                        S.dma('sp', krF[0:64, half * NTOK:(half + 1) * NTOK], GTH[l][2][half * 128:half * 128 + 64, 0:NTOK], reads=[bgth[2]], writes=[bldl[5 + half]])
                    S.dma('sp', krF[0:64, 2 * NTOK:NK], KRC[:, :], writes=[bldl[7]])
                    S.dma('sp', tab[:], rope[0:64, 2 * NTOK:4 * NTOK].rearrange("p (a t) -> p a t", a=2), writes=[bldl[8]])
                    MS(krF[64:128, :], 0.0, [bldl[7]])
                    wq = [T(ph, "wq%d" % i, [128, 4, 192], BF16) for i in range(2)]
                    wkv = [T(ph, "wkv%d" % i, [128, 4, 256], BF16) for i in range(2)]
                    bwh = [Buf(), Buf()]
                    KnT = T(ph, "KnT", [128, NK], BF16)
                    bkn = Buf()
                    Vh = T(ph, "Vh", [128, NK // 128, 128], BF16)
                    bvh = Buf()
                    qn = [T(ph, "qn%d" % i, [128, 512], BF16) for i in range(2)]
                    qr = [T(ph, "qr%d" % i, [128, 512], BF16) for i in range(2)]
                    bq = [Buf(), Buf()]
                    for i_ in range(2):
                        MS(qr[i_][64:128, :], 0.0, [bq[i_]])
                    rawf = T(ph, "mrawf", [64, 512], F32)
                    braw = Buf()
                    tmpa = T(ph, "mtmpa", [64, 512], F32)
                    btmp = Buf()
                    PT = [T(ph, "PT%d" % i, [128, 2, 512], BF16) for i in range(3)]
                    bpt = [Buf(), Buf(), Buf()]
                    rinv = [T(ph, "rinv%d" % i, [128, 512], F32) for i in range(2)]
                    brinv = [Buf(), Buf()]
                    accD = [T(ph, "accD%d" % i, [128, 2, 512], F32) for i in range(2)]
                    accP = [T(ph, "accP%d" % i, [128, 2, 512], F32) for i in range(2)]
                    baccD = [Buf(), Buf()]
                    baccP = [Buf(), Buf()]
                    szb = [T(ph, "szb%d" % i, [128, 512], BF16) for i in range(3)]
                    bszb = [Buf(), Buf(), Buf()]
                    ybt = [T(ph, "ybt%d" % i, [128, 512], BF16) for i in range(2)]
                    bybt = [Buf(), Buf()]
                    scale_b = (128 + 64) ** -0.5
                    NKT = NK // 128

                    def load_w(h):
                        hk = h % 2
                        S.dma('pool', wq[hk][:], wsrc(w_uq[l][:, h * 192:(h + 1) * 192]), writes=[bwh[hk]])
                        S.dma('pool', wkv[hk][:], wsrc(w_ukv[l][:, h * 256:(h + 1) * 256]), writes=[bwh[hk]])

                    def kv_proj(h):
                        hk = h % 2
                        for kb in range((NK + 511) // 512):
                            k0 = kb * 512
                            n = min(512, NK - k0)
                            b = kb % 2
                            for kc in range(4):
                                MM(bank(b)[:, 0:n], wkv[hk][:, kc, 0:128], ckT[:, kc, k0:k0 + n], kc == 0, kc == 3, [bwh[hk]] + bldl, [pb[b]])
                            CP(KnT[:, k0:k0 + n], bank(b)[:, 0:n], [pb[b]], [bkn], eng='act')
                        for kt in range(NKT):
                            b = (kt // 4) % 2
                            j = kt % 4
                            for kc in range(4):
                                MM(bank(b)[:, j * 128:(j + 1) * 128], ckT[:, kc, kt * 128:(kt + 1) * 128], wkv[hk][:, kc, 128:256],
                                   kc == 0, kc == 3, [bwh[hk]] + bldl, [pb[b]])
                            if j == 3 or kt == NKT - 1:
                                k0 = kt - j
                                CP(Vh[:, k0:kt + 1, :], bank(b)[:, 0:(j + 1) * 128].rearrange("p (a d) -> p a d", d=128), [pb[b]], [bvh],
                                   eng='act' if (kt // 4) % 2 else 'dve')

                    def prologue(h, tb, k, part, z=0):
                        hk = h % 2
                        t0, n = TBS[tb]
                        if part == 0:
                            for kc in range(4):
                                MM(bank(0)[:, 0:n], wq[hk][:, kc, 0:128], cqs[k][:, kc, 0:n], kc == 0, kc == 3, [bwh[hk], bcqs[k]], [pb[0]])
                            CP(qn[k][:, 0:n], bank(0)[:, 0:n], [pb[0]], [bq[k]], eng='act')
                            for kc in range(4):
                                MM(bank(1)[0:64, 0:n], wq[hk][:, kc, 128:192], cqs[k][:, kc, 0:n], kc == 0, kc == 3, [bwh[hk], bcqs[k]], [pb[1]])
                            if tb < 4:
                                CP(rawf[0:64, :], bank(1)[0:64, 0:n], [pb[1]], [braw], eng='act')
                            else:
                                CP(qr[k][0:64, 0:n], bank(1)[0:64, 0:n], [pb[1]], [bq[k]], eng='dve')
                            S.dma('sp', szb[z][:, 0:n], SZB[:, h * NLOC + t0:h * NLOC + t0 + n], writes=[bszb[z]])
                        elif tb < 4:
                            MM(bank(1)[0:64, 0:512], R64, rawf[0:64, :], True, True, [braw, bcs], [pb[1]])
                            TT(tmpa[0:64, :], rawf[0:64, :], tab[0:64, 0, t0:t0 + n], ALU.mult, [braw, bldl[8]], [btmp])
                            TT(rawf[0:64, :], bank(1)[0:64, 0:512], tab[0:64, 1, t0:t0 + n], ALU.mult, [pb[1], braw, bldl[8]], [braw])
                            TT(qr[k][0:64, 0:n], tmpa[0:64, :], rawf[0:64, :], ALU.add, [btmp, braw], [bq[k]])

                    def load_cq(tb, k):
                        t0, n = TBS[tb]
                        S.dma('sp', cqs[k][:, :, 0:n], CQN3[:, :, t0:t0 + n], writes=[bcqs[k]])

                    def finalize(h, tb, k, usedP, z):
                        t0, n = TBS[tb]
                        bO = 2 + k
                        if usedP:
                            TT(accD[k][:, :, 0:n], accD[k][:, :, 0:n], accP[k][:, :, 0:n], ALU.add, [baccD[k], baccP[k]], [baccD[k]])
                        TT(rinv[k][:, 0:n], accD[k][:, 0, 0:n], accD[k][:, 1, 0:n], ALU.add, [baccD[k]], [brinv[k]])

                        def later():
                            MM(bank(1)[:, 0:n], ones32[:, :], rinv[k][:, 0:n], True, True, [bones, brinv[k]], [pb[1]])
                            ACT(rinv[k][:, 0:n], bank(1)[:, 0:n], AF.Ln, [pb[1]], [brinv[k]])
                            ACT(rinv[k][:, 0:n], rinv[k][:, 0:n], AF.Exp, [brinv[k]], [brinv[k]], scale=-1.0)
                            TT(rinv[k][:, 0:n], bank(bO)[:, 0:n], rinv[k][:, 0:n], ALU.mult, [pb[bO], brinv[k]], [brinv[k]])
                            TT(ybt[k][:, 0:n], rinv[k][:, 0:n], szb[z][:, 0:n], ALU.mult, [brinv[k], bszb[z]], [bybt[k]])
                            S.dma('sp', Y[:, (8 + h) * NLOC + t0:(8 + h) * NLOC + t0 + n], ybt[k][:, 0:n], reads=[bybt[k]])
                        deferred.append((unit_of[0], later))

                    def flush_deferred(upto=None):
                        while deferred and (upto is None or deferred[0][0] <= upto):
                            deferred.pop(0)[1]()

                    deferred = []
                    unit_of = [0]
                    units = [(h, tb) for h in range(8) for tb in range(5) if not (tb == 4 and last)]
                    gp = [0]
                    load_w(0)
                    load_cq(units[0][1], 0)
                    prologue(0, units[0][1], 0, 0, 0)
                    prologue(0, units[0][1], 0, 1, 0)
                    ui = 0
                    for h in range(8):
                        if h + 1 < 8:
                            load_w(h + 1)
                        kv_proj(h)
                        steps = []
                        while ui < len(units) and units[ui][0] == h:
                            _, tb = units[ui]
                            k = ui % 2
                            t0, n = TBS[tb]
                            kts = list(range(NKT)) if tb < 4 else [32, 33]
                            pairs = [(kts[2 * j], kts[2 * j + 1]) for j in range(len(kts) // 2)]
                            nxt = units[ui + 1] if ui + 1 < len(units) else None
                            npair = len(pairs)
                            hook0 = min(2, npair - 1)
                            hook1 = min(4, npair - 1)
                            hookf = min(5, npair - 1)
                            for j, (ka, kb_) in enumerate(pairs):
                                g_ = gp[0]
                                gp[0] += 1

                                def A(k=k, n=n, j=j, ka=ka, kb_=kb_, g_=g_, nxt=nxt, hf=(j == hookf), hk0=(j == hook0), hk1=(j == hook1), ui=ui, npair=npair):
                                    pp = 2 + g_ % 2
                                    p3 = g_ % 3
                                    for t, kt in enumerate((ka, kb_)):
                                        sb = 2 * pp + t
                                        MM(bank(sb)[:, 0:n], KnT[:, kt * 128:(kt + 1) * 128], qn[k][:, 0:n], True, False, [bkn, bq[k]], [pb[sb]])
                                        MM(bank(sb)[:, 0:n], krF[:, kt * 128:(kt + 1) * 128], qr[k][:, 0:n], False, True, bldl + [bq[k]], [pb[sb]])
                                    ACT(PT[p3][:, :, 0:n], ps[pp][:, :].rearrange("p (t q) -> p t q", t=2)[:, :, 0:n], AF.Exp,
                                        [pb[2 * pp], pb[2 * pp + 1]], [bpt[p3]], scale=scale_b)
                                    if j == 0 and nxt is not None:
                                        load_cq(nxt[1], (ui + 1) % 2)
                                    if j == min(1, npair - 1):
                                        flush_deferred(ui - 2)
                                    if hf:
                                        flush_deferred()
                                    if hk0 and nxt is not None:
                                        prologue(nxt[0], nxt[1], (ui + 1) % 2, 0, (ui + 1) % 3)
                                    if hk1 and nxt is not None:
                                        prologue(nxt[0], nxt[1], (ui + 1) % 2, 1, (ui + 1) % 3)

                                def B(h=h, tb=tb, k=k, n=n, j=j, ka=ka, kb_=kb_, g_=g_, npair=npair, ui=ui):
                                    p3 = g_ % 3
                                    bO = 2 + k
                                    for t, kt in enumerate((ka, kb_)):
                                        MM(bank(bO)[:, 0:n], Vh[:, kt, :], PT[p3][:, t, 0:n], j == 0 and t == 0, j == npair - 1 and t == 1,
                                           [bvh, bpt[p3]], [pb[bO]])
                                    if j % 2 == 1:
                                        eng, acc, bacc, first = 'pool', accP[k], baccP[k], (j == 1)
                                    else:
                                        eng, acc, bacc, first = 'dve', accD[k], baccD[k], (j == 0)
                                    if first:
                                        CP(acc[:, :, 0:n], PT[p3][:, :, 0:n], [bpt[p3]], [bacc], eng=eng)
                                    else:
                                        TT(acc[:, :, 0:n], acc[:, :, 0:n], PT[p3][:, :, 0:n], ALU.add, [bacc, bpt[p3]], [bacc], eng=eng)
                                    if j == npair - 1:
                                        unit_of[0] = ui
                                        finalize(h, tb, k, npair > 1, ui % 3)
                                steps.append((A, B))
                            ui += 1
                        pipeline(steps)
                    flush_deferred()
                    S.barrier()

            with ExitStack() as ph:
                YT = T(ph, "YT", [128, 24, NLOC], BF16)
                by2 = [[Buf() for _ in range(5)] for _ in range(3)]
                for tb_ in (0, 1, 2, 3, 4):
                    t0_, n_ = TBS[tb_]
                    for br in range(3):
                        S.dma('sp', YT[:, br * 8:(br + 1) * 8, t0_:t0_ + n_],
                              Y[:, br * 8 * NLOC:(br + 1) * 8 * NLOC].rearrange("p (a t) -> p a t", a=8)[:, :, t0_:t0_ + n_], writes=[by2[br][tb_]])
                wp = [[T(ph, "wp%d_%d" % (br, i), [128, 8, 256], BF16) for i in range(2)] for br in range(3)]
                bwp = [[Buf(), Buf()] for _ in range(3)]
                sgt = [[T(ph, "sgt%d_%d" % (br, i), [128, NLOC], BF16) for i in range(2)] for br in range(3)]
                bsg = [[Buf(), Buf()] for _ in range(3)]
                acc = T(ph, "acc", [128, NLOC], F32)
                tmp = T(ph, "mtmp", [128, NLOC], F32)
                bacc = [Buf() for _ in range(5)]
                btmp = [Buf() for _ in range(5)]
                mj = [T(ph, "mj%d" % i, [128, NLOC], BF16) for i in range(2)]
                bmj = [Buf(), Buf()]
                rot = [0]

                def nb(n):
                    r = [(rot[0] + i) % 8 for i in range(n)]
                    rot[0] = (rot[0] + n) % 8
                    return r
                for cg in range(8):
                    ck = cg % 2
                    for br in range(3):
                        S.dma('pool', wp[br][ck][:], wsrc(w_p[br][l][:, cg * 256:(cg + 1) * 256]), writes=[bwp[br][ck]])
                    for jj in range(2):
                        j = cg * 2 + jj
                        jk = j % 2
                        for br in range(3):
                            S.dma('sp', sgt[br][jk][:], SG[:, (br * 16 + j) * NLOC:(br * 16 + j + 1) * NLOC], writes=[bsg[br][jk]])
                        for grp in GROUPS:
                            for br in range(3):
                                bks = nb(len(grp))
                                for kc in range(8):
                                    for tb, b in zip(grp, bks):
                                        t0, n = TBS[tb]
                                        MM(bank(b)[:, 0:n], wp[br][ck][:, kc, jj * 128:(jj + 1) * 128], YT[:, br * 8 + kc, t0:t0 + n],
                                           kc == 0, kc == 7, [bwp[br][ck], by2[br][tb]], [pb[b]])
                                for tb, b in zip(grp, bks):
                                    t0, n = TBS[tb]
                                    sg = sgt[br][jk][:, t0:t0 + n]
                                    if br == 0:
                                        TT(acc[:, t0:t0 + n], bank(b)[:, 0:n], sg, ALU.mult, [pb[b], bsg[br][jk]], [bacc[tb]])
                                    elif br == 1:
                                        TT(tmp[:, t0:t0 + n], bank(b)[:, 0:n], sg, ALU.mult, [pb[b], bsg[br][jk]], [btmp[tb]])
                                        TT(acc[:, t0:t0 + n], acc[:, t0:t0 + n], tmp[:, t0:t0 + n], ALU.add, [bacc[tb], btmp[tb]], [bacc[tb]])
                                    else:
                                        TT(tmp[:, t0:t0 + n], bank(b)[:, 0:n], sg, ALU.mult, [pb[b], bsg[br][jk]], [btmp[tb]])
                                        TT(mj[jk][:, t0:t0 + n], acc[:, t0:t0 + n], tmp[:, t0:t0 + n], ALU.add, [bacc[tb], btmp[tb]], [bmj[jk]])
                        S.dma('sp', MT[:, j * NLOC:(j + 1) * NLOC], mj[jk][:], reads=[bmj[jk]])
                S.barrier()

            with ExitStack() as ph:
                mT = T(ph, "mT", [128, 16, NLOC], BF16)
                bm4 = [Buf() for _ in range(6)]
                for q_ in range(6):
                    S.dma('sp', mT[:, :, q_ * 384:(q_ + 1) * 384], MT[:, :].rearrange("p (a t) -> p a t", a=16)[:, :, q_ * 384:(q_ + 1) * 384], writes=[bm4[q_]])
                wo = [T(ph, "wo%d" % i, [128, 16, 512], BF16) for i in range(4)]
                bwo = [Buf() for _ in range(4)]
                for n4 in range(4):
                    S.dma('pool', wo[n4][:], wsrc(w_out[l][:, n4 * 512:(n4 + 1) * 512]), writes=[bwo[n4]])
                gB = [T(ph, "gateB%d" % r, [128, D], F32) for r in range(2)]
                bg = Buf()
                for r in range(2):
                    S.dma('sp', gB[r][:], GATEB[:, (l * 2 + r) * D:(l * 2 + r + 1) * D], writes=[bg])
                if last:
                    fg = T(ph, "fg", [128, D], F32)
                    S.dma('sp', fg[:], fgb[:, :], writes=[bg])
                    junk = T(ph, "junk7", [128, D], BF16)
                    bj = Buf()
                xt = [T(ph, "oxt%d" % i, [128, D], F32) for i in range(2)]
                bx = [Buf(), Buf()]
                ot = [T(ph, "ot%d" % i, [128, D], F32) for i in range(2)]
                bo = [Buf(), Buf()]
                for i in range(16 if last else NTILE):
                    k = i % 2
                    r = 0 if i < 16 else 1
                    S.dma('sp', xt[k][:], xsrc(i), writes=[bx[k]])
                    for n4 in range(4):
                        b = k * 4 + n4
                        for kc in range(16):
                            MM(bank(b)[:, :], mT[:, kc, i * 128:(i + 1) * 128], wo[n4][:, kc, :], kc == 0, kc == 15, [bm4[i // 3], bwo[n4]], [pb[b]])
                    for hf in range(2):
                        P = ps[k * 2 + hf]
                        sl = slice(hf * 1024, (hf + 1) * 1024)
                        TT(ot[k][:, sl], P[:, :], gB[r][:, sl], ALU.mult, [pb[k * 4 + 2 * hf], pb[k * 4 + 2 * hf + 1], bg], [bo[k]])
                        TT(ot[k][:, sl], ot[k][:, sl], xt[k][:, sl], ALU.add, [bo[k], bx[k]], [bo[k]])
                    if not last:
                        S.dma('sp', X1[i * 128:(i + 1) * 128, :], ot[k][:], reads=[bo[k]])
                    else:
                        MS(sst[k][:, 0:1], 0.0, [bss[k]])
                        ACT(junk[:], ot[k][:], AF.Square, [bo[k]], [bj, bss[k]], accum=sst[k][:, 0:1])
                        rstd_from_ss(k, 1.0 / D)
                        STT(ot[k][:], ot[k][:], sst[k][:, 3:4], fg[:], ALU.mult, ALU.mult, [bo[k], bss[k], bg], [bo[k]])
                        S.dma('sp', out[i * 128:(i + 1) * 128, :], ot[k][:], reads=[bo[k]])
                S.barrier()

        phase_mods(0)
        for l in range(n_layers):
            layer(l)
        S.wait_cc()
        S.barrier()
        S.replay(block)
    return nc


def _rope_tables(s):
    pos = (s * NTOK + np.arange(NTOK)).astype(np.float32)
    row = np.floor(pos / 64.0).astype(np.float32)
    col = (pos - row * 64.0).astype(np.float32)

    def tabs(dim):
        half = dim // 4
        inv = (np.float32(THETA) ** (-(np.arange(half, dtype=np.float32)) / np.float32(half))).astype(np.float32)
        cos = np.zeros((128, NTOK), np.float32)
        sin = np.zeros((128, NTOK), np.float32)
        for d in range(dim):
            axis = row if d < dim // 2 else col
            dd = d % (dim // 2)
            ang = (axis * inv[dd % half]).astype(np.float32)
            cos[d] = np.cos(ang)
            sin[d] = np.sin(ang) * (-1.0 if dd < half else 1.0)
        return cos, sin
    c128, s128 = tabs(128)
    c64, s64 = tabs(64)
    return np.ascontiguousarray(np.concatenate([c128, s128, c64, s64], axis=1))


def _perm(dim):
    half = dim // 4
    R = np.zeros((dim, dim), np.float32)
    for m in range(dim):
        dd = m % (dim // 2)
        partner = m + half if dd < half else m - half
        R[partner, m] = 1.0
    return R


def _consts():
    c = np.zeros((128, 580), np.float32)
    c[:, 0:128] = np.eye(128, dtype=np.float32)
    c[:, 128:256] = _perm(128)
    c[0:64, 256:320] = _perm(64)
    c[0, 320:448] = 1.0
    c[1, 448:576] = 1.0
    c[0, 576] = 1.0
    c[1, 577] = 1.0
    return c


def _masks(s):
    j = np.arange(128)[:, None]
    i = np.arange(128)[None, :]
    prev = (j >= i).astype(np.float32)
    nxt = (j <= i).astype(np.float32)
    m = np.zeros((128, 4, 4, 128), np.float32)
    m[:, 0] = prev[:, None, :]
    m[:, 1] = nxt[:, None, :]
    m[:, 2] = 0.0 if s == 0 else prev[:, None, :]
    m[:, 3] = 0.0 if s == 1 else nxt[:, None, :]
    return np.ascontiguousarray(m.reshape(128, 4 * 512))


def make_in_maps(x, c, ctx, c_ctx, ada_w, ada_b, norm_g, w_in, sink_a, mla_gq, mla_gkv, w_uq, w_ukv,
                 sgu_ln_g, sgu_ln_b, sgu_w, sgu_b, w_pa, w_pb, w_pc, w_out, final_g):
    f = lambda a: np.ascontiguousarray(np.asarray(a, dtype=np.float32))
    x, c, ctx, c_ctx = f(x), f(c), f(ctx), f(c_ctx)
    bc = lambda v, n: np.ascontiguousarray(np.broadcast_to(np.asarray(v, np.float32)[:, None, :], (2, 128, n)))
    shared = {
        "ada_w": f(ada_w),
        "ada_b": np.ascontiguousarray(np.broadcast_to(f(ada_b)[:, None, :], (2, 2, 3 * D))),
        "normg": np.ascontiguousarray(f(norm_g).reshape(2, 16, 128).transpose(0, 2, 1)),
        "w_in": f(w_in),
        "sinkb": np.ascontiguousarray(np.broadcast_to(f(sink_a)[:, None, :, None], (2, 128, 8, 128)).reshape(2, 128, 1024)),
        "gqb": bc(mla_gq, 512), "gkvb": bc(mla_gkv, 512),
        "w_uq": f(w_uq), "w_ukv": f(w_ukv),
        "lngb": bc(sgu_ln_g, 1024), "lnbb": bc(sgu_ln_b, 1024),
        "wsT": np.ascontiguousarray(f(sgu_w).transpose(0, 3, 1, 2).reshape(2, 128, 1024)),
        "bsb": np.ascontiguousarray(np.broadcast_to(f(sgu_b).reshape(2, 1, 1024), (2, 128, 1024))),
        "w_pa": f(w_pa), "w_pb": f(w_pb), "w_pc": f(w_pc), "w_out": f(w_out),
        "fgb": np.ascontiguousarray(np.broadcast_to(f(final_g)[None, :], (128, D))),
        "cst": _consts(),
    }
    maps = []
    for core in range(8):
        b, s = core // 2, core % 2
        cv = np.zeros((128, 32), np.float32)
        cv[:, 0::2] = c[b].reshape(16, 128).T
        cv[:, 1::2] = c_ctx.reshape(16, 128).T
        m = dict(shared)
        m["x"] = np.ascontiguousarray(x[b, s * NTOK:(s + 1) * NTOK])
        m["ctx"] = np.ascontiguousarray(ctx[b])
        m["cvec"] = cv
        m["rope"] = _rope_tables(s)
        m["masks"] = _masks(s)
        maps.append(m)
    return maps


def kernel(**inputs):
    nc = build()
    maps = make_in_maps(**inputs)
    res = run_bass_kernel_spmd(nc, maps, core_ids=list(range(8)))
    outp = np.zeros((4, 2 * NTOK, D), np.float32)
    for core in range(8):
        b, s = core // 2, core % 2
        outp[b, s * NTOK:(s + 1) * NTOK] = np.asarray(res.results[core]["out"], dtype=np.float32)
    return outp
```
